# Optimizing a Trainium2 kernel written in Bass

```python
import jax, jax.numpy as jnp
from jax import lax
import numpy as np

D_MODEL = 1024
BATCH = 16
SEQ = 4096
DEPTH = 4

D_FF = 2816
MLA_HEADS = 8
Q_LORA = 256
KV_LORA = 128
QK_NOPE = 64
QK_ROPE = 32
V_HEAD = 64
QK_HEAD = QK_NOPE + QK_ROPE
MLA_WIDTH = MLA_HEADS * V_HEAD
ROPE_THETA = 10000.0
Q_BLOCK = 128
POOL_WINDOWS = (2, 4, 8, 16)
POOL_GROUPS = len(POOL_WINDOWS)
POOL_WIDTH = D_MODEL - MLA_WIDTH
POOL_GROUP_DIM = POOL_WIDTH // POOL_GROUPS
EVEN_IN = Q_LORA + KV_LORA + QK_ROPE + POOL_WIDTH
CHUNK = 128
SG_GROUPS = 4
SG_WIDTH = D_MODEL
SG_GROUP_DIM = SG_WIDTH // SG_GROUPS
EPS = 1e-6
N_EVEN = (DEPTH + 1) // 2
N_ODD = DEPTH // 2
MAX_POS_OFFSET = 4096

kernel_name = "hybrid_mla_pool_gmlp_macaron"


def _rmsnorm(x, g):
    xf = x.astype(jnp.float32)
    y = xf * lax.rsqrt(jnp.mean(xf * xf, axis=-1, keepdims=True) + EPS)
    return (y * g.astype(jnp.float32)).astype(x.dtype)


def _swiglu(x, w_gate, w_up, w_down):
    return (jax.nn.silu(x @ w_gate) * (x @ w_up)) @ w_down


def _rope_tables(positions):
    inv_freq = ROPE_THETA ** (-jnp.arange(0, QK_ROPE, 2, dtype=jnp.float32) / QK_ROPE)
    ang = positions.astype(jnp.float32)[..., None] * inv_freq
    return jnp.cos(ang)[:, :, None, :], jnp.sin(ang)[:, :, None, :]


def _rope(x, cos, sin):
    xf = x.astype(jnp.float32)
    x1, x2 = xf[..., : QK_ROPE // 2], xf[..., QK_ROPE // 2:]
    return jnp.concatenate([x1 * cos - x2 * sin, x1 * sin + x2 * cos], axis=-1).astype(x.dtype)


def _causal_attention(q, k, v):
    B, S, H, D = q.shape
    nb = S // Q_BLOCK
    scale = D ** -0.5
    qb = q.reshape(B, nb, Q_BLOCK, H, D).transpose(1, 0, 2, 3, 4)
    k_pos = jnp.arange(S)

    def block(args):
        q_blk, start = args
        s = jnp.einsum('bqhd,bkhd->bhqk', q_blk, k, preferred_element_type=jnp.float32) * scale
        q_pos = start + jnp.arange(Q_BLOCK)
        s = jnp.where(k_pos[None, :] <= q_pos[:, None], s, -jnp.inf)
        p = jax.nn.softmax(s, axis=-1)
        return jnp.einsum('bhqk,bkhd->bqhd', p.astype(v.dtype), v)

    out = lax.map(block, (qb, jnp.arange(nb) * Q_BLOCK))
    return out.transpose(1, 0, 2, 3, 4).reshape(B, S, H, v.shape[-1])


def _multiscale_pool(p, pool_w, pool_scale):
    B, S, _ = p.shape
    pg = p.reshape(B, S, POOL_GROUPS, POOL_GROUP_DIM).astype(jnp.float32)
    cs = jnp.concatenate([jnp.zeros((B, 1, POOL_GROUPS, POOL_GROUP_DIM), jnp.float32),
                          jnp.cumsum(pg, axis=1)], axis=1)
    t = jnp.arange(S)
    outs = []
    for g, w in enumerate(POOL_WINDOWS):
        upper = cs[:, 1:, g]
        lower = jnp.concatenate([jnp.zeros((B, w - 1, POOL_GROUP_DIM), jnp.float32),
                                 cs[:, : S - w + 1, g]], axis=1)
        count = jnp.minimum(t + 1, w).astype(jnp.float32)[None, :, None]
        outs.append((upper - lower) / count - pg[:, :, g])
    pooled = jnp.stack(outs, axis=2).astype(p.dtype)
    mixed = jnp.einsum('bsgc,gcd->bsgd', pooled, pool_w)
    return mixed.reshape(B, S, POOL_WIDTH) * pool_scale


def _mla_pool_mixer(hn, cos, sin, w_in, q_a_g, kv_a_g, w_uq, w_ukv, q_g, k_g,
                    pool_w, pool_scale, w_out):
    B, S, _ = hn.shape
    proj = hn @ w_in
    c_q, c_kv, k_pe, p = jnp.split(
        proj, [Q_LORA, Q_LORA + KV_LORA, Q_LORA + KV_LORA + QK_ROPE], axis=-1)
    q = (_rmsnorm(c_q, q_a_g) @ w_uq).reshape(B, S, MLA_HEADS, QK_HEAD)
    kv = (_rmsnorm(c_kv, kv_a_g) @ w_ukv).reshape(B, S, MLA_HEADS, QK_NOPE + V_HEAD)
    k_nope, v = kv[..., :QK_NOPE], kv[..., QK_NOPE:]
    k = jnp.concatenate(
        [k_nope, jnp.broadcast_to(k_pe[:, :, None, :], (B, S, MLA_HEADS, QK_ROPE))], axis=-1)
    q = _rmsnorm(q, q_g)
    k = _rmsnorm(k, k_g)
    q = jnp.concatenate([q[..., :QK_NOPE], _rope(q[..., QK_NOPE:], cos, sin)], axis=-1)
    k = jnp.concatenate([k[..., :QK_NOPE], _rope(k[..., QK_NOPE:], cos, sin)], axis=-1)
    attn = _causal_attention(q, k, v).reshape(B, S, MLA_WIDTH)
    pooled = _multiscale_pool(p, pool_w, pool_scale)
    return jnp.concatenate([attn, pooled], axis=-1) @ w_out


def _spatial_gating_mixer(hn, w_in, sg_norm_g, sg_w, sg_b, w_out):
    B, S, _ = hn.shape
    uv = jax.nn.gelu(hn @ w_in)
    u, v = jnp.split(uv, 2, axis=-1)
    v = _rmsnorm(v, sg_norm_g)
    vc = v.reshape(B, S // CHUNK, CHUNK, SG_GROUPS, SG_GROUP_DIM)
    w = sg_w * jnp.tril(jnp.ones((CHUNK, CHUNK), sg_w.dtype))
    mixed = jnp.einsum('gts,bnsgc->bntgc', w, vc) + sg_b.T[None, None, :, :, None]
    return (u * mixed.reshape(B, S, SG_WIDTH)) @ w_out


def setup_inputs(seed: int = 0) -> dict:
    key = jax.random.key(seed)
    ks = jax.random.split(key, 24)
    f32 = jnp.float32

    def nrm(k, shape, fan_in):
        return jax.random.normal(k, shape, f32) * (fan_in ** -0.5)

    def gain(k, shape):
        return 1.0 + 0.02 * jax.random.normal(k, shape, f32)

    x = jax.random.normal(ks[0], (BATCH, SEQ, D_MODEL), f32)
    offset = jax.random.randint(ks[1], (BATCH, 1), 0, MAX_POS_OFFSET, dtype=jnp.int32)
    positions = (offset + jnp.arange(SEQ, dtype=jnp.int32)[None, :]).astype(jnp.int32)
    return {
        "x": x,
        "positions": positions,
        "ffn_norm": gain(ks[2], (DEPTH, 2, D_MODEL)),
        "ffn_w_gate": nrm(ks[3], (DEPTH, 2, D_MODEL, D_FF), D_MODEL),
        "ffn_w_up": nrm(ks[4], (DEPTH, 2, D_MODEL, D_FF), D_MODEL),
        "ffn_w_down": nrm(ks[5], (DEPTH, 2, D_FF, D_MODEL), D_FF),
        "mix_norm": gain(ks[6], (DEPTH, D_MODEL)),
        "even_w_in": nrm(ks[7], (N_EVEN, D_MODEL, EVEN_IN), D_MODEL),
        "q_a_norm": gain(ks[8], (N_EVEN, Q_LORA)),
        "kv_a_norm": gain(ks[9], (N_EVEN, KV_LORA)),
        "w_uq": nrm(ks[10], (N_EVEN, Q_LORA, MLA_HEADS * QK_HEAD), Q_LORA),
        "w_ukv": nrm(ks[11], (N_EVEN, KV_LORA, MLA_HEADS * (QK_NOPE + V_HEAD)), KV_LORA),
        "q_norm": gain(ks[12], (N_EVEN, QK_HEAD)),
        "k_norm": gain(ks[13], (N_EVEN, QK_HEAD)),
        "pool_w": nrm(ks[14], (N_EVEN, POOL_GROUPS, POOL_GROUP_DIM, POOL_GROUP_DIM), POOL_GROUP_DIM),
        "pool_scale": gain(ks[15], (N_EVEN, POOL_WIDTH)),
        "even_w_out": nrm(ks[16], (N_EVEN, D_MODEL, D_MODEL), D_MODEL),
        "odd_w_in": nrm(ks[17], (N_ODD, D_MODEL, 2 * SG_WIDTH), D_MODEL),
        "sg_norm": gain(ks[18], (N_ODD, SG_WIDTH)),
        "sg_w": nrm(ks[19], (N_ODD, SG_GROUPS, CHUNK, CHUNK), CHUNK),
        "sg_b": gain(ks[20], (N_ODD, SG_GROUPS, CHUNK)),
        "odd_w_out": nrm(ks[21], (N_ODD, SG_WIDTH, D_MODEL), SG_WIDTH),
    }


def reference(x, positions, ffn_norm, ffn_w_gate, ffn_w_up, ffn_w_down, mix_norm,
              even_w_in, q_a_norm, kv_a_norm, w_uq, w_ukv, q_norm, k_norm,
              pool_w, pool_scale, even_w_out, odd_w_in, sg_norm, sg_w, sg_b, odd_w_out):
    cos, sin = _rope_tables(positions)
    h = x
    for layer in range(DEPTH):
        h = h + 0.5 * _swiglu(_rmsnorm(h, ffn_norm[layer, 0]), ffn_w_gate[layer, 0],
                              ffn_w_up[layer, 0], ffn_w_down[layer, 0])
        hn = _rmsnorm(h, mix_norm[layer])
        i = layer // 2
        if layer % 2 == 0:
            h = h + _mla_pool_mixer(hn, cos, sin, even_w_in[i], q_a_norm[i], kv_a_norm[i],
                                    w_uq[i], w_ukv[i], q_norm[i], k_norm[i],
                                    pool_w[i], pool_scale[i], even_w_out[i])
        else:
            h = h + _spatial_gating_mixer(hn, odd_w_in[i], sg_norm[i], sg_w[i], sg_b[i],
                                          odd_w_out[i])
        h = h + 0.5 * _swiglu(_rmsnorm(h, ffn_norm[layer, 1]), ffn_w_gate[layer, 1],
                              ffn_w_up[layer, 1], ffn_w_down[layer, 1])
    return h
```

```python
import contextlib
import numpy as np
import concourse.bass as bass
import concourse.mybir as mybir
from concourse.bass_utils import run_bass_kernel_spmd

F32 = mybir.dt.float32
BF16 = mybir.dt.bfloat16
I32 = mybir.dt.int32
AF = mybir.ActivationFunctionType
ALU = mybir.AluOpType

D = 1024
DFF = 2816
NFF = 22
DEPTH = 4
EPS = 1e-6
SB_BASE = 16512
SB_END = 229376
HRES_BYTES = 8 * 4096 * 4
ENGS = ("pe", "act", "dve", "pool", "sp")


def _dsize(dt):
    return 2 if dt == BF16 else 4


class Sched:
    def __init__(self, nc, stack):
        self.nc = nc
        self.stack = stack
        self.q = {e: [] for e in ENGS}
        self.tok = {}
        self.sem = {}
        self.seen = {}
        self.res = {}

    def _semh(self, key):
        if key not in self.sem:
            name = "s_" + "_".join(str(x) for x in key)
            self.sem[key] = self.stack.enter_context(self.nc.semaphore(name))
        return self.sem[key]

    def _collect(self, eng, reads, writes):
        need = {}

        def add(t):
            if t is not None:
                k, v = t
                if need.get(k, 0) < v:
                    need[k] = v

        for r in reads:
            e = self.res.get(r)
            if e is not None:
                add(e[0])
        for w in writes:
            e = self.res.get(w)
            if e is not None:
                add(e[0])
                for k, v in e[1].items():
                    add((k, v))
        waits = []
        for k, v in need.items():
            if eng == "pe" and k == ("eng", "pe"):
                continue
            if self.seen.get((eng, k), 0) < v:
                waits.append((k, v))
                self.seen[(eng, k)] = v
        return waits

    def _commit(self, reads, writes, token):
        k, v = token
        for r in reads:
            e = self.res.setdefault(r, [None, {}])
            if e[1].get(k, 0) < v:
                e[1][k] = v
        for w in writes:
            self.res[w] = [token, {}]

    def op(self, eng, fn, reads=(), writes=()):
        waits = self._collect(eng, reads, writes)
        key = ("eng", eng)
        self._semh(key)
        self.tok[key] = self.tok.get(key, 0) + 1
        self._commit(reads, writes, (key, self.tok[key]))
        self.q[eng].append((waits, fn, (key, 1)))

    def dma(self, queue, fn, reads=(), writes=(), sem=None, token_val=None):
        waits = self._collect(queue, reads, writes)
        key = ("dma",) + tuple(sem)
        self._semh(key)
        self.tok[key] = self.tok.get(key, 0) + 16
        tv = self.tok[key] if token_val is None else token_val
        self._commit(reads, writes, (key, tv))
        self.q[queue].append((waits, fn, (key, 16)))

    def barrier_all(self):
        for e in ENGS:
            waits = []
            for k, v in self.tok.items():
                if self.seen.get((e, k), 0) < v:
                    waits.append((k, v))
                    self.seen[(e, k)] = v
            if waits:
                self.q[e].append((waits, None, None))
        self.res = {}

    def emit(self):
        nc = self.nc
        with nc.Block() as block:
            decos = {"pe": block.tensor, "act": block.scalar, "dve": block.vector,
                     "pool": block.gpsimd, "sp": block.sync}
            for eng in ENGS:
                def body(e, eng=eng):
                    for waits, fn, inc in self.q[eng]:
                        if fn is None:
                            for k, v in waits:
                                e.wait_ge(self.sem[k], v)
                            continue
                        for k, v in waits[:-1]:
                            e.wait_ge(self.sem[k], v)
                        att = (self.sem[waits[-1][0]], waits[-1][1]) if waits else None
                        ins = fn(_EngProxy(e, att))
                        ins.then_inc(self.sem[inc[0]], inc[1])
                decos[eng](body)


class _EngProxy:
    def __init__(self, e, att):
        self._e = e
        self._att = att

    def __getattr__(self, name):
        f = getattr(self._e, name)

        def call(*a, **kw):
            ins = f(*a, **kw)
            if self._att is not None:
                ins._wait_ge(self._att[0], self._att[1])
                self._att = None
            return ins
        return call


class Arena:
    def __init__(self, nc, lo, hi):
        self.nc = nc
        self.lo = lo
        self.hi = hi
        self.cur = lo
        self.n = 0

    def alloc(self, name, shape, dt):
        nbytes = int(np.prod(shape[1:])) * _dsize(dt)
        nbytes = (nbytes + 31) // 32 * 32
        assert self.cur + nbytes <= self.hi, (name, self.cur, nbytes, self.hi)
        self.n += 1
        t = self.nc.alloc_sbuf_tensor_at("%s_%d" % (name, self.n), list(shape), dt, offset=self.cur)
        self.cur += nbytes
        return t

    def mark(self):
        return self.cur

    def reset(self, m):
        self.cur = m


class Stream:
    def __init__(self, K, name, slots, units, loader, src_keys):
        self.K = K
        self.name = name
        self.slots = slots
        self.units = units
        self.loader = loader
        self.src_keys = src_keys
        self.issued = 0

    def key(self, i):
        return ("ws", self.name, i % len(self.slots))

    def _issue_upto(self, n):
        while self.issued < min(n, len(self.units)):
            i = self.issued
            tile = self.slots[i % len(self.slots)]
            fn = self.loader(self.units[i], tile)
            self.K.dma("sp", fn, reads=self.src_keys(self.units[i]), writes=[self.key(i)],
                       sem=(self.name, i % len(self.slots)))
            self.issued += 1

    def get(self, i):
        self._issue_upto(i + len(self.slots))
        return self.slots[i % len(self.slots)], self.key(i)


def build(S, NSEQ, plan):
    assert S % 512 == 0
    NB = S // 512
    NCH = S // 128
    nc = bass.Bass("TRN2", target_bir_lowering=False)

    def din(name, shape, dt=F32):
        return nc.dram_tensor(name, list(shape), dt, kind="ExternalInput").ap()

    def dscr(name, shape, dt=BF16):
        return nc.dram_tensor(name, list(shape), dt, kind="Internal").ap()

    x = din("x", [NSEQ, S, D])
    positions = din("positions", [NSEQ, S], I32)
    ffn_norm = din("ffn_norm", [DEPTH, 2, D])
    ffn_w_gate = din("ffn_w_gate", [DEPTH, 2, D, DFF])
    ffn_w_up = din("ffn_w_up", [DEPTH, 2, D, DFF])
    ffn_w_down = din("ffn_w_down", [DEPTH, 2, DFF, D])
    mix_norm = din("mix_norm", [DEPTH, D])
    even_w_in = din("even_w_in", [2, D, 928])
    q_a_norm = din("q_a_norm", [2, 256])
    kv_a_norm = din("kv_a_norm", [2, 128])
    w_uq = din("w_uq", [2, 256, 768])
    w_ukv = din("w_ukv", [2, 128, 1024])
    q_norm = din("q_norm", [2, 96])
    k_norm = din("k_norm", [2, 96])
    pool_w = din("pool_w", [2, 4, 128, 128])
    pool_scale = din("pool_scale", [2, 512])
    even_w_out = din("even_w_out", [2, D, D])
    odd_w_in = din("odd_w_in", [2, D, 2048])
    sg_norm = din("sg_norm", [2, D])
    sg_w = din("sg_w", [2, 4, 128, 128])
    sg_b = din("sg_b", [2, 4, 128])
    odd_w_out = din("odd_w_out", [2, D, D])
    c_ident = din("c_ident", [128, 128])
    c_triu = din("c_triu", [128, 128])
    c_rotm = din("c_rotm", [128, 128])
    c_invfreq = din("c_invfreq", [128, 1])
    c_invcnt = din("c_invcnt", [128, 64])
    y = nc.dram_tensor("y", [NSEQ, S, D], F32, kind="ExternalOutput").ap()

    need_ffn = sorted({(l, j) for (l, p) in plan for j in ((0,) if p == "ffn1" else (1,) if p == "ffn2" else ())})
    need_even = sorted({l // 2 for (l, p) in plan if p == "mix" and l % 2 == 0})
    need_odd = sorted({l // 2 for (l, p) in plan if p == "mix" and l % 2 == 1})

    s_gu = {lj: dscr("s_gu_%d_%d" % lj, [NFF, 128, 2, 8, 128]) for lj in need_ffn}
    s_d = {lj: dscr("s_d_%d_%d" % lj, [8, 128, NFF, 128]) for lj in need_ffn}
    s_ewin = {i: dscr("s_ewin_%d" % i, [7, 128, 8, 128]) for i in need_even}
    s_ekpe = {i: dscr("s_ekpe_%d" % i, [128, 8, 32]) for i in need_even}
    s_euq = {i: dscr("s_euq_%d" % i, [8, 128, 2, 96]) for i in need_even}
    s_eukv = {i: dscr("s_eukv_%d" % i, [128, 1024]) for i in need_even}
    s_epw = {i: dscr("s_epw_%d" % i, [128, 4, 128]) for i in need_even}
    s_ewoa = {i: dscr("s_ewoa_%d" % i, [512, 1024]) for i in need_even}
    s_ewop = {i: dscr("s_ewop_%d" % i, [8, 128, 4, 128]) for i in need_even}
    s_owu = {i: dscr("s_owu_%d" % i, [8, 128, 8, 128]) for i in need_odd}
    s_owv = {i: dscr("s_owv_%d" % i, [128, 8, 1024]) for i in need_odd}
    s_owo = {i: dscr("s_owo_%d" % i, [8, 128, 8, 128]) for i in need_odd}
    s_cs = dscr("s_cs", [NSEQ, 2, 32, S], F32)

    stack = contextlib.ExitStack()
    with stack:
        K = Sched(nc, stack)
        h_res = nc.alloc_sbuf_tensor_at("h_res", [128, 8, S], F32, offset=SB_BASE)
        ar = Arena(nc, SB_BASE + HRES_BYTES, SB_END)
        ps = [stack.enter_context(nc.psum_tensor("ps%d" % i, [128, 512], F32)) for i in range(8)]

        ones_bf = ar.alloc("ones_bf", [128, 128], BF16)
        ones_f = ar.alloc("ones_f", [128, 128], F32)
        ident = ar.alloc("ident", [128, 128], F32)
        g_ffn = ar.alloc("g_ffn", [128, 8, 8], F32)
        g_mix = ar.alloc("g_mix", [128, 4, 8], F32)
        K.op("pool", lambda e: e.memset(ones_bf[:], 1.0), writes=["ones_bf"])
        K.op("pool", lambda e: e.memset(ones_f[:], 1.0), writes=["ones_f"])
        K.dma("sp", lambda e: e.dma_start(out=ident[:], in_=c_ident[:, :]), writes=["ident"], sem=("c", 0))
        g_raw = ar.alloc("g_raw", [128, 128], F32)
        K.dma("sp", lambda e: e.dma_start(out=g_raw[0:64, :], in_=ffn_norm.rearrange("l j (c p) -> (l j c) p", p=128)),
              writes=["g_raw0"], sem=("c", 1))
        K.dma("sp", lambda e: e.dma_start(out=g_raw[64:96, :], in_=mix_norm.rearrange("l (c p) -> (l c) p", p=128)),
              writes=["g_raw1"], sem=("c", 2))
        K.op("pe", lambda e: e.transpose(out=ps[7][:, 0:96], in_=g_raw[0:96, :], identity=ident[0:96, 0:96]),
             reads=["g_raw0", "g_raw1", "ident"], writes=[("ps", 7)])
        K.op("dve", lambda e: e.tensor_scalar(out=g_ffn[:].rearrange("p a c -> p (a c)"), in0=ps[7][:, 0:64], scalar1=32.0, scalar2=None, op0=ALU.mult),
             reads=[("ps", 7)], writes=["g_ffn"])
        K.op("dve", lambda e: e.tensor_scalar(out=g_mix[:].rearrange("p a c -> p (a c)"), in0=ps[7][:, 64:96], scalar1=32.0, scalar2=None, op0=ALU.mult),
             reads=[("ps", 7)], writes=["g_mix"])
        ar_base = ar.mark()

        def cast_group(grp, items):
            n = len(items)
            for idx, (dst, src) in enumerate(items):
                K.dma("pool", (lambda e, dst=dst, src=src: e.dma_start(out=dst, in_=src)),
                      writes=[("scrp", idx) + tuple(grp)], sem=("cast",) + tuple(grp), token_val=16 * n)
            K.res[("scr",) + tuple(grp)] = [(("dma", "cast") + tuple(grp), 16 * n), {}]

        def cast_ffn(l, j):
            items = []
            for m in range(NFF):
                items.append((s_gu[(l, j)][m, :, 0], ffn_w_gate[l, j, :, m * 128:(m + 1) * 128].rearrange("(k p) n -> p k n", p=128)))
                items.append((s_gu[(l, j)][m, :, 1], ffn_w_up[l, j, :, m * 128:(m + 1) * 128].rearrange("(k p) n -> p k n", p=128)))
            for mo in range(8):
                items.append((s_d[(l, j)][mo], ffn_w_down[l, j, :, mo * 128:(mo + 1) * 128].rearrange("(k p) n -> p k n", p=128)))
            cast_group(("ffn", l, j), items)

        def cast_odd(i):
            items = []
            for fc in range(8):
                items.append((s_owu[i][fc], odd_w_in[i, :, fc * 128:(fc + 1) * 128].rearrange("(k p) n -> p k n", p=128)))
            items.append((s_owv[i][:, :, :], odd_w_in[i, :, 1024:2048].rearrange("(k p) n -> p k n", p=128)))
            for mo in range(8):
                items.append((s_owo[i][mo], odd_w_out[i, :, mo * 128:(mo + 1) * 128].rearrange("(k p) n -> p k n", p=128)))
            cast_group(("odd", i), items)

        def cast_even(i):
            items = []
            cols = [0, 128, 256, 416, 544, 672, 800]
            for t, c0 in enumerate(cols):
                items.append((s_ewin[i][t], even_w_in[i, :, c0:c0 + 128].rearrange("(k p) n -> p k n", p=128)))
            items.append((s_ekpe[i][:, :, :], even_w_in[i, :, 384:416].rearrange("(k p) n -> p k n", p=128)))
            for h in range(8):
                items.append((s_euq[i][h], w_uq[i, :, h * 96:(h + 1) * 96].rearrange("(k p) n -> p k n", p=128)))
            items.append((s_eukv[i][:, :], w_ukv[i, :, :]))
            items.append((s_epw[i][:, :, :], pool_w[i].rearrange("g c d -> c g d")))
            items.append((s_ewoa[i][:, :], even_w_out[i, 0:512, :]))
            for mo in range(8):
                items.append((s_ewop[i][mo], even_w_out[i, 512:1024, mo * 128:(mo + 1) * 128].rearrange("(g p) n -> p g n", p=128)))
            cast_group(("even", i), items)

        done_cast = set()
        for (l, p) in plan:
            if p in ("ffn1", "ffn2"):
                key = ("ffn", l, 0 if p == "ffn1" else 1)
                if key not in done_cast:
                    cast_ffn(l, key[2])
            elif l % 2 == 0:
                key = ("even", l // 2)
                if key not in done_cast:
                    cast_even(l // 2)
            else:
                key = ("odd", l // 2)
                if key not in done_cast:
                    cast_odd(l // 2)
            done_cast.add(key)
        scr_tokens = {k: v for k, v in K.res.items() if k[0] == "scr"}

        def barrier():
            K.barrier_all()
            K.res.update({k: [v[0], {}] for k, v in scr_tokens.items()})
            ar.reset(ar_base)

        def hkey(c, b):
            return ("h", c, b)

        def emit_norm_act(b, sq, nchunk=8, c0=0):
            K.op("act", lambda e: e.activation(out=sq[:, 0:nchunk, :], in_=h_res[:, c0:c0 + nchunk, b * 512:(b + 1) * 512], func=AF.Square),
                 reads=[hkey(c, b) for c in range(c0, c0 + nchunk)], writes=["sq"])

        def emit_norm_rest(b, sq, rstd, hn, hn_key, gain, pstat):
            def mm(e):
                for c in range(8):
                    ins = e.matmul(ps[pstat][:, :], lhsT=ones_bf[:, :], rhs=sq[:, c, :], start=(c == 0), stop=(c == 7))
                return ins
            K.op("pe", mm, reads=["sq", "ones_bf"], writes=[("ps", pstat)])
            K.op("act", lambda e: e.activation(out=rstd[:, :], in_=ps[pstat][:, :], func=AF.Sqrt, bias=float(D * EPS)),
                 reads=[("ps", pstat)], writes=["rstd"])
            K.op("dve", lambda e: e.reciprocal(out=rstd[:, :], in_=rstd[:, :]), reads=["rstd"], writes=["rstd"])
            for c in range(8):
                K.op("dve", lambda e, c=c: e.scalar_tensor_tensor(out=hn[:, c, :], in0=h_res[:, c, b * 512:(b + 1) * 512],
                                                                  scalar=gain[:, c:c + 1], in1=rstd[:, :],
                                                                  op0=ALU.mult, op1=ALU.mult),
                     reads=[hkey(c, b), "rstd", "g_ffn", "g_mix"], writes=[hn_key])

        def load_seq(s):
            barrier()
            xs = [ar.alloc("xs", [128, D], F32) for _ in range(2)]
            for tc in range(NCH):
                sl = tc % 2
                K.dma("sp", lambda e, tc=tc, sl=sl: e.dma_start(out=xs[sl][:], in_=x[s, tc * 128:(tc + 1) * 128, :]),
                      writes=[("xs", sl)], sem=("xs", sl))
                for half in range(2):
                    bank = (tc * 2 + half) % 4

                    def tr(e, tc=tc, sl=sl, half=half, bank=bank):
                        for q in range(4):
                            c = half * 4 + q
                            ins = e.transpose(out=ps[bank][:, q * 128:(q + 1) * 128], in_=xs[sl][:, c * 128:(c + 1) * 128], identity=ident[:, :])
                        return ins
                    K.op("pe", tr, reads=[("xs", sl), "ident"], writes=[("ps", bank)])
                    b = tc // 4
                    eng = "act" if half == 0 else "dve"
                    dst = h_res[:, half * 4:half * 4 + 4, tc * 128:(tc + 1) * 128]
                    src = ps[bank][:, :].rearrange("p (q n) -> p q n", q=4)
                    if eng == "act":
                        K.op("act", lambda e, dst=dst, src=src: e.activation(out=dst, in_=src, func=AF.Copy),
                             reads=[("ps", bank)], writes=[hkey(c, b) for c in range(half * 4, half * 4 + 4)])
                    else:
                        K.op("dve", lambda e, dst=dst, src=src: e.tensor_copy(out=dst, in_=src),
                             reads=[("ps", bank)], writes=[hkey(c, b) for c in range(half * 4, half * 4 + 4)])

        def store_seq(s):
            barrier()
            xs = [ar.alloc("xs", [128, D], F32) for _ in range(2)]
            for tc in range(NCH):
                sl = tc % 2
                b = tc // 4
                for half in range(2):
                    bank = (tc * 2 + half) % 4

                    def tr(e, tc=tc, half=half, bank=bank):
                        for q in range(4):
                            c = half * 4 + q
                            ins = e.transpose(out=ps[bank][:, q * 128:(q + 1) * 128], in_=h_res[:, c, tc * 128:(tc + 1) * 128], identity=ident[:, :])
                        return ins
                    K.op("pe", tr, reads=[hkey(c, b) for c in range(half * 4, half * 4 + 4)] + ["ident"], writes=[("ps", bank)])
                    dst = xs[sl][:, half * 512:(half + 1) * 512]
                    if half == 0:
                        K.op("act", lambda e, dst=dst, bank=bank: e.activation(out=dst, in_=ps[bank][:, :], func=AF.Copy),
                             reads=[("ps", bank)], writes=[("xs", sl, half)])
                    else:
                        K.op("dve", lambda e, dst=dst, bank=bank: e.tensor_copy(out=dst, in_=ps[bank][:, :]),
                             reads=[("ps", bank)], writes=[("xs", sl, half)])
                K.dma("sp", lambda e, tc=tc, sl=sl: e.dma_start(out=y[s, tc * 128:(tc + 1) * 128, :], in_=xs[sl][:]),
                      reads=[("xs", sl, 0), ("xs", sl, 1)], writes=[("yout", tc)], sem=("ys", sl))

        def ffn_phase(s, l, j):
            barrier()
            hn = [ar.alloc("hn", [128, 8, 512], BF16) for _ in range(2)]
            sq = ar.alloc("sq", [128, 8, 512], BF16)
            rstd = ar.alloc("rstd", [128, 512], F32)
            sg = [ar.alloc("sg", [128, 512], F32) for _ in range(2)]
            act = ar.alloc("act", [128, NFF, 512], BF16)
            wgu = [ar.alloc("wgu", [128, 2, 8, 128], BF16) for _ in range(3)]
            wd = [ar.alloc("wd", [128, NFF, 128], BF16) for _ in range(2)]
            gain = g_ffn[:, l * 2 + j, :]
            scr = ("scr", "ffn", l, j)
            gu_units = [(b, m) for b in range(NB) for m in range(NFF)]
            d_units = [(b, mo) for b in range(NB) for mo in range(8)]
            st_gu = Stream(K, "wgu", wgu, gu_units,
                           lambda u, t: (lambda e: e.dma_start(out=t[:], in_=s_gu[(l, j)][u[1]])),
                           lambda u: [scr])
            st_d = Stream(K, "wd", wd, d_units,
                          lambda u, t: (lambda e: e.dma_start(out=t[:], in_=s_d[(l, j)][u[1]])),
                          lambda u: [scr])
            PG, PU, PD, PST = (0, 1), (2, 3), (4, 5), 6

            emit_norm_act(0, sq)
            emit_norm_rest(0, sq, rstd, hn[0], ("hn", 0), gain, PST)
            for b in range(NB):
                hs = b % 2
                for m in range(NFF):
                    wt, wkey = st_gu.get(b * NFF + m)
                    gb, ub = PG[m % 2], PU[m % 2]

                    def mmg(e, wt=wt, gb=gb, hs=hs):
                        for k in range(8):
                            ins = e.matmul(ps[gb][:, :], lhsT=wt[:, 0, k, :], rhs=hn[hs][:, k, :], start=(k == 0), stop=(k == 7))
                        return ins

                    def mmu(e, wt=wt, ub=ub, hs=hs):
                        for k in range(8):
                            ins = e.matmul(ps[ub][:, :], lhsT=wt[:, 1, k, :], rhs=hn[hs][:, k, :], start=(k == 0), stop=(k == 7))
                        return ins
                    K.op("pe", mmg, reads=[wkey, ("hn", hs)], writes=[("ps", gb)])
                    K.op("pe", mmu, reads=[wkey, ("hn", hs)], writes=[("ps", ub)])
                    K.op("act", lambda e, gb=gb, m=m: e.activation(out=sg[m % 2][:, :], in_=ps[gb][:, :], func=AF.Silu),
                         reads=[("ps", gb)], writes=[("sg", m % 2)])
                    K.op("dve", lambda e, ub=ub, m=m: e.tensor_tensor(out=act[:, m, :], in0=ps[ub][:, :], in1=sg[m % 2][:, :], op=ALU.mult),
                         reads=[("ps", ub), ("sg", m % 2)], writes=[("act", m)])
                    if b + 1 < NB and m == 4:
                        emit_norm_act(b + 1, sq)
                    if b + 1 < NB and m == 12:
                        emit_norm_rest(b + 1, sq, rstd, hn[1 - hs], ("hn", 1 - hs), gain, PST)
                for mo in range(8):
                    wt, wkey = st_d.get(b * 8 + mo)
                    db = PD[mo % 2]

                    def mmd(e, wt=wt, db=db):
                        for k in range(NFF):
                            ins = e.matmul(ps[db][:, :], lhsT=wt[:, k, :], rhs=act[:, k, :], start=(k == 0), stop=(k == NFF - 1))
                        return ins
                    K.op("pe", mmd, reads=[wkey] + [("act", k) for k in range(NFF)], writes=[("ps", db)])
                    hv = h_res[:, mo, b * 512:(b + 1) * 512]
                    K.op("dve", lambda e, db=db, hv=hv: e.scalar_tensor_tensor(out=hv, in0=ps[db][:, :], scalar=0.5, in1=hv,
                                                                               op0=ALU.mult, op1=ALU.add),
                         reads=[("ps", db), hkey(mo, b)], writes=[hkey(mo, b)])

        def odd_phase(s, l):
            i = l // 2
            barrier()
            scr = ("scr", "odd", i)
            wv = ar.alloc("wv", [128, 8, 1024], BF16)
            hn = ar.alloc("hn", [128, 8, 512], BF16)
            sqg = ar.alloc("sqg", [128, 8, 512], BF16)
            rstd = ar.alloc("rstd", [128, 512], F32)
            v32 = ar.alloc("v32", [128, 1024], F32)
            ss = ar.alloc("ss", [128, 8], F32)
            vn = ar.alloc("vn", [128, 4, 1024], BF16)
            uf = [ar.alloc("uf", [128, 512], F32) for _ in range(2)]
            wu = [ar.alloc("wu", [128, 8, 128], BF16) for _ in range(3)]
            wo = [ar.alloc("wo", [128, 8, 128], BF16) for _ in range(3)]
            sgwT = ar.alloc("sgwT", [128, 4, 128], BF16)
            sgw_raw = ar.alloc("sgw_raw", [128, 4, 128], F32)
            sgb = ar.alloc("sgb", [128, 512], F32)
            gsg = ar.alloc("gsg", [128, 1024], F32)
            triu = ar.alloc("triu", [128, 128], F32)
            K.dma("sp", lambda e: e.dma_start(out=wv[:], in_=s_owv[i][:, :, :]), reads=[scr], writes=["wv"], sem=("o", 0))
            K.dma("sp", lambda e: e.dma_start(out=sgw_raw[:], in_=sg_w[i].rearrange("g t s -> t g s")), writes=["sgw_raw"], sem=("o", 1))
            K.dma("sp", lambda e: e.dma_start(out=triu[:], in_=c_triu[:, :]), writes=["triu"], sem=("o", 2))
            K.dma("sp", lambda e: e.dma_start(out=sgb[0:1, :], in_=sg_b[i:i + 1].rearrange("o g t -> o (g t)")), writes=["sgb"], sem=("o", 3))
            K.dma("sp", lambda e: e.dma_start(out=gsg[:], in_=sg_norm[i:i + 1, :].broadcast_to([128, 1024])), writes=["gsg"], sem=("o", 4))
            K.op("dve", lambda e: e.tensor_scalar(out=gsg[:], in0=gsg[:], scalar1=32.0, scalar2=None, op0=ALU.mult), reads=["gsg"], writes=["gsg"])
            for g in range(4):
                K.op("pe", lambda e, g=g: e.transpose(out=ps[g % 2][:, 0:128], in_=sgw_raw[:, g, :], identity=ident[:, :]),
                     reads=["sgw_raw", "ident"], writes=[("ps", g % 2)])
                K.op("dve", lambda e, g=g: e.tensor_tensor(out=sgwT[:, g, :], in0=ps[g % 2][:, 0:128], in1=triu[:, :], op=ALU.mult),
                     reads=[("ps", g % 2), "triu"], writes=["sgwT"])
            u_units = [(b, fc) for b in range(NB) for fc in range(8)]
            st_u = Stream(K, "wu", wu, u_units, lambda u, t: (lambda e: e.dma_start(out=t[:], in_=s_owu[i][u[1]])), lambda u: [scr])
            st_o = Stream(K, "wo", wo, u_units, lambda u, t: (lambda e: e.dma_start(out=t[:], in_=s_owo[i][u[1]])), lambda u: [scr])
            gain = g_mix[:, l, :]
            for b in range(NB):
                emit_norm_act(b, sqg)
                emit_norm_rest(b, sqg, rstd, hn, "hn", gain, 6)
                for ch in range(4):
                    for half in range(2):
                        bank = half

                        def mmv(e, ch=ch, half=half, bank=bank):
                            for k in range(8):
                                ins = e.matmul(ps[bank][:, :], lhsT=hn[:, k, ch * 128:(ch + 1) * 128], rhs=wv[:, k, half * 512:(half + 1) * 512],
                                               start=(k == 0), stop=(k == 7))
                            return ins
                        K.op("pe", mmv, reads=["hn", "wv"], writes=[("ps", bank)])
                        K.op("act", lambda e, half=half, bank=bank: e.activation(out=v32[:, half * 512:(half + 1) * 512], in_=ps[bank][:, :], func=AF.Gelu_apprx_tanh),
                             reads=[("ps", bank)], writes=[("v32", half)])
                    K.op("act", lambda e, ch=ch: e.activation(out=vn[:, ch, :], in_=v32[:, :], func=AF.Square, accum_out=ss[:, ch:ch + 1]),
                         reads=[("v32", 0), ("v32", 1)], writes=[("vn", ch), ("ss", ch)])
                    K.op("act", lambda e, ch=ch: e.activation(out=ss[:, ch:ch + 1], in_=ss[:, ch:ch + 1], func=AF.Sqrt, bias=float(D * EPS)),
                         reads=[("ss", ch)], writes=[("ss", ch)])
                    K.op("dve", lambda e, ch=ch: e.reciprocal(out=ss[:, ch:ch + 1], in_=ss[:, ch:ch + 1]), reads=[("ss", ch)], writes=[("ss", ch)])
                    K.op("dve", lambda e, ch=ch: e.scalar_tensor_tensor(out=vn[:, ch, :], in0=v32[:, :], scalar=ss[:, ch:ch + 1], in1=gsg[:, :],
                                                                        op0=ALU.mult, op1=ALU.mult),
                         reads=[("v32", 0), ("v32", 1), ("ss", ch), "gsg"], writes=[("vn", ch)])
                for fc in range(8):
                    g = fc // 2
                    wt, wkey = st_u.get(b * 8 + fc)
                    ub = 2 + fc % 2
                    mb = 4 + fc % 2

                    def mmu(e, wt=wt, ub=ub):
                        for k in range(8):
                            ins = e.matmul(ps[ub][:, :], lhsT=wt[:, k, :], rhs=hn[:, k, :], start=(k == 0), stop=(k == 7))
                        return ins
                    K.op("pe", mmu, reads=[wkey, "hn"], writes=[("ps", ub)])
                    K.op("act", lambda e, ub=ub, fc=fc: e.activation(out=uf[fc % 2][:, :], in_=ps[ub][:, :], func=AF.Gelu_apprx_tanh),
                         reads=[("ps", ub)], writes=[("uf", fc % 2)])

                    def mmx(e, fc=fc, g=g, mb=mb):
                        for ch in range(4):
                            e.matmul(ps[mb][:, ch * 128:(ch + 1) * 128], lhsT=vn[:, ch, fc * 128:(fc + 1) * 128], rhs=sgwT[:, g, :], start=True, stop=False)
                            ins = e.matmul(ps[mb][:, ch * 128:(ch + 1) * 128], lhsT=ones_f[0:1, :], rhs=sgb[0:1, g * 128:(g + 1) * 128], start=False, stop=True)
                        return ins
                    K.op("pe", mmx, reads=[("vn", c) for c in range(4)] + ["sgwT", "sgb", "ones_f"], writes=[("ps", mb)])
                    K.op("dve", lambda e, fc=fc, mb=mb: e.tensor_tensor(out=sqg[:, fc, :], in0=ps[mb][:, :], in1=uf[fc % 2][:, :], op=ALU.mult),
                         reads=[("ps", mb), ("uf", fc % 2)], writes=["sq"])
                for mo in range(8):
                    wt, wkey = st_o.get(b * 8 + mo)
                    db = 6 + mo % 2

                    def mmo(e, wt=wt, db=db):
                        for k in range(8):
                            ins = e.matmul(ps[db][:, :], lhsT=wt[:, k, :], rhs=sqg[:, k, :], start=(k == 0), stop=(k == 7))
                        return ins
                    K.op("pe", mmo, reads=[wkey, "sq"], writes=[("ps", db)])
                    hv = h_res[:, mo, b * 512:(b + 1) * 512]
                    K.op("dve", lambda e, db=db, hv=hv: e.tensor_tensor(out=hv, in0=ps[db][:, :], in1=hv, op=ALU.add),
                         reads=[("ps", db), hkey(mo, b)], writes=[hkey(mo, b)])

        s_kpe = dscr("s_kpe", [32, S], F32)
        TWO_PI = 2.0 * np.pi
        C1 = 6.28125
        C2 = TWO_PI - C1
        SC = 1.0 - 1e-6

        def tables_phase(s):
            barrier()
            invf = ar.alloc("invf", [128, 1], F32)
            pi_ = ar.alloc("pi_", [128, 512], I32)
            pf = ar.alloc("pf", [128, 512], F32)
            av = ar.alloc("av", [128, 512], F32)
            tf = ar.alloc("tf", [128, 512], F32)
            ki = ar.alloc("ki", [128, 512], I32)
            kf = ar.alloc("kf", [128, 512], F32)
            rr = ar.alloc("rr", [128, 512], F32)
            outt = [ar.alloc("outt", [128, 512], F32) for _ in range(2)]
            R = slice(64, 96)
            K.dma("sp", lambda e: e.dma_start(out=invf[:], in_=c_invfreq[:, :]), writes=["invf"], sem=("t", 0))
            for b in range(NB):
                bl = slice(b * 512, (b + 1) * 512)
                K.dma("sp", lambda e, bl=bl: e.dma_start(out=pi_[R, :], in_=positions[s:s + 1, bl].broadcast_to([32, 512])), writes=["pi"], sem=("t", 1))
                K.op("dve", lambda e: e.tensor_copy(out=pf[R, :], in_=pi_[R, :]), reads=["pi"], writes=["pf"])
                K.op("dve", lambda e: e.tensor_scalar(out=av[R, :], in0=pf[R, :], scalar1=invf[R, 0:1], scalar2=None, op0=ALU.mult), reads=["pf", "invf"], writes=["av"])
                for which in range(2):
                    off = 0.25 if which == 0 else 0.0
                    K.op("dve", lambda e, off=off: e.tensor_scalar(out=ki[R, :], in0=av[R, :], scalar1=float(1.0 / TWO_PI), scalar2=float(off), op0=ALU.mult, op1=ALU.add),
                         reads=["av"], writes=["ki"])
                    K.op("dve", lambda e: e.tensor_copy(out=kf[R, :], in_=ki[R, :]), reads=["ki"], writes=["kf"])
                    K.op("dve", lambda e: e.scalar_tensor_tensor(out=rr[R, :], in0=kf[R, :], scalar=float(-C1), in1=av[R, :], op0=ALU.mult, op1=ALU.add),
                         reads=["kf", "av"], writes=["rr"])
                    K.op("dve", lambda e: e.scalar_tensor_tensor(out=rr[R, :], in0=kf[R, :], scalar=float(-C2), in1=rr[R, :], op0=ALU.mult, op1=ALU.add),
                         reads=["kf", "rr"], writes=["rr"])
                    bias = float(np.pi / 2 * SC) if which == 0 else 0.0
                    K.op("act", lambda e, which=which, bias=bias: e.activation(out=outt[which][R, :], in_=rr[R, :], func=AF.Sin, scale=float(SC), bias=bias),
                         reads=["rr"], writes=[("outt", which)])
                    K.dma("sp", lambda e, which=which, bl=bl: e.dma_start(out=s_cs[s, which, :, bl], in_=outt[which][R, :]),
                          reads=[("outt", which)], writes=[("cs", b, which)], sem=("t", 2 + which))

        def even_phase(s, l):
            i = l // 2
            barrier()
            scr = ("scr", "even", i)
            R = slice(64, 96)
            cqn = ar.alloc("cqn", [128, 2, S], BF16)
            ckvn = ar.alloc("ckvn", [128, S], BF16)
            gq = ar.alloc("gq", [128, 2], F32)
            gkv = ar.alloc("gkv", [128, 1], F32)
            qg = ar.alloc("qg", [128, 1], F32)
            kg = ar.alloc("kg", [128, 1], F32)
            pscale = ar.alloc("pscale", [128, 4], F32)
            K.dma("sp", lambda e: e.dma_start(out=gq[:], in_=q_a_norm[i:i + 1, :].rearrange("o (c p) -> p (o c)", p=128)), writes=["gq"], sem=("e", 0))
            K.dma("sp", lambda e: e.dma_start(out=gkv[:], in_=kv_a_norm[i:i + 1, :].rearrange("o p -> p o")), writes=["gkv"], sem=("e", 1))
            K.dma("sp", lambda e: e.dma_start(out=qg[0:96, :], in_=q_norm[i:i + 1, :].rearrange("o p -> p o")), writes=["qg"], sem=("e", 2))
            K.dma("sp", lambda e: e.dma_start(out=kg[0:96, :], in_=k_norm[i:i + 1, :].rearrange("o p -> p o")), writes=["kg"], sem=("e", 3))
            K.dma("sp", lambda e: e.dma_start(out=pscale[:], in_=pool_scale[i:i + 1, :].rearrange("o (g p) -> p (o g)", p=128)), writes=["pscale"], sem=("e", 4))
            K.op("dve", lambda e: e.tensor_scalar(out=gq[:], in0=gq[:], scalar1=16.0, scalar2=None, op0=ALU.mult), reads=["gq"], writes=["gq"])
            K.op("dve", lambda e: e.tensor_scalar(out=gkv[:], in0=gkv[:], scalar1=float(np.sqrt(128.0)), scalar2=None, op0=ALU.mult), reads=["gkv"], writes=["gkv"])
            K.op("dve", lambda e: e.tensor_scalar(out=kg[0:96, :], in0=kg[0:96, :], scalar1=float(np.sqrt(96.0)), scalar2=None, op0=ALU.mult), reads=["kg"], writes=["kg"])
            mA = ar.mark()
            hn = ar.alloc("hn", [128, 8, 512], BF16)
            sq = ar.alloc("sq", [128, 8, 512], BF16)
            rstd = ar.alloc("rstd", [128, 512], F32)
            rstc = ar.alloc("rstc", [128, 512], F32)
            win = [ar.alloc("win", [128, 8, 128], BF16) for _ in range(3)]
            wkpe = ar.alloc("wkpe", [128, 8, 32], BF16)
            pbuf = ar.alloc("pbuf", [128, 528], F32)
            t1 = ar.alloc("t1", [128, 528], F32)
            t2 = ar.alloc("t2", [128, 528], F32)
            halo = ar.alloc("halo", [128, 4, 16], F32)
            pooled = [ar.alloc("pooled", [128, 512], BF16) for _ in range(2)]
            pfix = ar.alloc("pfix", [128, 16], F32)
            pout = ar.alloc("pout", [128, 4, 512], BF16)
            pw = ar.alloc("pw", [128, 4, 128], BF16)
            invc = ar.alloc("invc", [128, 64], F32)
            wop = [ar.alloc("wop", [128, 4, 128], BF16) for _ in range(3)]
            kst = ar.alloc("kst", [128, 512], F32)
            K.dma("sp", lambda e: e.dma_start(out=wkpe[:], in_=s_ekpe[i][:, :, :]), reads=[scr], writes=["wkpe"], sem=("e", 5))
            K.dma("sp", lambda e: e.dma_start(out=pw[:], in_=s_epw[i][:, :, :]), reads=[scr], writes=["pw"], sem=("e", 6))
            K.dma("sp", lambda e: e.dma_start(out=invc[:], in_=c_invcnt[:, :]), writes=["invc"], sem=("e", 7))
            K.op("pool", lambda e: e.memset(halo[:], 0.0), writes=["halo"])
            in_units = [(b, t) for b in range(NB) for t in range(7)]
            st_in = Stream(K, "win", win, in_units, lambda u, t: (lambda e: e.dma_start(out=t[:], in_=s_ewin[i][u[1]])), lambda u: [scr])
            op_units = [(b, mo) for b in range(NB) for mo in range(8)]
            st_op = Stream(K, "wop", wop, op_units, lambda u, t: (lambda e: e.dma_start(out=t[:], in_=s_ewop[i][u[1]])), lambda u: [scr])
            gain = g_mix[:, l, :]
            WIN = (2, 4, 8, 16)
            for b in range(NB):
                bl = slice(b * 512, (b + 1) * 512)
                emit_norm_act(b, sq)
                emit_norm_rest(b, sq, rstd, hn, "hn", gain, 6)
                banks = [0, 1, 2, 4, 5, 4, 5]
                for t in range(7):
                    wt, wkey = st_in.get(b * 7 + t)
                    bk = banks[t]

                    def mmi(e, wt=wt, bk=bk):
                        for k in range(8):
                            ins = e.matmul(ps[bk][:, :], lhsT=wt[:, k, :], rhs=hn[:, k, :], start=(k == 0), stop=(k == 7))
                        return ins
                    K.op("pe", mmi, reads=[wkey, "hn"], writes=[("ps", bk)])
                    if t == 1:
                        for c in range(2):
                            K.op("act", lambda e, c=c: e.activation(out=sq[:, c, :], in_=ps[c][:, :], func=AF.Square), reads=[("ps", c)], writes=["sq"])

                        def mms(e):
                            e.matmul(ps[6][:, :], lhsT=ones_bf[:, :], rhs=sq[:, 0, :], start=True, stop=False)
                            return e.matmul(ps[6][:, :], lhsT=ones_bf[:, :], rhs=sq[:, 1, :], start=False, stop=True)
                        K.op("pe", mms, reads=["sq", "ones_bf"], writes=[("ps", 6)])
                        K.op("act", lambda e: e.activation(out=rstc[:, :], in_=ps[6][:, :], func=AF.Sqrt, bias=float(256 * EPS)), reads=[("ps", 6)], writes=["rstc"])
                        K.op("dve", lambda e: e.reciprocal(out=rstc[:, :], in_=rstc[:, :]), reads=["rstc"], writes=["rstc"])
                        for c in range(2):
                            K.op("dve", lambda e, c=c, bl=bl: e.scalar_tensor_tensor(out=cqn[:, c, bl], in0=ps[c][:, :], scalar=gq[:, c:c + 1], in1=rstc[:, :],
                                                                                     op0=ALU.mult, op1=ALU.mult),
                                 reads=[("ps", c), "gq", "rstc"], writes=[("cqn", b)])
                    if t == 2:
                        K.op("act", lambda e: e.activation(out=sq[:, 2, :], in_=ps[2][:, :], func=AF.Square), reads=[("ps", 2)], writes=["sq"])
                        K.op("pe", lambda e: e.matmul(ps[6][:, :], lhsT=ones_bf[:, :], rhs=sq[:, 2, :], start=True, stop=True), reads=["sq", "ones_bf"], writes=[("ps", 6)])
                        K.op("act", lambda e: e.activation(out=rstc[:, :], in_=ps[6][:, :], func=AF.Sqrt, bias=float(128 * EPS)), reads=[("ps", 6)], writes=["rstc"])
                        K.op("dve", lambda e: e.reciprocal(out=rstc[:, :], in_=rstc[:, :]), reads=["rstc"], writes=["rstc"])
                        K.op("dve", lambda e, bl=bl: e.scalar_tensor_tensor(out=ckvn[:, bl], in0=ps[2][:, :], scalar=gkv[:, 0:1], in1=rstc[:, :], op0=ALU.mult, op1=ALU.mult),
                             reads=[("ps", 2), "gkv", "rstc"], writes=[("ckvn", b)])

                        def mmk(e):
                            for k in range(8):
                                ins = e.matmul(ps[3][R, :], lhsT=wkpe[:, k, :], rhs=hn[:, k, :], start=(k == 0), stop=(k == 7))
                            return ins
                        K.op("pe", mmk, reads=["wkpe", "hn"], writes=[("ps", 3)])
                        K.op("act", lambda e: e.activation(out=kst[R, :], in_=ps[3][R, :], func=AF.Copy), reads=[("ps", 3)], writes=["kst"])
                        K.dma("sp", lambda e, bl=bl: e.dma_start(out=s_kpe[:, bl], in_=kst[R, :]), reads=["kst"], writes=[("kpe", b)], sem=("e", 8))
                    if t >= 3:
                        g = t - 3
                        w = WIN[g]
                        K.op("act", lambda e, bk=bk: e.activation(out=pbuf[:, 16:528], in_=ps[bk][:, :], func=AF.Copy), reads=[("ps", bk)], writes=["pbuf"])
                        K.op("pool", lambda e, g=g: e.tensor_copy(out=pbuf[:, 0:16], in_=halo[:, g, :]), reads=["halo"], writes=["pbuf"])
                        K.op("pool", lambda e: e.tensor_tensor(out=t1[:, 1:528], in0=pbuf[:, 1:528], in1=pbuf[:, 0:527], op=ALU.add), reads=["pbuf"], writes=["t1"])
                        fin = t1
                        if g >= 1:
                            K.op("pool", lambda e: e.tensor_tensor(out=t2[:, 3:528], in0=t1[:, 3:528], in1=t1[:, 1:526], op=ALU.add), reads=["t1"], writes=["t2"])
                            fin = t2
                        if g >= 2:
                            K.op("pool", lambda e: e.tensor_tensor(out=t1[:, 7:528], in0=t2[:, 7:528], in1=t2[:, 3:524], op=ALU.add), reads=["t2"], writes=["t1"])
                            fin = t1
                        if g >= 3:
                            K.op("pool", lambda e: e.tensor_tensor(out=t2[:, 15:528], in0=t1[:, 15:528], in1=t1[:, 7:520], op=ALU.add), reads=["t1"], writes=["t2"])
                            fin = t2
                        fkey = "t1" if fin is t1 else "t2"
                        pl = pooled[g % 2]
                        K.op("dve", lambda e, fin=fin, pl=pl, w=w: e.scalar_tensor_tensor(out=pl[:, :], in0=fin[:, 16:528], scalar=float(1.0 / w), in1=pbuf[:, 16:528],
                                                                                           op0=ALU.mult, op1=ALU.subtract),
                             reads=[fkey, "pbuf"], writes=[("pooled", g % 2)])
                        if b == 0:
                            K.op("dve", lambda e, fin=fin, g=g: e.tensor_tensor(out=pfix[:, :], in0=fin[:, 16:32], in1=invc[:, g * 16:(g + 1) * 16], op=ALU.mult),
                                 reads=[fkey, "invc"], writes=["pfix"])
                            K.op("dve", lambda e, pl=pl: e.tensor_tensor(out=pl[:, 0:16], in0=pfix[:, :], in1=pbuf[:, 16:32], op=ALU.subtract),
                                 reads=["pfix", "pbuf", ("pooled", g % 2)], writes=[("pooled", g % 2)])
                        K.op("pool", lambda e, g=g: e.tensor_copy(out=halo[:, g, :], in_=pbuf[:, 512:528]), reads=["pbuf"], writes=["halo"])
                        K.op("pe", lambda e, g=g, pl=pl: e.matmul(ps[7][:, :], lhsT=pw[:, g, :], rhs=pl[:, :], start=True, stop=True),
                             reads=["pw", ("pooled", g % 2)], writes=[("ps", 7)])
                        K.op("act", lambda e, g=g: e.activation(out=pout[:, g, :], in_=ps[7][:, :], func=AF.Copy, scale=pscale[:, g:g + 1]),
                             reads=[("ps", 7), "pscale"], writes=[("pout", g)])
                for mo in range(8):
                    wt, wkey = st_op.get(b * 8 + mo)
                    db = mo % 2

                    def mmo(e, wt=wt, db=db):
                        for g in range(4):
                            ins = e.matmul(ps[db][:, :], lhsT=wt[:, g, :], rhs=pout[:, g, :], start=(g == 0), stop=(g == 3))
                        return ins
                    K.op("pe", mmo, reads=[wkey] + [("pout", g) for g in range(4)], writes=[("ps", db)])
                    hv = h_res[:, mo, bl]
                    K.op("dve", lambda e, db=db, hv=hv: e.tensor_tensor(out=hv, in0=ps[db][:, :], in1=hv, op=ALU.add),
                         reads=[("ps", db), hkey(mo, b)], writes=[hkey(mo, b)])

            K.barrier_all()
            K.res.update({k: [v[0], {}] for k, v in scr_tokens.items()})
            ar.reset(mA)
            kh = ar.alloc("kh", [128, S], BF16)
            vx = ar.alloc("vx", [128, NCH, 65], BF16)
            wuq = [ar.alloc("wuq", [128, 2, 96], BF16) for _ in range(2)]
            wukv = ar.alloc("wukv", [128, 1024], BF16)
            woa = [ar.alloc("woa", [128, 1024], BF16) for _ in range(2)]
            cosT = ar.alloc("cosT", [128, 512], F32)
            sinT = ar.alloc("sinT", [128, 512], F32)
            xk = ar.alloc("xk", [128, 512], F32)
            sqk = ar.alloc("sqk", [128, 512], BF16)
            rstb = ar.alloc("rstb", [128, 512], F32)
            xr = ar.alloc("xr", [128, 512], F32)
            tmp = ar.alloc("tmp", [128, 512], F32)
            qh = [ar.alloc("qh", [128, 512], BF16) for _ in range(2)]
            pT = [ar.alloc("pT", [128, 512], BF16) for _ in range(3)]
            rl = ar.alloc("rl", [128, 512], F32)
            bc = ar.alloc("bc", [128, 512], F32)
            ao = ar.alloc("ao", [128, 512], BF16)
            rotm = ar.alloc("rotm", [128, 128], F32)
            trb = ar.alloc("trb", [128, 128], BF16)
            K.dma("sp", lambda e: e.dma_start(out=wukv[:], in_=s_eukv[i][:, :]), reads=[scr], writes=["wukv"], sem=("e", 9))
            K.dma("sp", lambda e: e.dma_start(out=rotm[:], in_=c_rotm[:, :]), writes=["rotm"], sem=("e", 10))
            K.dma("pool", lambda e: e.dma_start(out=trb[:], in_=c_triu[:, :]), writes=["trb"], sem=("e", 11))
            K.op("pool", lambda e: e.memset(vx[:, :, 64:65], 1.0), writes=["vx1"])

            def norm_rope(src_nope, src_rope, rope_key, gcol, dst, dst_key, nrows_nope, j):
                bl = slice(j * 512, (j + 1) * 512)
                K.dma("sp", lambda e, bl=bl: e.dma_start(out=cosT[R, :], in_=s_cs[s, 0, :, bl]), reads=[("cs", j, 0)], writes=["cosT"], sem=("e", 12))
                K.dma("sp", lambda e, bl=bl: e.dma_start(out=sinT[R, :], in_=s_cs[s, 1, :, bl]), reads=[("cs", j, 1)], writes=["sinT"], sem=("e", 13))
                if src_rope is None:
                    K.op("act", lambda e: e.activation(out=sqk[0:96, :], in_=src_nope[0:96, :], func=AF.Square), reads=[("ps", 0)], writes=["sqk"])
                    rsrc = src_nope
                else:
                    K.op("act", lambda e: e.activation(out=sqk[0:64, :], in_=src_nope[0:64, :], func=AF.Square), reads=[("ps", 0)], writes=["sqk"])
                    K.op("act", lambda e: e.activation(out=sqk[R, :], in_=src_rope[R, :], func=AF.Square), reads=[rope_key, "sqk"], writes=["sqk"])
                    rsrc = src_rope
                K.op("pe", lambda e: e.matmul(ps[1][0:96, :], lhsT=ones_bf[0:96, 0:96], rhs=sqk[0:96, :], start=True, stop=True), reads=["sqk", "ones_bf"], writes=[("ps", 1)])
                K.op("act", lambda e: e.activation(out=rstb[0:96, :], in_=ps[1][0:96, :], func=AF.Sqrt, bias=float(96 * EPS)), reads=[("ps", 1)], writes=["rstb"])
                K.op("dve", lambda e: e.reciprocal(out=rstb[0:96, :], in_=rstb[0:96, :]), reads=["rstb"], writes=["rstb"])
                K.op("dve", lambda e: e.scalar_tensor_tensor(out=dst[0:64, :], in0=src_nope[0:64, :], scalar=gcol[0:64, 0:1], in1=rstb[0:64, :], op0=ALU.mult, op1=ALU.mult),
                     reads=[("ps", 0), "rstb", "qg", "kg"], writes=[dst_key])
                K.op("dve", lambda e: e.scalar_tensor_tensor(out=xr[R, :], in0=rsrc[R, :], scalar=gcol[R, 0:1], in1=rstb[R, :], op0=ALU.mult, op1=ALU.mult),
                     reads=[("ps", 0), rope_key, "rstb", "qg", "kg"], writes=["xr"])
                K.op("pe", lambda e: e.matmul(ps[2][R, :], lhsT=rotm[R, 0:32], rhs=xr[R, :], start=True, stop=True), reads=["xr", "rotm"], writes=[("ps", 2)])
                K.op("dve", lambda e: e.tensor_tensor(out=tmp[R, :], in0=ps[2][R, :], in1=sinT[R, :], op=ALU.mult), reads=[("ps", 2), "sinT"], writes=["tmp"])
                K.op("pool", lambda e: e.tensor_tensor(out=xr[R, :], in0=xr[R, :], in1=cosT[R, :], op=ALU.mult), reads=["xr", "cosT", ("ps", 2)], writes=["xr"])
                K.op("dve", lambda e: e.tensor_tensor(out=dst[R, :], in0=xr[R, :], in1=tmp[R, :], op=ALU.add), reads=["xr", "tmp"], writes=[dst_key])

            sc_i = 0
            for h in range(8):
                hs = h % 2
                K.dma("sp", lambda e, h=h, hs=hs: e.dma_start(out=wuq[hs][:], in_=s_euq[i][h]), reads=[scr], writes=[("wuq", hs)], sem=("wuq", hs))
                K.dma("sp", lambda e, h=h, hs=hs: e.dma_start(out=woa[hs][0:64, :], in_=s_ewoa[i][h * 64:(h + 1) * 64, :]), reads=[scr], writes=[("woa", hs)], sem=("woa", hs))
                for j in range(NB):
                    bl = slice(j * 512, (j + 1) * 512)
                    K.dma("sp", lambda e, bl=bl: e.dma_start(out=xk[R, :], in_=s_kpe[:, bl]), reads=[("kpe", j)], writes=["xk"], sem=("e", 14))
                    K.op("pe", lambda e, h=h, bl=bl: e.matmul(ps[0][0:64, :], lhsT=wukv[:, h * 128:h * 128 + 64], rhs=ckvn[:, bl], start=True, stop=True),
                         reads=["wukv", ("ckvn", j)], writes=[("ps", 0)])
                    norm_rope(ps[0], xk, "xk", kg, kh[:, bl], ("kh", j), 64, j)
                for c0 in range(0, NCH, 8):
                    nq = min(8, NCH - c0)

                    def mmv(e, h=h, c0=c0, nq=nq):
                        for q in range(nq):
                            ins = e.matmul(ps[3][:, q * 64:(q + 1) * 64], lhsT=ckvn[:, (c0 + q) * 128:(c0 + q + 1) * 128], rhs=wukv[:, h * 128 + 64:(h + 1) * 128],
                                           start=True, stop=True)
                        return ins
                    K.op("pe", mmv, reads=["wukv"] + [("ckvn", (c0 + q) // 4) for q in range(nq)], writes=[("ps", 3)])
                    K.op("act", lambda e, c0=c0, nq=nq: e.activation(out=vx[:, c0:c0 + nq, 0:64], in_=ps[3][:, 0:nq * 64].rearrange("p (q n) -> p q n", q=nq), func=AF.Copy),
                         reads=[("ps", 3)], writes=[("vx", c0 // 8)])
                for j in range(NB):
                    bl = slice(j * 512, (j + 1) * 512)
                    qt = qh[j % 2]

                    def mmq(e, hs=hs, bl=bl):
                        e.matmul(ps[0][0:96, :], lhsT=wuq[hs][:, 0, :], rhs=cqn[:, 0, bl], start=True, stop=False)
                        return e.matmul(ps[0][0:96, :], lhsT=wuq[hs][:, 1, :], rhs=cqn[:, 1, bl], start=False, stop=True)
                    K.op("pe", mmq, reads=[("wuq", hs), ("cqn", j)], writes=[("ps", 0)])
                    norm_rope(ps[0], None, ("ps", 0), qg, qt, ("qh", j % 2), 96, j)
                    nkt = 4 * (j + 1)
                    for kt in range(nkt):
                        d = kt - 4 * j
                        lo = max(0, d) * 128
                        sb = 4 + sc_i % 2
                        pt = pT[sc_i % 3]
                        pkey = ("pT", sc_i % 3)
                        sc_i += 1
                        K.op("pe", lambda e, kt=kt, lo=lo, sb=sb, qt=qt: e.matmul(ps[sb][:, lo:512], lhsT=kh[0:96, kt * 128:(kt + 1) * 128], rhs=qt[0:96, lo:512], start=True, stop=True),
                             reads=[("kh", kt // 4), ("qh", j % 2)], writes=[("ps", sb)])
                        K.op("act", lambda e, lo=lo, sb=sb, pt=pt: e.activation(out=pt[:, lo:512], in_=ps[sb][:, lo:512], func=AF.Exp), reads=[("ps", sb)], writes=[pkey])
                        if d >= 0:
                            K.op("pool", lambda e, lo=lo, pt=pt: e.tensor_tensor(out=pt[:, lo:lo + 128], in0=pt[:, lo:lo + 128], in1=trb[:, :], op=ALU.mult),
                                 reads=[pkey, "trb"], writes=[pkey])
                        K.op("pe", lambda e, kt=kt, lo=lo, pt=pt, nkt=nkt: e.matmul(ps[6][0:65, lo:512], lhsT=vx[:, kt, 0:65], rhs=pt[:, lo:512], start=(kt == 0), stop=(kt == nkt - 1)),
                             reads=[pkey, ("vx", kt // 8), "vx1"], writes=[("ps", 6)])
                    K.op("dve", lambda e: e.reciprocal(out=rl[64:65, :], in_=ps[6][64:65, :]), reads=[("ps", 6)], writes=["rl"])
                    K.op("pe", lambda e: e.matmul(ps[2][0:64, :], lhsT=ones_f[64:65, 0:64], rhs=rl[64:65, :], start=True, stop=True), reads=["rl", "ones_f"], writes=[("ps", 2)])
                    K.op("act", lambda e: e.activation(out=bc[0:64, :], in_=ps[2][0:64, :], func=AF.Copy), reads=[("ps", 2)], writes=["bc"])
                    K.op("dve", lambda e: e.tensor_tensor(out=ao[0:64, :], in0=ps[6][0:64, :], in1=bc[0:64, :], op=ALU.mult), reads=[("ps", 6), "bc"], writes=["ao"])
                    for mo in range(8):
                        db = 7 if mo % 2 == 0 else 3
                        K.op("pe", lambda e, mo=mo, db=db, hs=hs: e.matmul(ps[db][:, :], lhsT=woa[hs][0:64, mo * 128:(mo + 1) * 128], rhs=ao[0:64, :], start=True, stop=True),
                             reads=[("woa", hs), "ao"], writes=[("ps", db)])
                        hv = h_res[:, mo, bl]
                        K.op("dve", lambda e, db=db, hv=hv: e.tensor_tensor(out=hv, in0=ps[db][:, :], in1=hv, op=ALU.add),
                             reads=[("ps", db), hkey(mo, j)], writes=[hkey(mo, j)])

        PHASES = {"ffn": ffn_phase, "odd": odd_phase, "even": even_phase}

        for s in range(NSEQ):
            load_seq(s)
            if need_even:
                tables_phase(s)
            for (l, p) in plan:
                if p == "ffn1":
                    ffn_phase(s, l, 0)
                elif p == "ffn2":
                    ffn_phase(s, l, 1)
                elif l % 2 == 1:
                    PHASES["odd"](s, l)
                else:
                    PHASES["even"](s, l)
            store_seq(s)
        K.barrier_all()
        with nc.allow_non_contiguous_dma(reason="small constant loads"):
            K.emit()
    return nc


FULL_PLAN = [(l, p) for l in range(DEPTH) for p in ("ffn1", "mix", "ffn2")]


def make_consts():
    ident = np.eye(128, dtype=np.float32)
    triu = np.triu(np.ones((128, 128), np.float32))
    rotm = np.zeros((128, 128), np.float32)
    for i in range(16):
        rotm[64 + i + 16, i] = -1.0
        rotm[64 + i, i + 16] = 1.0
    invf = np.zeros((128, 1), np.float32)
    f = (10000.0 ** (-np.arange(0, 32, 2, dtype=np.float32) / 32)).astype(np.float32)
    invf[64:80, 0] = f
    invf[80:96, 0] = f
    invcnt = np.zeros((128, 64), np.float32)
    for g, w in enumerate((2, 4, 8, 16)):
        for t in range(16):
            invcnt[:, g * 16 + t] = 1.0 / min(t + 1, w)
    return {"c_ident": ident, "c_triu": triu, "c_rotm": rotm, "c_invfreq": invf, "c_invcnt": invcnt}


_CACHE = {}


def kernel(**inputs):
    n = 8
    S = inputs["x"].shape[1]
    B = inputs["x"].shape[0]
    nseq = B // n
    key = (S, nseq)
    if key not in _CACHE:
        _CACHE[key] = build(S, nseq, FULL_PLAN)
    nc = _CACHE[key]
    consts = make_consts()
    in_maps = []
    for c in range(n):
        m = {k: np.ascontiguousarray(v) for k, v in inputs.items() if k not in ("x", "positions")}
        m["x"] = np.ascontiguousarray(inputs["x"][c * nseq:(c + 1) * nseq])
        m["positions"] = np.ascontiguousarray(inputs["positions"][c * nseq:(c + 1) * nseq]).astype(np.int32)
        m.update(consts)
        in_maps.append(m)
    res = run_bass_kernel_spmd(nc, in_maps, core_ids=list(range(n)))
    return np.concatenate([np.asarray(r["y"]) for r in res.results], axis=0).astype(np.float32)
```

```python
import contextlib
import numpy as np
import concourse.bass as bass
import concourse.mybir as mybir
from concourse.bass_utils import run_bass_kernel_spmd

F32 = mybir.dt.float32
BF16 = mybir.dt.bfloat16
I32 = mybir.dt.int32
AF = mybir.ActivationFunctionType
ALU = mybir.AluOpType

D = 1024
DFF = 2816
NFF = 22
DEPTH = 4
EPS = 1e-6
SB_BASE = 16512
SB_END = 229376
HRES_BYTES = 8 * 4096 * 4
ENGS = ("pe", "act", "dve", "pool", "sp")


def _dsize(dt):
    return 2 if dt == BF16 else 4


class Sched:
    def __init__(self, nc, stack):
        self.nc = nc
        self.stack = stack
        self.q = {e: [] for e in ENGS}
        self.tok = {}
        self.sem = {}
        self.seen = {}
        self.res = {}

    def _semh(self, key):
        if key not in self.sem:
            name = "s_" + "_".join(str(x) for x in key)
            self.sem[key] = self.stack.enter_context(self.nc.semaphore(name))
        return self.sem[key]

    def _collect(self, eng, reads, writes):
        need = {}

        def add(t):
            if t is not None:
                k, v = t
                if need.get(k, 0) < v:
                    need[k] = v

        for r in reads:
            e = self.res.get(r)
            if e is not None:
                add(e[0])
        for w in writes:
            e = self.res.get(w)
            if e is not None:
                add(e[0])
                for k, v in e[1].items():
                    add((k, v))
        waits = []
        for k, v in need.items():
            if eng == "pe" and k == ("eng", "pe"):
                continue
            if self.seen.get((eng, k), 0) < v:
                waits.append((k, v))
                self.seen[(eng, k)] = v
        return waits

    def _commit(self, reads, writes, token):
        k, v = token
        for r in reads:
            e = self.res.setdefault(r, [None, {}])
            if e[1].get(k, 0) < v:
                e[1][k] = v
        for w in writes:
            self.res[w] = [token, {}]

    def op(self, eng, fn, reads=(), writes=()):
        waits = self._collect(eng, reads, writes)
        key = ("eng", eng)
        self._semh(key)
        self.tok[key] = self.tok.get(key, 0) + 1
        self._commit(reads, writes, (key, self.tok[key]))
        self.q[eng].append((waits, fn, (key, 1)))

    def dma(self, queue, fn, reads=(), writes=(), sem=None, token_val=None):
        waits = self._collect(queue, reads, writes)
        key = ("dma",) + tuple(sem)
        self._semh(key)
        self.tok[key] = self.tok.get(key, 0) + 16
        tv = self.tok[key] if token_val is None else token_val
        self._commit(reads, writes, (key, tv))
        self.q[queue].append((waits, fn, (key, 16)))

    def barrier_all(self):
        waits = []
        for k, v in self.tok.items():
            if k[0] == "dma" and self.seen.get(("sp", k), 0) < v:
                waits.append((k, v))
                self.seen[("sp", k)] = v
        for k, v in self.tok.items():
            if k[0] == "eng" and k != ("eng", "sp") and self.seen.get(("sp", k), 0) < v:
                waits.append((k, v))
                self.seen[("sp", k)] = v
        key = ("eng", "sp")
        self._semh(key)
        self.tok[key] = self.tok.get(key, 0) + 1
        self.q["sp"].append((waits, (lambda e: e.nop()), (key, 1)))
        for e in ENGS:
            if e == "sp":
                continue
            ws = []
            for k, v in self.tok.items():
                if k[0] == "eng" and k != ("eng", e) and self.seen.get((e, k), 0) < v:
                    ws.append((k, v))
                    self.seen[(e, k)] = v
            for k, v in self.tok.items():
                if k[0] == "dma":
                    self.seen[(e, k)] = max(self.seen.get((e, k), 0), v)
            own = ("eng", e)
            if own in self.tok:
                ws.append((own, self.tok[own]))
                self.seen[(e, own)] = self.tok[own]
            if ws:
                self.q[e].append((ws, None, None))
        self.res = {}

    def emit(self):
        nc = self.nc
        with nc.Block() as block:
            decos = {"pe": block.tensor, "act": block.scalar, "dve": block.vector,
                     "pool": block.gpsimd, "sp": block.sync}
            for eng in ENGS:
                def body(e, eng=eng):
                    for waits, fn, inc in self.q[eng]:
                        if fn is None:
                            for k, v in waits:
                                e.wait_ge(self.sem[k], v)
                            continue
                        for k, v in waits[:-1]:
                            e.wait_ge(self.sem[k], v)
                        att = (self.sem[waits[-1][0]], waits[-1][1]) if waits else None
                        ins = fn(_EngProxy(e, att))
                        ins.then_inc(self.sem[inc[0]], inc[1])
                decos[eng](body)


class _EngProxy:
    def __init__(self, e, att):
        self._e = e
        self._att = att

    def __getattr__(self, name):
        f = getattr(self._e, name)

        def call(*a, **kw):
            ins = f(*a, **kw)
            if self._att is not None:
                ins._wait_ge(self._att[0], self._att[1])
                self._att = None
            return ins
        return call


class Arena:
    def __init__(self, nc, lo, hi):
        self.nc = nc
        self.lo = lo
        self.hi = hi
        self.cur = lo
        self.n = 0

    def alloc(self, name, shape, dt):
        nbytes = int(np.prod(shape[1:])) * _dsize(dt)
        nbytes = (nbytes + 31) // 32 * 32
        assert self.cur + nbytes <= self.hi, (name, self.cur, nbytes, self.hi)
        self.n += 1
        t = self.nc.alloc_sbuf_tensor_at("%s_%d" % (name, self.n), list(shape), dt, offset=self.cur)
        self.cur += nbytes
        return t

    def mark(self):
        return self.cur

    def reset(self, m):
        self.cur = m


class Stream:
    def __init__(self, K, name, slots, units, loader, src_keys):
        self.K = K
        self.name = name
        self.slots = slots
        self.units = units
        self.loader = loader
        self.src_keys = src_keys
        self.issued = 0

    def key(self, i):
        return ("ws", self.name, i % len(self.slots))

    def _issue_upto(self, n):
        while self.issued < min(n, len(self.units)):
            i = self.issued
            tile = self.slots[i % len(self.slots)]
            fn = self.loader(self.units[i], tile)
            self.K.dma("sp", fn, reads=self.src_keys(self.units[i]), writes=[self.key(i)],
                       sem=(self.name, i % len(self.slots)))
            self.issued += 1

    def get(self, i):
        self._issue_upto(i + len(self.slots))
        return self.slots[i % len(self.slots)], self.key(i)


def build(S, NSEQ, plan):
    assert S % 512 == 0
    NB = S // 512
    NCH = S // 128
    nc = bass.Bass("TRN2", target_bir_lowering=False)

    def din(name, shape, dt=F32):
        return nc.dram_tensor(name, list(shape), dt, kind="ExternalInput").ap()

    def dscr(name, shape, dt=BF16):
        return nc.dram_tensor(name, list(shape), dt, kind="Internal").ap()

    x = din("x", [NSEQ, S, D])
    positions = din("positions", [NSEQ, S], I32)
    ffn_norm = din("ffn_norm", [DEPTH, 2, D])
    ffn_w_gate = din("ffn_w_gate", [DEPTH, 2, D, DFF])
    ffn_w_up = din("ffn_w_up", [DEPTH, 2, D, DFF])
    ffn_w_down = din("ffn_w_down", [DEPTH, 2, DFF, D])
    mix_norm = din("mix_norm", [DEPTH, D])
    even_w_in = din("even_w_in", [2, D, 928])
    q_a_norm = din("q_a_norm", [2, 256])
    kv_a_norm = din("kv_a_norm", [2, 128])
    w_uq = din("w_uq", [2, 256, 768])
    w_ukv = din("w_ukv", [2, 128, 1024])
    q_norm = din("q_norm", [2, 96])
    k_norm = din("k_norm", [2, 96])
    pool_w = din("pool_w", [2, 4, 128, 128])
    pool_scale = din("pool_scale", [2, 512])
    even_w_out = din("even_w_out", [2, D, D])
    odd_w_in = din("odd_w_in", [2, D, 2048])
    sg_norm = din("sg_norm", [2, D])
    sg_w = din("sg_w", [2, 4, 128, 128])
    sg_b = din("sg_b", [2, 4, 128])
    odd_w_out = din("odd_w_out", [2, D, D])
    c_ident = din("c_ident", [128, 128])
    c_triu = din("c_triu", [128, 128])
    c_maskneg = din("c_maskneg", [128, 128])
    c_rotm = din("c_rotm", [128, 128])
    c_invfreq = din("c_invfreq", [128, 1])
    c_invcnt = din("c_invcnt", [128, 64])
    y = nc.dram_tensor("y", [NSEQ, S, D], F32, kind="ExternalOutput").ap()

    need_ffn = sorted({(l, j) for (l, p) in plan for j in ((0,) if p == "ffn1" else (1,) if p == "ffn2" else ())})
    need_even = sorted({l // 2 for (l, p) in plan if p == "mix" and l % 2 == 0})
    need_odd = sorted({l // 2 for (l, p) in plan if p == "mix" and l % 2 == 1})

    s_gu = {lj: dscr("s_gu_%d_%d" % lj, [NFF, 128, 2, 8, 128]) for lj in need_ffn}
    s_d = {lj: dscr("s_d_%d_%d" % lj, [8, 128, NFF, 128]) for lj in need_ffn}
    s_ewin = {i: dscr("s_ewin_%d" % i, [7, 128, 8, 128]) for i in need_even}
    s_ekpe = {i: dscr("s_ekpe_%d" % i, [128, 8, 32]) for i in need_even}
    s_euq = {i: dscr("s_euq_%d" % i, [8, 128, 2, 96]) for i in need_even}
    s_eukv = {i: dscr("s_eukv_%d" % i, [128, 1024]) for i in need_even}
    s_epw = {i: dscr("s_epw_%d" % i, [128, 4, 128]) for i in need_even}
    s_ewoa = {i: dscr("s_ewoa_%d" % i, [512, 1024]) for i in need_even}
    s_ewop = {i: dscr("s_ewop_%d" % i, [8, 128, 4, 128]) for i in need_even}
    s_owu = {i: dscr("s_owu_%d" % i, [8, 128, 8, 128]) for i in need_odd}
    s_owv = {i: dscr("s_owv_%d" % i, [128, 8, 1024]) for i in need_odd}
    s_owo = {i: dscr("s_owo_%d" % i, [8, 128, 8, 128]) for i in need_odd}
    s_cs = dscr("s_cs", [NSEQ, 2, 32, S], F32)

    stack = contextlib.ExitStack()
    with stack:
        K = Sched(nc, stack)
        h_res = nc.alloc_sbuf_tensor_at("h_res", [128, 8, S], F32, offset=SB_BASE)
        ar = Arena(nc, SB_BASE + HRES_BYTES, SB_END)
        ps = [stack.enter_context(nc.psum_tensor("ps%d" % i, [128, 512], F32)) for i in range(8)]

        ones_bf = ar.alloc("ones_bf", [128, 128], BF16)
        ones_f = ar.alloc("ones_f", [128, 128], F32)
        ident = ar.alloc("ident", [128, 128], F32)
        g_ffn = ar.alloc("g_ffn", [128, 8, 8], F32)
        g_mix = ar.alloc("g_mix", [128, 4, 8], F32)
        K.op("dve", lambda e: e.memset(ones_bf[:], 1.0), writes=["ones_bf"])
        K.op("dve", lambda e: e.memset(ones_f[:], 1.0), writes=["ones_f"])
        K.dma("sp", lambda e: e.dma_start(out=ident[:], in_=c_ident[:, :]), writes=["ident"], sem=("c", 0))
        g_raw = ar.alloc("g_raw", [128, 128], F32)
        K.dma("sp", lambda e: e.dma_start(out=g_raw[0:64, :], in_=ffn_norm.rearrange("l j (c p) -> (l j c) p", p=128)),
              writes=["g_raw0"], sem=("c", 1))
        K.dma("sp", lambda e: e.dma_start(out=g_raw[64:96, :], in_=mix_norm.rearrange("l (c p) -> (l c) p", p=128)),
              writes=["g_raw1"], sem=("c", 2))
        K.op("pe", lambda e: e.transpose(out=ps[7][:, 0:96], in_=g_raw[0:96, :], identity=ident[0:96, 0:96]),
             reads=["g_raw0", "g_raw1", "ident"], writes=[("ps", 7)])
        K.op("dve", lambda e: e.tensor_scalar(out=g_ffn[:].rearrange("p a c -> p (a c)"), in0=ps[7][:, 0:64], scalar1=32.0, scalar2=None, op0=ALU.mult),
             reads=[("ps", 7)], writes=["g_ffn"])
        K.op("dve", lambda e: e.tensor_scalar(out=g_mix[:].rearrange("p a c -> p (a c)"), in0=ps[7][:, 64:96], scalar1=32.0, scalar2=None, op0=ALU.mult),
             reads=[("ps", 7)], writes=["g_mix"])
        ar_base = ar.mark()

        def cast_group(grp, items):
            n = len(items)
            for idx, (dst, src) in enumerate(items):
                K.dma("pool", (lambda e, dst=dst, src=src: e.dma_start(out=dst, in_=src)),
                      writes=[("scrp", idx) + tuple(grp)], sem=("cast",) + tuple(grp), token_val=16 * n)
            K.res[("scr",) + tuple(grp)] = [(("dma", "cast") + tuple(grp), 16 * n), {}]

        def cast_ffn(l, j):
            items = []
            for m in range(NFF):
                items.append((s_gu[(l, j)][m, :, 0], ffn_w_gate[l, j, :, m * 128:(m + 1) * 128].rearrange("(k p) n -> p k n", p=128)))
                items.append((s_gu[(l, j)][m, :, 1], ffn_w_up[l, j, :, m * 128:(m + 1) * 128].rearrange("(k p) n -> p k n", p=128)))
            for mo in range(8):
                items.append((s_d[(l, j)][mo], ffn_w_down[l, j, :, mo * 128:(mo + 1) * 128].rearrange("(k p) n -> p k n", p=128)))
            cast_group(("ffn", l, j), items)

        def cast_odd(i):
            items = []
            for fc in range(8):
                items.append((s_owu[i][fc], odd_w_in[i, :, fc * 128:(fc + 1) * 128].rearrange("(k p) n -> p k n", p=128)))
            items.append((s_owv[i][:, :, :], odd_w_in[i, :, 1024:2048].rearrange("(k p) n -> p k n", p=128)))
            for mo in range(8):
                items.append((s_owo[i][mo], odd_w_out[i, :, mo * 128:(mo + 1) * 128].rearrange("(k p) n -> p k n", p=128)))
            cast_group(("odd", i), items)

        def cast_even(i):
            items = []
            cols = [0, 128, 256, 416, 544, 672, 800]
            for t, c0 in enumerate(cols):
                items.append((s_ewin[i][t], even_w_in[i, :, c0:c0 + 128].rearrange("(k p) n -> p k n", p=128)))
            items.append((s_ekpe[i][:, :, :], even_w_in[i, :, 384:416].rearrange("(k p) n -> p k n", p=128)))
            for h in range(8):
                items.append((s_euq[i][h], w_uq[i, :, h * 96:(h + 1) * 96].rearrange("(k p) n -> p k n", p=128)))
            items.append((s_eukv[i][:, :], w_ukv[i, :, :]))
            items.append((s_epw[i][:, :, :], pool_w[i].rearrange("g c d -> c g d")))
            items.append((s_ewoa[i][:, :], even_w_out[i, 0:512, :]))
            for mo in range(8):
                items.append((s_ewop[i][mo], even_w_out[i, 512:1024, mo * 128:(mo + 1) * 128].rearrange("(g p) n -> p g n", p=128)))
            cast_group(("even", i), items)

        done_cast = set()
        for (l, p) in plan:
            if p in ("ffn1", "ffn2"):
                key = ("ffn", l, 0 if p == "ffn1" else 1)
                if key not in done_cast:
                    cast_ffn(l, key[2])
            elif l % 2 == 0:
                key = ("even", l // 2)
                if key not in done_cast:
                    cast_even(l // 2)
            else:
                key = ("odd", l // 2)
                if key not in done_cast:
                    cast_odd(l // 2)
            done_cast.add(key)
        scr_tokens = {k: v for k, v in K.res.items() if k[0] == "scr"}

        def barrier():
            K.barrier_all()
            K.res.update({k: [v[0], {}] for k, v in scr_tokens.items()})
            ar.reset(ar_base)

        def hkey(c, b):
            return ("h", c, b)

        def emit_norm_act(b, sq, nchunk=8, c0=0):
            K.op("act", lambda e: e.activation(out=sq[:, 0:nchunk, :], in_=h_res[:, c0:c0 + nchunk, b * 512:(b + 1) * 512], func=AF.Square),
                 reads=[hkey(c, b) for c in range(c0, c0 + nchunk)], writes=["sq"])

        def emit_norm_rest(b, sq, rstd, hn, hn_key, gain, pstat):
            def mm(e):
                for c in range(8):
                    ins = e.matmul(ps[pstat][:, :], lhsT=ones_bf[:, :], rhs=sq[:, c, :], start=(c == 0), stop=(c == 7))
                return ins
            K.op("pe", mm, reads=["sq", "ones_bf"], writes=[("ps", pstat)])
            K.op("act", lambda e: e.activation(out=rstd[:, :], in_=ps[pstat][:, :], func=AF.Sqrt, bias=float(D * EPS)),
                 reads=[("ps", pstat)], writes=["rstd"])
            K.op("dve", lambda e: e.reciprocal(out=rstd[:, :], in_=rstd[:, :]), reads=["rstd"], writes=["rstd"])
            for c in range(8):
                K.op("dve", lambda e, c=c: e.scalar_tensor_tensor(out=hn[:, c, :], in0=h_res[:, c, b * 512:(b + 1) * 512],
                                                                  scalar=gain[:, c:c + 1], in1=rstd[:, :],
                                                                  op0=ALU.mult, op1=ALU.mult),
                     reads=[hkey(c, b), "rstd", "g_ffn", "g_mix"], writes=[hn_key])

        def load_seq(s):
            barrier()
            xs = [ar.alloc("xs", [128, D], F32) for _ in range(2)]
            for tc in range(NCH):
                sl = tc % 2
                K.dma("sp", lambda e, tc=tc, sl=sl: e.dma_start(out=xs[sl][:], in_=x[s, tc * 128:(tc + 1) * 128, :]),
                      writes=[("xs", sl)], sem=("xs", sl))
                for half in range(2):
                    bank = (tc * 2 + half) % 4

                    def tr(e, tc=tc, sl=sl, half=half, bank=bank):
                        for q in range(4):
                            c = half * 4 + q
                            ins = e.transpose(out=ps[bank][:, q * 128:(q + 1) * 128], in_=xs[sl][:, c * 128:(c + 1) * 128], identity=ident[:, :])
                        return ins
                    K.op("pe", tr, reads=[("xs", sl), "ident"], writes=[("ps", bank)])
                    b = tc // 4
                    eng = "act" if half == 0 else "dve"
                    dst = h_res[:, half * 4:half * 4 + 4, tc * 128:(tc + 1) * 128]
                    src = ps[bank][:, :].rearrange("p (q n) -> p q n", q=4)
                    if eng == "act":
                        K.op("act", lambda e, dst=dst, src=src: e.activation(out=dst, in_=src, func=AF.Copy),
                             reads=[("ps", bank)], writes=[hkey(c, b) for c in range(half * 4, half * 4 + 4)])
                    else:
                        K.op("dve", lambda e, dst=dst, src=src: e.tensor_copy(out=dst, in_=src),
                             reads=[("ps", bank)], writes=[hkey(c, b) for c in range(half * 4, half * 4 + 4)])

        def store_seq(s):
            barrier()
            xs = [ar.alloc("xs", [128, D], F32) for _ in range(2)]
            for tc in range(NCH):
                sl = tc % 2
                b = tc // 4
                for half in range(2):
                    bank = (tc * 2 + half) % 4

                    def tr(e, tc=tc, half=half, bank=bank):
                        for q in range(4):
                            c = half * 4 + q
                            ins = e.transpose(out=ps[bank][:, q * 128:(q + 1) * 128], in_=h_res[:, c, tc * 128:(tc + 1) * 128], identity=ident[:, :])
                        return ins
                    K.op("pe", tr, reads=[hkey(c, b) for c in range(half * 4, half * 4 + 4)] + ["ident"], writes=[("ps", bank)])
                    dst = xs[sl][:, half * 512:(half + 1) * 512]
                    if half == 0:
                        K.op("act", lambda e, dst=dst, bank=bank: e.activation(out=dst, in_=ps[bank][:, :], func=AF.Copy),
                             reads=[("ps", bank)], writes=[("xs", sl, half)])
                    else:
                        K.op("dve", lambda e, dst=dst, bank=bank: e.tensor_copy(out=dst, in_=ps[bank][:, :]),
                             reads=[("ps", bank)], writes=[("xs", sl, half)])
                K.dma("sp", lambda e, tc=tc, sl=sl: e.dma_start(out=y[s, tc * 128:(tc + 1) * 128, :], in_=xs[sl][:]),
                      reads=[("xs", sl, 0), ("xs", sl, 1)], writes=[("yout", tc)], sem=("ys", sl))

        def ffn_phase(s, l, j):
            barrier()
            hn = [ar.alloc("hn", [128, 8, 512], BF16) for _ in range(2)]
            sq = ar.alloc("sq", [128, 8, 512], BF16)
            rstd = ar.alloc("rstd", [128, 512], F32)
            sg = [ar.alloc("sg", [128, 512], F32) for _ in range(2)]
            act = ar.alloc("act", [128, NFF, 512], BF16)
            wgu = [ar.alloc("wgu", [128, 2, 8, 128], BF16) for _ in range(3)]
            wd = [ar.alloc("wd", [128, NFF, 128], BF16) for _ in range(2)]
            gain = g_ffn[:, l * 2 + j, :]
            scr = ("scr", "ffn", l, j)
            gu_units = [(b, m) for b in range(NB) for m in range(NFF)]
            d_units = [(b, mo) for b in range(NB) for mo in range(8)]
            st_gu = Stream(K, "wgu", wgu, gu_units,
                           lambda u, t: (lambda e: e.dma_start(out=t[:], in_=s_gu[(l, j)][u[1]])),
                           lambda u: [scr])
            st_d = Stream(K, "wd", wd, d_units,
                          lambda u, t: (lambda e: e.dma_start(out=t[:], in_=s_d[(l, j)][u[1]])),
                          lambda u: [scr])
            PG, PU, PD, PST = (0, 1), (2, 3), (4, 5), 6

            emit_norm_act(0, sq)
            emit_norm_rest(0, sq, rstd, hn[0], ("hn", 0), gain, PST)
            for b in range(NB):
                hs = b % 2
                for m in range(NFF):
                    wt, wkey = st_gu.get(b * NFF + m)
                    gb, ub = PG[m % 2], PU[m % 2]

                    def mmg(e, wt=wt, gb=gb, hs=hs):
                        for k in range(8):
                            ins = e.matmul(ps[gb][:, :], lhsT=wt[:, 0, k, :], rhs=hn[hs][:, k, :], start=(k == 0), stop=(k == 7))
                        return ins

                    def mmu(e, wt=wt, ub=ub, hs=hs):
                        for k in range(8):
                            ins = e.matmul(ps[ub][:, :], lhsT=wt[:, 1, k, :], rhs=hn[hs][:, k, :], start=(k == 0), stop=(k == 7))
                        return ins
                    K.op("pe", mmg, reads=[wkey, ("hn", hs)], writes=[("ps", gb)])
                    K.op("pe", mmu, reads=[wkey, ("hn", hs)], writes=[("ps", ub)])
                    K.op("act", lambda e, gb=gb, m=m: e.activation(out=sg[m % 2][:, :], in_=ps[gb][:, :], func=AF.Silu),
                         reads=[("ps", gb)], writes=[("sg", m % 2)])
                    K.op("dve", lambda e, ub=ub, m=m: e.tensor_tensor(out=act[:, m, :], in0=ps[ub][:, :], in1=sg[m % 2][:, :], op=ALU.mult),
                         reads=[("ps", ub), ("sg", m % 2)], writes=[("act", m)])
                    if b + 1 < NB and m == 4:
                        emit_norm_act(b + 1, sq)
                    if b + 1 < NB and m == 12:
                        emit_norm_rest(b + 1, sq, rstd, hn[1 - hs], ("hn", 1 - hs), gain, PST)
                for mo in range(8):
                    wt, wkey = st_d.get(b * 8 + mo)
                    db = PD[mo % 2]

                    def mmd(e, wt=wt, db=db):
                        for k in range(NFF):
                            ins = e.matmul(ps[db][:, :], lhsT=wt[:, k, :], rhs=act[:, k, :], start=(k == 0), stop=(k == NFF - 1))
                        return ins
                    K.op("pe", mmd, reads=[wkey] + [("act", k) for k in range(NFF)], writes=[("ps", db)])
                    hv = h_res[:, mo, b * 512:(b + 1) * 512]
                    K.op("dve", lambda e, db=db, hv=hv: e.scalar_tensor_tensor(out=hv, in0=ps[db][:, :], scalar=0.5, in1=hv,
                                                                               op0=ALU.mult, op1=ALU.add),
                         reads=[("ps", db), hkey(mo, b)], writes=[hkey(mo, b)])

        def odd_phase(s, l):
            i = l // 2
            barrier()
            scr = ("scr", "odd", i)
            wv = ar.alloc("wv", [128, 8, 1024], BF16)
            hn = ar.alloc("hn", [128, 8, 512], BF16)
            sqg = ar.alloc("sqg", [128, 8, 512], BF16)
            rstd = ar.alloc("rstd", [128, 512], F32)
            v32 = ar.alloc("v32", [128, 1024], F32)
            ss = ar.alloc("ss", [128, 8], F32)
            vn = ar.alloc("vn", [128, 4, 1024], BF16)
            uf = [ar.alloc("uf", [128, 512], F32) for _ in range(2)]
            wu = [ar.alloc("wu", [128, 8, 128], BF16) for _ in range(3)]
            wo = [ar.alloc("wo", [128, 8, 128], BF16) for _ in range(3)]
            sgwT = ar.alloc("sgwT", [128, 4, 128], BF16)
            sgw_raw = ar.alloc("sgw_raw", [128, 4, 128], F32)
            sgb = ar.alloc("sgb", [128, 512], F32)
            gsg = ar.alloc("gsg", [128, 1024], F32)
            triu = ar.alloc("triu", [128, 128], F32)
            K.dma("sp", lambda e: e.dma_start(out=wv[:], in_=s_owv[i][:, :, :]), reads=[scr], writes=["wv"], sem=("o", 0))
            K.dma("sp", lambda e: e.dma_start(out=sgw_raw[:], in_=sg_w[i].rearrange("g t s -> t g s")), writes=["sgw_raw"], sem=("o", 1))
            K.dma("sp", lambda e: e.dma_start(out=triu[:], in_=c_triu[:, :]), writes=["triu"], sem=("o", 2))
            K.dma("sp", lambda e: e.dma_start(out=sgb[0:1, :], in_=sg_b[i:i + 1].rearrange("o g t -> o (g t)")), writes=["sgb"], sem=("o", 3))
            K.dma("sp", lambda e: e.dma_start(out=gsg[:], in_=sg_norm[i:i + 1, :].broadcast_to([128, 1024])), writes=["gsg"], sem=("o", 4))
            K.op("dve", lambda e: e.tensor_scalar(out=gsg[:], in0=gsg[:], scalar1=32.0, scalar2=None, op0=ALU.mult), reads=["gsg"], writes=["gsg"])
            for g in range(4):
                K.op("pe", lambda e, g=g: e.transpose(out=ps[g % 2][:, 0:128], in_=sgw_raw[:, g, :], identity=ident[:, :]),
                     reads=["sgw_raw", "ident"], writes=[("ps", g % 2)])
                K.op("dve", lambda e, g=g: e.tensor_tensor(out=sgwT[:, g, :], in0=ps[g % 2][:, 0:128], in1=triu[:, :], op=ALU.mult),
                     reads=[("ps", g % 2), "triu"], writes=["sgwT"])
            u_units = [(b, fc) for b in range(NB) for fc in range(8)]
            st_u = Stream(K, "wu", wu, u_units, lambda u, t: (lambda e: e.dma_start(out=t[:], in_=s_owu[i][u[1]])), lambda u: [scr])
            st_o = Stream(K, "wo", wo, u_units, lambda u, t: (lambda e: e.dma_start(out=t[:], in_=s_owo[i][u[1]])), lambda u: [scr])
            gain = g_mix[:, l, :]
            for b in range(NB):
                emit_norm_act(b, sqg)
                emit_norm_rest(b, sqg, rstd, hn, "hn", gain, 6)
                for ch in range(4):
                    for half in range(2):
                        bank = half

                        def mmv(e, ch=ch, half=half, bank=bank):
                            for k in range(8):
                                ins = e.matmul(ps[bank][:, :], lhsT=hn[:, k, ch * 128:(ch + 1) * 128], rhs=wv[:, k, half * 512:(half + 1) * 512],
                                               start=(k == 0), stop=(k == 7))
                            return ins
                        K.op("pe", mmv, reads=["hn", "wv"], writes=[("ps", bank)])
                        K.op("act", lambda e, half=half, bank=bank: e.activation(out=v32[:, half * 512:(half + 1) * 512], in_=ps[bank][:, :], func=AF.Gelu_apprx_tanh),
                             reads=[("ps", bank)], writes=[("v32", half)])
                    K.op("act", lambda e, ch=ch: e.activation(out=vn[:, ch, :], in_=v32[:, :], func=AF.Square, accum_out=ss[:, ch:ch + 1]),
                         reads=[("v32", 0), ("v32", 1)], writes=[("vn", ch), ("ss", ch)])
                    K.op("act", lambda e, ch=ch: e.activation(out=ss[:, ch:ch + 1], in_=ss[:, ch:ch + 1], func=AF.Sqrt, bias=float(D * EPS)),
                         reads=[("ss", ch)], writes=[("ss", ch)])
                    K.op("dve", lambda e, ch=ch: e.reciprocal(out=ss[:, ch:ch + 1], in_=ss[:, ch:ch + 1]), reads=[("ss", ch)], writes=[("ss", ch)])
                    K.op("dve", lambda e, ch=ch: e.scalar_tensor_tensor(out=vn[:, ch, :], in0=v32[:, :], scalar=ss[:, ch:ch + 1], in1=gsg[:, :],
                                                                        op0=ALU.mult, op1=ALU.mult),
                         reads=[("v32", 0), ("v32", 1), ("ss", ch), "gsg"], writes=[("vn", ch)])
                for fc in range(8):
                    g = fc // 2
                    wt, wkey = st_u.get(b * 8 + fc)
                    ub = 2 + fc % 2
                    mb = 4 + fc % 2

                    def mmu(e, wt=wt, ub=ub):
                        for k in range(8):
                            ins = e.matmul(ps[ub][:, :], lhsT=wt[:, k, :], rhs=hn[:, k, :], start=(k == 0), stop=(k == 7))
                        return ins
                    K.op("pe", mmu, reads=[wkey, "hn"], writes=[("ps", ub)])
                    K.op("act", lambda e, ub=ub, fc=fc: e.activation(out=uf[fc % 2][:, :], in_=ps[ub][:, :], func=AF.Gelu_apprx_tanh),
                         reads=[("ps", ub)], writes=[("uf", fc % 2)])

                    def mmx(e, fc=fc, g=g, mb=mb):
                        for ch in range(4):
                            e.matmul(ps[mb][:, ch * 128:(ch + 1) * 128], lhsT=vn[:, ch, fc * 128:(fc + 1) * 128], rhs=sgwT[:, g, :], start=True, stop=False)
                            ins = e.matmul(ps[mb][:, ch * 128:(ch + 1) * 128], lhsT=ones_f[0:1, :], rhs=sgb[0:1, g * 128:(g + 1) * 128], start=False, stop=True)
                        return ins
                    K.op("pe", mmx, reads=[("vn", c) for c in range(4)] + ["sgwT", "sgb", "ones_f"], writes=[("ps", mb)])
                    K.op("dve", lambda e, fc=fc, mb=mb: e.tensor_tensor(out=sqg[:, fc, :], in0=ps[mb][:, :], in1=uf[fc % 2][:, :], op=ALU.mult),
                         reads=[("ps", mb), ("uf", fc % 2)], writes=["sq"])
                for mo in range(8):
                    wt, wkey = st_o.get(b * 8 + mo)
                    db = 6 + mo % 2

                    def mmo(e, wt=wt, db=db):
                        for k in range(8):
                            ins = e.matmul(ps[db][:, :], lhsT=wt[:, k, :], rhs=sqg[:, k, :], start=(k == 0), stop=(k == 7))
                        return ins
                    K.op("pe", mmo, reads=[wkey, "sq"], writes=[("ps", db)])
                    hv = h_res[:, mo, b * 512:(b + 1) * 512]
                    K.op("dve", lambda e, db=db, hv=hv: e.tensor_tensor(out=hv, in0=ps[db][:, :], in1=hv, op=ALU.add),
                         reads=[("ps", db), hkey(mo, b)], writes=[hkey(mo, b)])

        s_kpe = dscr("s_kpe", [32, S], F32)
        TWO_PI = 2.0 * np.pi
        C1 = 6.28125
        C2 = TWO_PI - C1
        SC = 1.0 - 1e-6

        def tables_phase(s):
            barrier()
            invf = ar.alloc("invf", [128, 1], F32)
            pi_ = ar.alloc("pi_", [128, 512], I32)
            pf = ar.alloc("pf", [128, 512], F32)
            av = ar.alloc("av", [128, 512], F32)
            tf = ar.alloc("tf", [128, 512], F32)
            ki = ar.alloc("ki", [128, 512], I32)
            kf = ar.alloc("kf", [128, 512], F32)
            rr = ar.alloc("rr", [128, 512], F32)
            outt = [ar.alloc("outt", [128, 512], F32) for _ in range(2)]
            R = slice(64, 96)
            K.dma("sp", lambda e: e.dma_start(out=invf[:], in_=c_invfreq[:, :]), writes=["invf"], sem=("t", 0))
            for b in range(NB):
                bl = slice(b * 512, (b + 1) * 512)
                K.dma("sp", lambda e, bl=bl: e.dma_start(out=pi_[R, :], in_=positions[s:s + 1, bl].broadcast_to([32, 512])), writes=["pi"], sem=("t", 1))
                K.op("dve", lambda e: e.tensor_copy(out=pf[R, :], in_=pi_[R, :]), reads=["pi"], writes=["pf"])
                K.op("dve", lambda e: e.tensor_scalar(out=av[R, :], in0=pf[R, :], scalar1=invf[R, 0:1], scalar2=None, op0=ALU.mult), reads=["pf", "invf"], writes=["av"])
                for which in range(2):
                    off = 0.25 if which == 0 else 0.0
                    K.op("dve", lambda e, off=off: e.tensor_scalar(out=ki[R, :], in0=av[R, :], scalar1=float(1.0 / TWO_PI), scalar2=float(off), op0=ALU.mult, op1=ALU.add),
                         reads=["av"], writes=["ki"])
                    K.op("dve", lambda e: e.tensor_copy(out=kf[R, :], in_=ki[R, :]), reads=["ki"], writes=["kf"])
                    K.op("dve", lambda e: e.scalar_tensor_tensor(out=rr[R, :], in0=kf[R, :], scalar=float(-C1), in1=av[R, :], op0=ALU.mult, op1=ALU.add),
                         reads=["kf", "av"], writes=["rr"])
                    K.op("dve", lambda e: e.scalar_tensor_tensor(out=rr[R, :], in0=kf[R, :], scalar=float(-C2), in1=rr[R, :], op0=ALU.mult, op1=ALU.add),
                         reads=["kf", "rr"], writes=["rr"])
                    bias = float(np.pi / 2 * SC) if which == 0 else 0.0
                    K.op("act", lambda e, which=which, bias=bias: e.activation(out=outt[which][R, :], in_=rr[R, :], func=AF.Sin, scale=float(SC), bias=bias),
                         reads=["rr"], writes=[("outt", which)])
                    K.dma("sp", lambda e, which=which, bl=bl: e.dma_start(out=s_cs[s, which, :, bl], in_=outt[which][R, :]),
                          reads=[("outt", which)], writes=[("cs", b, which)], sem=("t", 2 + which))

        def even_phase(s, l):
            i = l // 2
            barrier()
            scr = ("scr", "even", i)
            R = slice(64, 96)
            cqn = ar.alloc("cqn", [128, 2, S], BF16)
            ckvn = ar.alloc("ckvn", [128, S], BF16)
            gq = ar.alloc("gq", [128, 2], F32)
            gkv = ar.alloc("gkv", [128, 1], F32)
            qg = ar.alloc("qg", [128, 1], F32)
            kg = ar.alloc("kg", [128, 1], F32)
            pscale = ar.alloc("pscale", [128, 4], F32)
            K.dma("sp", lambda e: e.dma_start(out=gq[:], in_=q_a_norm[i:i + 1, :].rearrange("o (c p) -> p (o c)", p=128)), writes=["gq"], sem=("e", 0))
            K.dma("sp", lambda e: e.dma_start(out=gkv[:], in_=kv_a_norm[i:i + 1, :].rearrange("o p -> p o")), writes=["gkv"], sem=("e", 1))
            K.dma("sp", lambda e: e.dma_start(out=qg[0:96, :], in_=q_norm[i:i + 1, :].rearrange("o p -> p o")), writes=["qg"], sem=("e", 2))
            K.dma("sp", lambda e: e.dma_start(out=kg[0:96, :], in_=k_norm[i:i + 1, :].rearrange("o p -> p o")), writes=["kg"], sem=("e", 3))
            K.dma("sp", lambda e: e.dma_start(out=pscale[:], in_=pool_scale[i:i + 1, :].rearrange("o (g p) -> p (o g)", p=128)), writes=["pscale"], sem=("e", 4))
            K.op("dve", lambda e: e.tensor_scalar(out=gq[:], in0=gq[:], scalar1=16.0, scalar2=None, op0=ALU.mult), reads=["gq"], writes=["gq"])
            K.op("dve", lambda e: e.tensor_scalar(out=gkv[:], in0=gkv[:], scalar1=float(np.sqrt(128.0)), scalar2=None, op0=ALU.mult), reads=["gkv"], writes=["gkv"])
            K.op("dve", lambda e: e.tensor_scalar(out=kg[0:96, :], in0=kg[0:96, :], scalar1=float(np.sqrt(96.0)), scalar2=None, op0=ALU.mult), reads=["kg"], writes=["kg"])
            mA = ar.mark()
            hn = ar.alloc("hn", [128, 8, 512], BF16)
            sq = ar.alloc("sq", [128, 8, 512], BF16)
            rstd = ar.alloc("rstd", [128, 512], F32)
            rstc = ar.alloc("rstc", [128, 512], F32)
            win = [ar.alloc("win", [128, 8, 128], BF16) for _ in range(3)]
            wkpe = ar.alloc("wkpe", [128, 8, 32], BF16)
            pbuf = ar.alloc("pbuf", [128, 528], F32)
            t1 = ar.alloc("t1", [128, 528], F32)
            t2 = ar.alloc("t2", [128, 528], F32)
            halo = ar.alloc("halo", [128, 4, 16], F32)
            pooled = [ar.alloc("pooled", [128, 512], BF16) for _ in range(2)]
            pfix = ar.alloc("pfix", [128, 16], F32)
            pout = ar.alloc("pout", [128, 4, 512], BF16)
            pw = ar.alloc("pw", [128, 4, 128], BF16)
            invc = ar.alloc("invc", [128, 64], F32)
            wop = [ar.alloc("wop", [128, 4, 128], BF16) for _ in range(3)]
            kst = ar.alloc("kst", [128, 512], F32)
            K.dma("sp", lambda e: e.dma_start(out=wkpe[:], in_=s_ekpe[i][:, :, :]), reads=[scr], writes=["wkpe"], sem=("e", 5))
            K.dma("sp", lambda e: e.dma_start(out=pw[:], in_=s_epw[i][:, :, :]), reads=[scr], writes=["pw"], sem=("e", 6))
            K.dma("sp", lambda e: e.dma_start(out=invc[:], in_=c_invcnt[:, :]), writes=["invc"], sem=("e", 7))
            K.op("dve", lambda e: e.memset(halo[:], 0.0), writes=["halo"])
            in_units = [(b, t) for b in range(NB) for t in range(7)]
            st_in = Stream(K, "win", win, in_units, lambda u, t: (lambda e: e.dma_start(out=t[:], in_=s_ewin[i][u[1]])), lambda u: [scr])
            op_units = [(b, mo) for b in range(NB) for mo in range(8)]
            st_op = Stream(K, "wop", wop, op_units, lambda u, t: (lambda e: e.dma_start(out=t[:], in_=s_ewop[i][u[1]])), lambda u: [scr])
            gain = g_mix[:, l, :]
            WIN = (2, 4, 8, 16)
            for b in range(NB):
                bl = slice(b * 512, (b + 1) * 512)
                emit_norm_act(b, sq)
                emit_norm_rest(b, sq, rstd, hn, "hn", gain, 6)
                banks = [0, 1, 2, 4, 5, 4, 5]
                for t in range(7):
                    wt, wkey = st_in.get(b * 7 + t)
                    bk = banks[t]

                    def mmi(e, wt=wt, bk=bk):
                        for k in range(8):
                            ins = e.matmul(ps[bk][:, :], lhsT=wt[:, k, :], rhs=hn[:, k, :], start=(k == 0), stop=(k == 7))
                        return ins
                    K.op("pe", mmi, reads=[wkey, "hn"], writes=[("ps", bk)])
                    if t == 1:
                        for c in range(2):
                            K.op("act", lambda e, c=c: e.activation(out=sq[:, c, :], in_=ps[c][:, :], func=AF.Square), reads=[("ps", c)], writes=["sq"])

                        def mms(e):
                            e.matmul(ps[6][:, :], lhsT=ones_bf[:, :], rhs=sq[:, 0, :], start=True, stop=False)
                            return e.matmul(ps[6][:, :], lhsT=ones_bf[:, :], rhs=sq[:, 1, :], start=False, stop=True)
                        K.op("pe", mms, reads=["sq", "ones_bf"], writes=[("ps", 6)])
                        K.op("act", lambda e: e.activation(out=rstc[:, :], in_=ps[6][:, :], func=AF.Sqrt, bias=float(256 * EPS)), reads=[("ps", 6)], writes=["rstc"])
                        K.op("dve", lambda e: e.reciprocal(out=rstc[:, :], in_=rstc[:, :]), reads=["rstc"], writes=["rstc"])
                        for c in range(2):
                            K.op("dve", lambda e, c=c, bl=bl: e.scalar_tensor_tensor(out=cqn[:, c, bl], in0=ps[c][:, :], scalar=gq[:, c:c + 1], in1=rstc[:, :],
                                                                                     op0=ALU.mult, op1=ALU.mult),
                                 reads=[("ps", c), "gq", "rstc"], writes=[("cqn", b)])
                    if t == 2:
                        K.op("act", lambda e: e.activation(out=sq[:, 2, :], in_=ps[2][:, :], func=AF.Square), reads=[("ps", 2)], writes=["sq"])
                        K.op("pe", lambda e: e.matmul(ps[6][:, :], lhsT=ones_bf[:, :], rhs=sq[:, 2, :], start=True, stop=True), reads=["sq", "ones_bf"], writes=[("ps", 6)])
                        K.op("act", lambda e: e.activation(out=rstc[:, :], in_=ps[6][:, :], func=AF.Sqrt, bias=float(128 * EPS)), reads=[("ps", 6)], writes=["rstc"])
                        K.op("dve", lambda e: e.reciprocal(out=rstc[:, :], in_=rstc[:, :]), reads=["rstc"], writes=["rstc"])
                        K.op("dve", lambda e, bl=bl: e.scalar_tensor_tensor(out=ckvn[:, bl], in0=ps[2][:, :], scalar=gkv[:, 0:1], in1=rstc[:, :], op0=ALU.mult, op1=ALU.mult),
                             reads=[("ps", 2), "gkv", "rstc"], writes=[("ckvn", b)])

                        def mmk(e):
                            for k in range(8):
                                ins = e.matmul(ps[3][R, :], lhsT=wkpe[:, k, :], rhs=hn[:, k, :], start=(k == 0), stop=(k == 7))
                            return ins
                        K.op("pe", mmk, reads=["wkpe", "hn"], writes=[("ps", 3)])
                        K.op("act", lambda e: e.activation(out=kst[R, :], in_=ps[3][R, :], func=AF.Copy), reads=[("ps", 3)], writes=["kst"])
                        K.dma("sp", lambda e, bl=bl: e.dma_start(out=s_kpe[:, bl], in_=kst[R, :]), reads=["kst"], writes=[("kpe", b)], sem=("e", 8))
                    if t >= 3:
                        g = t - 3
                        w = WIN[g]
                        K.op("act", lambda e, bk=bk: e.activation(out=pbuf[:, 16:528], in_=ps[bk][:, :], func=AF.Copy), reads=[("ps", bk)], writes=["pbuf"])
                        K.op("dve", lambda e, g=g: e.tensor_copy(out=pbuf[:, 0:16], in_=halo[:, g, :]), reads=["halo"], writes=["pbuf"])
                        K.op("dve", lambda e: e.tensor_tensor(out=t1[:, 1:528], in0=pbuf[:, 1:528], in1=pbuf[:, 0:527], op=ALU.add), reads=["pbuf"], writes=["t1"])
                        fin = t1
                        if g >= 1:
                            K.op("dve", lambda e: e.tensor_tensor(out=t2[:, 3:528], in0=t1[:, 3:528], in1=t1[:, 1:526], op=ALU.add), reads=["t1"], writes=["t2"])
                            fin = t2
                        if g >= 2:
                            K.op("dve", lambda e: e.tensor_tensor(out=t1[:, 7:528], in0=t2[:, 7:528], in1=t2[:, 3:524], op=ALU.add), reads=["t2"], writes=["t1"])
                            fin = t1
                        if g >= 3:
                            K.op("dve", lambda e: e.tensor_tensor(out=t2[:, 15:528], in0=t1[:, 15:528], in1=t1[:, 7:520], op=ALU.add), reads=["t1"], writes=["t2"])
                            fin = t2
                        fkey = "t1" if fin is t1 else "t2"
                        pl = pooled[g % 2]
                        K.op("dve", lambda e, fin=fin, pl=pl, w=w: e.scalar_tensor_tensor(out=pl[:, :], in0=fin[:, 16:528], scalar=float(1.0 / w), in1=pbuf[:, 16:528],
                                                                                           op0=ALU.mult, op1=ALU.subtract),
                             reads=[fkey, "pbuf"], writes=[("pooled", g % 2)])
                        if b == 0:
                            K.op("dve", lambda e, fin=fin, g=g: e.tensor_tensor(out=pfix[:, :], in0=fin[:, 16:32], in1=invc[:, g * 16:(g + 1) * 16], op=ALU.mult),
                                 reads=[fkey, "invc"], writes=["pfix"])
                            K.op("dve", lambda e, pl=pl: e.tensor_tensor(out=pl[:, 0:16], in0=pfix[:, :], in1=pbuf[:, 16:32], op=ALU.subtract),
                                 reads=["pfix", "pbuf", ("pooled", g % 2)], writes=[("pooled", g % 2)])
                        K.op("dve", lambda e, g=g: e.tensor_copy(out=halo[:, g, :], in_=pbuf[:, 512:528]), reads=["pbuf"], writes=["halo"])
                        K.op("pe", lambda e, g=g, pl=pl: e.matmul(ps[7][:, :], lhsT=pw[:, g, :], rhs=pl[:, :], start=True, stop=True),
                             reads=["pw", ("pooled", g % 2)], writes=[("ps", 7)])
                        K.op("act", lambda e, g=g: e.activation(out=pout[:, g, :], in_=ps[7][:, :], func=AF.Copy, scale=pscale[:, g:g + 1]),
                             reads=[("ps", 7), "pscale"], writes=[("pout", g)])
                for mo in range(8):
                    wt, wkey = st_op.get(b * 8 + mo)
                    db = mo % 2

                    def mmo(e, wt=wt, db=db):
                        for g in range(4):
                            ins = e.matmul(ps[db][:, :], lhsT=wt[:, g, :], rhs=pout[:, g, :], start=(g == 0), stop=(g == 3))
                        return ins
                    K.op("pe", mmo, reads=[wkey] + [("pout", g) for g in range(4)], writes=[("ps", db)])
                    hv = h_res[:, mo, bl]
                    K.op("dve", lambda e, db=db, hv=hv: e.tensor_tensor(out=hv, in0=ps[db][:, :], in1=hv, op=ALU.add),
                         reads=[("ps", db), hkey(mo, b)], writes=[hkey(mo, b)])

            K.barrier_all()
            K.res.update({k: [v[0], {}] for k, v in scr_tokens.items()})
            ar.reset(mA)
            kh = ar.alloc("kh", [128, S], BF16)
            vx = ar.alloc("vx", [128, NCH, 65], BF16)
            wuq = [ar.alloc("wuq", [128, 2, 96], BF16) for _ in range(2)]
            wukv = ar.alloc("wukv", [128, 1024], BF16)
            woa = [ar.alloc("woa", [128, 1024], BF16) for _ in range(2)]
            cosT = ar.alloc("cosT", [128, 512], F32)
            sinT = ar.alloc("sinT", [128, 512], F32)
            xk = ar.alloc("xk", [128, 512], F32)
            sqk = ar.alloc("sqk", [128, 512], BF16)
            rstb = ar.alloc("rstb", [128, 512], F32)
            xr = ar.alloc("xr", [128, 512], F32)
            tmp = ar.alloc("tmp", [128, 512], F32)
            qh = [ar.alloc("qh", [128, 512], BF16) for _ in range(2)]
            pT = [ar.alloc("pT", [128, 512], BF16) for _ in range(3)]
            rl = ar.alloc("rl", [128, 512], F32)
            bc = ar.alloc("bc", [128, 512], F32)
            ao = ar.alloc("ao", [128, 512], BF16)
            rotm = ar.alloc("rotm", [128, 128], F32)
            trb = ar.alloc("trb", [128, 128], BF16)
            K.dma("sp", lambda e: e.dma_start(out=wukv[:], in_=s_eukv[i][:, :]), reads=[scr], writes=["wukv"], sem=("e", 9))
            K.dma("sp", lambda e: e.dma_start(out=rotm[:], in_=c_rotm[:, :]), writes=["rotm"], sem=("e", 10))
            idb = ar.alloc("idb", [128, 128], BF16)
            K.dma("pool", lambda e: e.dma_start(out=trb[:], in_=c_maskneg[:, :]), writes=["trb"], sem=("e", 11))
            K.dma("pool", lambda e: e.dma_start(out=idb[:], in_=c_ident[:, :]), writes=["idb"], sem=("e", 15))
            K.op("dve", lambda e: e.memset(vx[:, :, 64:65], 1.0), writes=["vx1"])

            def norm_rope(src_nope, src_rope, rope_key, gcol, dst, dst_key, nrows_nope, j):
                bl = slice(j * 512, (j + 1) * 512)
                K.dma("sp", lambda e, bl=bl: e.dma_start(out=cosT[R, :], in_=s_cs[s, 0, :, bl]), reads=[("cs", j, 0)], writes=["cosT"], sem=("e", 12))
                K.dma("sp", lambda e, bl=bl: e.dma_start(out=sinT[R, :], in_=s_cs[s, 1, :, bl]), reads=[("cs", j, 1)], writes=["sinT"], sem=("e", 13))
                if src_rope is None:
                    K.op("act", lambda e: e.activation(out=sqk[0:96, :], in_=src_nope[0:96, :], func=AF.Square), reads=[("ps", 0)], writes=["sqk"])
                    rsrc = src_nope
                else:
                    K.op("act", lambda e: e.activation(out=sqk[0:64, :], in_=src_nope[0:64, :], func=AF.Square), reads=[("ps", 0)], writes=["sqk"])
                    K.op("act", lambda e: e.activation(out=sqk[R, :], in_=src_rope[R, :], func=AF.Square), reads=[rope_key, "sqk"], writes=["sqk"])
                    rsrc = src_rope
                K.op("pe", lambda e: e.matmul(ps[1][0:96, :], lhsT=ones_bf[0:96, 0:96], rhs=sqk[0:96, :], start=True, stop=True), reads=["sqk", "ones_bf"], writes=[("ps", 1)])
                K.op("act", lambda e: e.activation(out=rstb[0:96, :], in_=ps[1][0:96, :], func=AF.Sqrt, bias=float(96 * EPS)), reads=[("ps", 1)], writes=["rstb"])
                K.op("dve", lambda e: e.reciprocal(out=rstb[0:96, :], in_=rstb[0:96, :]), reads=["rstb"], writes=["rstb"])
                K.op("dve", lambda e: e.scalar_tensor_tensor(out=dst[0:64, :], in0=src_nope[0:64, :], scalar=gcol[0:64, 0:1], in1=rstb[0:64, :], op0=ALU.mult, op1=ALU.mult),
                     reads=[("ps", 0), "rstb", "qg", "kg"], writes=[dst_key])
                K.op("dve", lambda e: e.scalar_tensor_tensor(out=xr[R, :], in0=rsrc[R, :], scalar=gcol[R, 0:1], in1=rstb[R, :], op0=ALU.mult, op1=ALU.mult),
                     reads=[("ps", 0), rope_key, "rstb", "qg", "kg"], writes=["xr"])
                K.op("pe", lambda e: e.matmul(ps[2][R, :], lhsT=rotm[R, 0:32], rhs=xr[R, :], start=True, stop=True), reads=["xr", "rotm"], writes=[("ps", 2)])
                K.op("dve", lambda e: e.tensor_tensor(out=tmp[R, :], in0=ps[2][R, :], in1=sinT[R, :], op=ALU.mult), reads=[("ps", 2), "sinT"], writes=["tmp"])
                K.op("dve", lambda e: e.tensor_tensor(out=xr[R, :], in0=xr[R, :], in1=cosT[R, :], op=ALU.mult), reads=["xr", "cosT", ("ps", 2)], writes=["xr"])
                K.op("dve", lambda e: e.tensor_tensor(out=dst[R, :], in0=xr[R, :], in1=tmp[R, :], op=ALU.add), reads=["xr", "tmp"], writes=[dst_key])

            sc_i = 0
            for h in range(8):
                hs = h % 2
                K.dma("sp", lambda e, h=h, hs=hs: e.dma_start(out=wuq[hs][:], in_=s_euq[i][h]), reads=[scr], writes=[("wuq", hs)], sem=("wuq", hs))
                K.dma("sp", lambda e, h=h, hs=hs: e.dma_start(out=woa[hs][0:64, :], in_=s_ewoa[i][h * 64:(h + 1) * 64, :]), reads=[scr], writes=[("woa", hs)], sem=("woa", hs))
                for j in range(NB):
                    bl = slice(j * 512, (j + 1) * 512)
                    K.dma("sp", lambda e, bl=bl: e.dma_start(out=xk[R, :], in_=s_kpe[:, bl]), reads=[("kpe", j)], writes=["xk"], sem=("e", 14))
                    K.op("pe", lambda e, h=h, bl=bl: e.matmul(ps[0][0:64, :], lhsT=wukv[:, h * 128:h * 128 + 64], rhs=ckvn[:, bl], start=True, stop=True),
                         reads=["wukv", ("ckvn", j)], writes=[("ps", 0)])
                    norm_rope(ps[0], xk, "xk", kg, kh[:, bl], ("kh", j), 64, j)
                for c0 in range(0, NCH, 8):
                    nq = min(8, NCH - c0)

                    def mmv(e, h=h, c0=c0, nq=nq):
                        for q in range(nq):
                            ins = e.matmul(ps[3][:, q * 64:(q + 1) * 64], lhsT=ckvn[:, (c0 + q) * 128:(c0 + q + 1) * 128], rhs=wukv[:, h * 128 + 64:(h + 1) * 128],
                                           start=True, stop=True)
                        return ins
                    K.op("pe", mmv, reads=["wukv"] + [("ckvn", (c0 + q) // 4) for q in range(nq)], writes=[("ps", 3)])
                    K.op("act", lambda e, c0=c0, nq=nq: e.activation(out=vx[:, c0:c0 + nq, 0:64], in_=ps[3][:, 0:nq * 64].rearrange("p (q n) -> p q n", q=nq), func=AF.Copy),
                         reads=[("ps", 3)], writes=[("vx", c0 // 8)])
                for j in range(NB):
                    bl = slice(j * 512, (j + 1) * 512)
                    qt = qh[j % 2]

                    def mmq(e, hs=hs, bl=bl):
                        e.matmul(ps[0][0:96, :], lhsT=wuq[hs][:, 0, :], rhs=cqn[:, 0, bl], start=True, stop=False)
                        return e.matmul(ps[0][0:96, :], lhsT=wuq[hs][:, 1, :], rhs=cqn[:, 1, bl], start=False, stop=True)
                    K.op("pe", mmq, reads=[("wuq", hs), ("cqn", j)], writes=[("ps", 0)])
                    norm_rope(ps[0], None, ("ps", 0), qg, qt, ("qh", j % 2), 96, j)
                    nkt = 4 * (j + 1)
                    for kt in range(nkt):
                        d = kt - 4 * j
                        lo = max(0, d) * 128
                        sb = 4 + sc_i % 2
                        pt = pT[sc_i % 3]
                        pkey = ("pT", sc_i % 3)
                        sc_i += 1
                        def mms(e, kt=kt, lo=lo, sb=sb, qt=qt, d=d):
                            ins = e.matmul(ps[sb][:, lo:512], lhsT=kh[0:96, kt * 128:(kt + 1) * 128], rhs=qt[0:96, lo:512], start=True, stop=(d < 0))
                            if d >= 0:
                                ins = e.matmul(ps[sb][:, lo:lo + 128], lhsT=idb[:, :], rhs=trb[:, :], start=False, stop=True)
                            return ins
                        K.op("pe", mms, reads=[("kh", kt // 4), ("qh", j % 2), "trb", "idb"], writes=[("ps", sb)])
                        K.op("act", lambda e, lo=lo, sb=sb, pt=pt: e.activation(out=pt[:, lo:512], in_=ps[sb][:, lo:512], func=AF.Exp), reads=[("ps", sb)], writes=[pkey])
                        K.op("pe", lambda e, kt=kt, lo=lo, pt=pt, nkt=nkt: e.matmul(ps[6][0:65, lo:512], lhsT=vx[:, kt, 0:65], rhs=pt[:, lo:512], start=(kt == 0), stop=(kt == nkt - 1)),
                             reads=[pkey, ("vx", kt // 8), "vx1"], writes=[("ps", 6)])
                    K.op("dve", lambda e: e.reciprocal(out=rl[64:65, :], in_=ps[6][64:65, :]), reads=[("ps", 6)], writes=["rl"])
                    K.op("pe", lambda e: e.matmul(ps[2][0:64, :], lhsT=ones_f[64:65, 0:64], rhs=rl[64:65, :], start=True, stop=True), reads=["rl", "ones_f"], writes=[("ps", 2)])
                    K.op("act", lambda e: e.activation(out=bc[0:64, :], in_=ps[2][0:64, :], func=AF.Copy), reads=[("ps", 2)], writes=["bc"])
                    K.op("dve", lambda e: e.tensor_tensor(out=ao[0:64, :], in0=ps[6][0:64, :], in1=bc[0:64, :], op=ALU.mult), reads=[("ps", 6), "bc"], writes=["ao"])
                    for mo in range(8):
                        db = 7 if mo % 2 == 0 else 3
                        K.op("pe", lambda e, mo=mo, db=db, hs=hs: e.matmul(ps[db][:, :], lhsT=woa[hs][0:64, mo * 128:(mo + 1) * 128], rhs=ao[0:64, :], start=True, stop=True),
                             reads=[("woa", hs), "ao"], writes=[("ps", db)])
                        hv = h_res[:, mo, bl]
                        K.op("dve", lambda e, db=db, hv=hv: e.tensor_tensor(out=hv, in0=ps[db][:, :], in1=hv, op=ALU.add),
                             reads=[("ps", db), hkey(mo, j)], writes=[hkey(mo, j)])

        PHASES = {"ffn": ffn_phase, "odd": odd_phase, "even": even_phase}

        for s in range(NSEQ):
            load_seq(s)
            if need_even:
                tables_phase(s)
            for (l, p) in plan:
                if p == "ffn1":
                    ffn_phase(s, l, 0)
                elif p == "ffn2":
                    ffn_phase(s, l, 1)
                elif l % 2 == 1:
                    PHASES["odd"](s, l)
                else:
                    PHASES["even"](s, l)
            store_seq(s)
        K.barrier_all()
        with nc.allow_non_contiguous_dma(reason="small constant loads"):
            K.emit()
    return nc


FULL_PLAN = [(l, p) for l in range(DEPTH) for p in ("ffn1", "mix", "ffn2")]


def make_consts():
    ident = np.eye(128, dtype=np.float32)
    triu = np.triu(np.ones((128, 128), np.float32))
    rotm = np.zeros((128, 128), np.float32)
    for i in range(16):
        rotm[64 + i + 16, i] = -1.0
        rotm[64 + i, i + 16] = 1.0
    invf = np.zeros((128, 1), np.float32)
    f = (10000.0 ** (-np.arange(0, 32, 2, dtype=np.float32) / 32)).astype(np.float32)
    invf[64:80, 0] = f
    invf[80:96, 0] = f
    invcnt = np.zeros((128, 64), np.float32)
    for g, w in enumerate((2, 4, 8, 16)):
        for t in range(16):
            invcnt[:, g * 16 + t] = 1.0 / min(t + 1, w)
    maskneg = ((1.0 - triu) * -30000.0).astype(np.float32)
    return {"c_ident": ident, "c_triu": triu, "c_maskneg": maskneg, "c_rotm": rotm, "c_invfreq": invf, "c_invcnt": invcnt}


_CACHE = {}


def kernel(**inputs):
    n = 8
    S = inputs["x"].shape[1]
    B = inputs["x"].shape[0]
    nseq = B // n
    key = (S, nseq)
    if key not in _CACHE:
        _CACHE[key] = build(S, nseq, FULL_PLAN)
    nc = _CACHE[key]
    consts = make_consts()
    in_maps = []
    for c in range(n):
        m = {k: np.ascontiguousarray(v) for k, v in inputs.items() if k not in ("x", "positions")}
        m["x"] = np.ascontiguousarray(inputs["x"][c * nseq:(c + 1) * nseq])
        m["positions"] = np.ascontiguousarray(inputs["positions"][c * nseq:(c + 1) * nseq]).astype(np.int32)
        m.update(consts)
        in_maps.append(m)
    res = run_bass_kernel_spmd(nc, in_maps, core_ids=list(range(n)))
    return np.concatenate([np.asarray(r["y"]) for r in res.results], axis=0).astype(np.float32)
```

```python
import contextlib
import numpy as np
import concourse.bass as bass
import concourse.mybir as mybir
from concourse.bass_utils import run_bass_kernel_spmd

F32 = mybir.dt.float32
BF16 = mybir.dt.bfloat16
I32 = mybir.dt.int32
AF = mybir.ActivationFunctionType
ALU = mybir.AluOpType

D = 1024
DFF = 2816
NFF = 22
DEPTH = 4
EPS = 1e-6
SB_BASE = 16512
SB_END = 229376
HRES_BYTES = 8 * 4096 * 4
ENGS = ("pe", "act", "dve", "pool", "sp")


def _dsize(dt):
    return 2 if dt == BF16 else 4


class Sched:
    def __init__(self, nc, stack):
        self.nc = nc
        self.stack = stack
        self.q = {e: [] for e in ENGS}
        self.tok = {}
        self.sem = {}
        self.seen = {}
        self.res = {}

    def _semh(self, key):
        if key not in self.sem:
            name = "s_" + "_".join(str(x) for x in key)
            self.sem[key] = self.stack.enter_context(self.nc.semaphore(name))
        return self.sem[key]

    def _collect(self, eng, reads, writes):
        need = {}

        def add(t):
            if t is not None:
                k, v = t
                if need.get(k, 0) < v:
                    need[k] = v

        for r in reads:
            e = self.res.get(r)
            if e is not None:
                add(e[0])
        for w in writes:
            e = self.res.get(w)
            if e is not None:
                add(e[0])
                for k, v in e[1].items():
                    add((k, v))
        waits = []
        for k, v in need.items():
            if eng == "pe" and k == ("eng", "pe"):
                continue
            if self.seen.get((eng, k), 0) < v:
                waits.append((k, v))
                self.seen[(eng, k)] = v
        return waits

    def _commit(self, reads, writes, token):
        k, v = token
        for r in reads:
            e = self.res.setdefault(r, [None, {}])
            if e[1].get(k, 0) < v:
                e[1][k] = v
        for w in writes:
            self.res[w] = [token, {}]

    def op(self, eng, fn, reads=(), writes=()):
        waits = self._collect(eng, reads, writes)
        key = ("eng", eng)
        self._semh(key)
        self.tok[key] = self.tok.get(key, 0) + 1
        self._commit(reads, writes, (key, self.tok[key]))
        self.q[eng].append((waits, fn, (key, 1)))

    def dma(self, queue, fn, reads=(), writes=(), sem=None, token_val=None):
        waits = self._collect(queue, reads, writes)
        key = ("dma",) + tuple(sem)
        self._semh(key)
        self.tok[key] = self.tok.get(key, 0) + 16
        tv = self.tok[key] if token_val is None else token_val
        self._commit(reads, writes, (key, tv))
        self.q[queue].append((waits, fn, (key, 16)))

    def barrier_all(self):
        waits = []
        for k, v in self.tok.items():
            if k[0] == "dma" and self.seen.get(("sp", k), 0) < v:
                waits.append((k, v))
                self.seen[("sp", k)] = v
        for k, v in self.tok.items():
            if k[0] == "eng" and k != ("eng", "sp") and self.seen.get(("sp", k), 0) < v:
                waits.append((k, v))
                self.seen[("sp", k)] = v
        key = ("eng", "sp")
        self._semh(key)
        self.tok[key] = self.tok.get(key, 0) + 1
        self.q["sp"].append((waits, (lambda e: e.nop()), (key, 1)))
        for e in ENGS:
            if e == "sp":
                continue
            ws = []
            for k, v in self.tok.items():
                if k[0] == "eng" and k != ("eng", e) and self.seen.get((e, k), 0) < v:
                    ws.append((k, v))
                    self.seen[(e, k)] = v
            for k, v in self.tok.items():
                if k[0] == "dma":
                    self.seen[(e, k)] = max(self.seen.get((e, k), 0), v)
            own = ("eng", e)
            if own in self.tok:
                ws.append((own, self.tok[own]))
                self.seen[(e, own)] = self.tok[own]
            if ws:
                self.q[e].append((ws, None, None))
        self.res = {}

    def emit(self):
        nc = self.nc
        with nc.Block() as block:
            decos = {"pe": block.tensor, "act": block.scalar, "dve": block.vector,
                     "pool": block.gpsimd, "sp": block.sync}
            for eng in ENGS:
                def body(e, eng=eng):
                    for waits, fn, inc in self.q[eng]:
                        if fn is None:
                            for k, v in waits:
                                e.wait_ge(self.sem[k], v)
                            continue
                        for k, v in waits[:-1]:
                            e.wait_ge(self.sem[k], v)
                        att = (self.sem[waits[-1][0]], waits[-1][1]) if waits else None
                        ins = fn(_EngProxy(e, att))
                        ins.then_inc(self.sem[inc[0]], inc[1])
                decos[eng](body)


class _EngProxy:
    def __init__(self, e, att):
        self._e = e
        self._att = att

    def __getattr__(self, name):
        f = getattr(self._e, name)

        def call(*a, **kw):
            ins = f(*a, **kw)
            if self._att is not None:
                ins._wait_ge(self._att[0], self._att[1])
                self._att = None
            return ins
        return call


class Arena:
    def __init__(self, nc, lo, hi):
        self.nc = nc
        self.lo = lo
        self.hi = hi
        self.cur = lo
        self.n = 0

    def alloc(self, name, shape, dt):
        nbytes = int(np.prod(shape[1:])) * _dsize(dt)
        nbytes = (nbytes + 31) // 32 * 32
        assert self.cur + nbytes <= self.hi, (name, self.cur, nbytes, self.hi)
        self.n += 1
        t = self.nc.alloc_sbuf_tensor_at("%s_%d" % (name, self.n), list(shape), dt, offset=self.cur)
        self.cur += nbytes
        return t

    def mark(self):
        return self.cur

    def reset(self, m):
        self.cur = m


class Stream:
    def __init__(self, K, name, slots, units, loader, src_keys):
        self.K = K
        self.name = name
        self.slots = slots
        self.units = units
        self.loader = loader
        self.src_keys = src_keys
        self.issued = 0

    def key(self, i):
        return ("ws", self.name, i % len(self.slots))

    def _issue_upto(self, n):
        while self.issued < min(n, len(self.units)):
            i = self.issued
            tile = self.slots[i % len(self.slots)]
            fn = self.loader(self.units[i], tile)
            self.K.dma("sp", fn, reads=self.src_keys(self.units[i]), writes=[self.key(i)],
                       sem=(self.name, i % len(self.slots)))
            self.issued += 1

    def get(self, i):
        self._issue_upto(i + len(self.slots))
        return self.slots[i % len(self.slots)], self.key(i)


def build(S, NSEQ, plan):
    assert S % 512 == 0
    NB = S // 512
    NCH = S // 128
    nc = bass.Bass("TRN2", target_bir_lowering=False)

    def din(name, shape, dt=F32):
        return nc.dram_tensor(name, list(shape), dt, kind="ExternalInput").ap()

    def dscr(name, shape, dt=BF16):
        return nc.dram_tensor(name, list(shape), dt, kind="Internal").ap()

    x = din("x", [NSEQ, S, D])
    positions = din("positions", [NSEQ, S], I32)
    ffn_norm = din("ffn_norm", [DEPTH, 2, D])
    ffn_w_gate = din("ffn_w_gate", [DEPTH, 2, D, DFF])
    ffn_w_up = din("ffn_w_up", [DEPTH, 2, D, DFF])
    ffn_w_down = din("ffn_w_down", [DEPTH, 2, DFF, D])
    mix_norm = din("mix_norm", [DEPTH, D])
    even_w_in = din("even_w_in", [2, D, 928])
    q_a_norm = din("q_a_norm", [2, 256])
    kv_a_norm = din("kv_a_norm", [2, 128])
    w_uq = din("w_uq", [2, 256, 768])
    w_ukv = din("w_ukv", [2, 128, 1024])
    q_norm = din("q_norm", [2, 96])
    k_norm = din("k_norm", [2, 96])
    pool_w = din("pool_w", [2, 4, 128, 128])
    pool_scale = din("pool_scale", [2, 512])
    even_w_out = din("even_w_out", [2, D, D])
    odd_w_in = din("odd_w_in", [2, D, 2048])
    sg_norm = din("sg_norm", [2, D])
    sg_w = din("sg_w", [2, 4, 128, 128])
    sg_b = din("sg_b", [2, 4, 128])
    odd_w_out = din("odd_w_out", [2, D, D])
    c_ident = din("c_ident", [128, 128])
    c_triu = din("c_triu", [128, 128])
    c_maskneg = din("c_maskneg", [128, 128])
    c_sgn = din("c_sgn", [128, 1])
    c_rotm = din("c_rotm", [128, 128])
    c_invfreq = din("c_invfreq", [128, 1])
    c_invcnt = din("c_invcnt", [128, 64])
    y = nc.dram_tensor("y", [NSEQ, S, D], F32, kind="ExternalOutput").ap()

    need_ffn = sorted({(l, j) for (l, p) in plan for j in ((0,) if p == "ffn1" else (1,) if p == "ffn2" else ())})
    need_even = sorted({l // 2 for (l, p) in plan if p == "mix" and l % 2 == 0})
    need_odd = sorted({l // 2 for (l, p) in plan if p == "mix" and l % 2 == 1})

    s_gu = {lj: dscr("s_gu_%d_%d" % lj, [NFF, 128, 2, 8, 128]) for lj in need_ffn}
    s_d = {lj: dscr("s_d_%d_%d" % lj, [8, 128, NFF, 128]) for lj in need_ffn}
    s_ewin = {i: dscr("s_ewin_%d" % i, [7, 128, 8, 128]) for i in need_even}
    s_ekpe = {i: dscr("s_ekpe_%d" % i, [128, 8, 32]) for i in need_even}
    s_euq = {i: dscr("s_euq_%d" % i, [8, 128, 2, 96]) for i in need_even}
    s_eukv = {i: dscr("s_eukv_%d" % i, [128, 1024]) for i in need_even}
    s_epw = {i: dscr("s_epw_%d" % i, [128, 4, 128]) for i in need_even}
    s_ewoa = {i: dscr("s_ewoa_%d" % i, [512, 1024]) for i in need_even}
    s_ewop = {i: dscr("s_ewop_%d" % i, [8, 128, 4, 128]) for i in need_even}
    s_owu = {i: dscr("s_owu_%d" % i, [8, 128, 8, 128]) for i in need_odd}
    s_owv = {i: dscr("s_owv_%d" % i, [128, 8, 1024]) for i in need_odd}
    s_owo = {i: dscr("s_owo_%d" % i, [8, 128, 8, 128]) for i in need_odd}
    s_cs = dscr("s_cs", [NSEQ, 2, 32, S], F32)

    stack = contextlib.ExitStack()
    with stack:
        K = Sched(nc, stack)
        h_res = nc.alloc_sbuf_tensor_at("h_res", [128, 8, S], F32, offset=SB_BASE)
        ar = Arena(nc, SB_BASE + HRES_BYTES, SB_END)
        ps = [stack.enter_context(nc.psum_tensor("ps%d" % i, [128, 512], F32)) for i in range(8)]

        ones_bf = ar.alloc("ones_bf", [128, 128], BF16)
        ones_f = ar.alloc("ones_f", [128, 128], F32)
        ident = ar.alloc("ident", [128, 128], F32)
        g_ffn = ar.alloc("g_ffn", [128, 8, 8], F32)
        g_mix = ar.alloc("g_mix", [128, 4, 8], F32)
        K.op("dve", lambda e: e.memset(ones_bf[:], 1.0), writes=["ones_bf"])
        K.op("dve", lambda e: e.memset(ones_f[:], 1.0), writes=["ones_f"])
        K.dma("sp", lambda e: e.dma_start(out=ident[:], in_=c_ident[:, :]), writes=["ident"], sem=("c", 0))
        g_raw = ar.alloc("g_raw", [128, 128], F32)
        K.dma("sp", lambda e: e.dma_start(out=g_raw[0:64, :], in_=ffn_norm.rearrange("l j (c p) -> (l j c) p", p=128)),
              writes=["g_raw0"], sem=("c", 1))
        K.dma("sp", lambda e: e.dma_start(out=g_raw[64:96, :], in_=mix_norm.rearrange("l (c p) -> (l c) p", p=128)),
              writes=["g_raw1"], sem=("c", 2))
        K.op("pe", lambda e: e.transpose(out=ps[7][:, 0:96], in_=g_raw[0:96, :], identity=ident[0:96, 0:96]),
             reads=["g_raw0", "g_raw1", "ident"], writes=[("ps", 7)])
        K.op("dve", lambda e: e.tensor_scalar(out=g_ffn[:].rearrange("p a c -> p (a c)"), in0=ps[7][:, 0:64], scalar1=32.0, scalar2=None, op0=ALU.mult),
             reads=[("ps", 7)], writes=["g_ffn"])
        K.op("dve", lambda e: e.tensor_scalar(out=g_mix[:].rearrange("p a c -> p (a c)"), in0=ps[7][:, 64:96], scalar1=32.0, scalar2=None, op0=ALU.mult),
             reads=[("ps", 7)], writes=["g_mix"])
        ar_base = ar.mark()

        def cast_group(grp, items):
            n = len(items)
            for idx, (dst, src) in enumerate(items):
                K.dma("pool", (lambda e, dst=dst, src=src: e.dma_start(out=dst, in_=src)),
                      writes=[("scrp", idx) + tuple(grp)], sem=("cast",) + tuple(grp), token_val=16 * n)
            K.res[("scr",) + tuple(grp)] = [(("dma", "cast") + tuple(grp), 16 * n), {}]

        def cast_ffn(l, j):
            items = []
            for m in range(NFF):
                items.append((s_gu[(l, j)][m, :, 0], ffn_w_gate[l, j, :, m * 128:(m + 1) * 128].rearrange("(k p) n -> p k n", p=128)))
                items.append((s_gu[(l, j)][m, :, 1], ffn_w_up[l, j, :, m * 128:(m + 1) * 128].rearrange("(k p) n -> p k n", p=128)))
            for mo in range(8):
                items.append((s_d[(l, j)][mo], ffn_w_down[l, j, :, mo * 128:(mo + 1) * 128].rearrange("(k p) n -> p k n", p=128)))
            cast_group(("ffn", l, j), items)

        def cast_odd(i):
            items = []
            for fc in range(8):
                items.append((s_owu[i][fc], odd_w_in[i, :, fc * 128:(fc + 1) * 128].rearrange("(k p) n -> p k n", p=128)))
            items.append((s_owv[i][:, :, :], odd_w_in[i, :, 1024:2048].rearrange("(k p) n -> p k n", p=128)))
            for mo in range(8):
                items.append((s_owo[i][mo], odd_w_out[i, :, mo * 128:(mo + 1) * 128].rearrange("(k p) n -> p k n", p=128)))
            cast_group(("odd", i), items)

        def cast_even(i):
            items = []
            cols = [0, 128, 256, 416, 544, 672, 800]
            for t, c0 in enumerate(cols):
                items.append((s_ewin[i][t], even_w_in[i, :, c0:c0 + 128].rearrange("(k p) n -> p k n", p=128)))
            items.append((s_ekpe[i][:, :, :], even_w_in[i, :, 384:416].rearrange("(k p) n -> p k n", p=128)))
            for h in range(8):
                items.append((s_euq[i][h], w_uq[i, :, h * 96:(h + 1) * 96].rearrange("(k p) n -> p k n", p=128)))
            items.append((s_eukv[i][:, :], w_ukv[i, :, :]))
            items.append((s_epw[i][:, :, :], pool_w[i].rearrange("g c d -> c g d")))
            items.append((s_ewoa[i][:, :], even_w_out[i, 0:512, :]))
            for mo in range(8):
                items.append((s_ewop[i][mo], even_w_out[i, 512:1024, mo * 128:(mo + 1) * 128].rearrange("(g p) n -> p g n", p=128)))
            cast_group(("even", i), items)

        done_cast = set()
        for (l, p) in plan:
            if p in ("ffn1", "ffn2"):
                key = ("ffn", l, 0 if p == "ffn1" else 1)
                if key not in done_cast:
                    cast_ffn(l, key[2])
            elif l % 2 == 0:
                key = ("even", l // 2)
                if key not in done_cast:
                    cast_even(l // 2)
            else:
                key = ("odd", l // 2)
                if key not in done_cast:
                    cast_odd(l // 2)
            done_cast.add(key)
        scr_tokens = {k: v for k, v in K.res.items() if k[0] == "scr"}

        def barrier():
            K.barrier_all()
            K.res.update({k: [v[0], {}] for k, v in scr_tokens.items()})
            ar.reset(ar_base)

        def hkey(c, b):
            return ("h", c, b)

        def emit_norm_act(b, sq, nchunk=8, c0=0):
            K.op("act", lambda e: e.activation(out=sq[:, 0:nchunk, :], in_=h_res[:, c0:c0 + nchunk, b * 512:(b + 1) * 512], func=AF.Square),
                 reads=[hkey(c, b) for c in range(c0, c0 + nchunk)], writes=["sq"])

        def emit_norm_rest(b, sq, rstd, hn, hn_key, gain, pstat):
            def mm(e):
                for c in range(8):
                    ins = e.matmul(ps[pstat][:, :], lhsT=ones_bf[:, :], rhs=sq[:, c, :], start=(c == 0), stop=(c == 7))
                return ins
            K.op("pe", mm, reads=["sq", "ones_bf"], writes=[("ps", pstat)])
            K.op("act", lambda e: e.activation(out=rstd[:, :], in_=ps[pstat][:, :], func=AF.Sqrt, bias=float(D * EPS)),
                 reads=[("ps", pstat)], writes=["rstd"])
            K.op("dve", lambda e: e.reciprocal(out=rstd[:, :], in_=rstd[:, :]), reads=["rstd"], writes=["rstd"])
            for c in range(8):
                K.op("dve", lambda e, c=c: e.scalar_tensor_tensor(out=hn[:, c, :], in0=h_res[:, c, b * 512:(b + 1) * 512],
                                                                  scalar=gain[:, c:c + 1], in1=rstd[:, :],
                                                                  op0=ALU.mult, op1=ALU.mult),
                     reads=[hkey(c, b), "rstd", "g_ffn", "g_mix"], writes=[hn_key])

        def load_seq(s):
            barrier()
            xs = [ar.alloc("xs", [128, D], F32) for _ in range(2)]
            for tc in range(NCH):
                sl = tc % 2
                K.dma("sp", lambda e, tc=tc, sl=sl: e.dma_start(out=xs[sl][:], in_=x[s, tc * 128:(tc + 1) * 128, :]),
                      writes=[("xs", sl)], sem=("xs", sl))
                for half in range(2):
                    bank = (tc * 2 + half) % 4

                    def tr(e, tc=tc, sl=sl, half=half, bank=bank):
                        for q in range(4):
                            c = half * 4 + q
                            ins = e.transpose(out=ps[bank][:, q * 128:(q + 1) * 128], in_=xs[sl][:, c * 128:(c + 1) * 128], identity=ident[:, :])
                        return ins
                    K.op("pe", tr, reads=[("xs", sl), "ident"], writes=[("ps", bank)])
                    b = tc // 4
                    eng = "act" if half == 0 else "dve"
                    dst = h_res[:, half * 4:half * 4 + 4, tc * 128:(tc + 1) * 128]
                    src = ps[bank][:, :].rearrange("p (q n) -> p q n", q=4)
                    if eng == "act":
                        K.op("act", lambda e, dst=dst, src=src: e.activation(out=dst, in_=src, func=AF.Copy),
                             reads=[("ps", bank)], writes=[hkey(c, b) for c in range(half * 4, half * 4 + 4)])
                    else:
                        K.op("dve", lambda e, dst=dst, src=src: e.tensor_copy(out=dst, in_=src),
                             reads=[("ps", bank)], writes=[hkey(c, b) for c in range(half * 4, half * 4 + 4)])

        def store_seq(s):
            barrier()
            xs = [ar.alloc("xs", [128, D], F32) for _ in range(2)]
            for tc in range(NCH):
                sl = tc % 2
                b = tc // 4
                for half in range(2):
                    bank = (tc * 2 + half) % 4

                    def tr(e, tc=tc, half=half, bank=bank):
                        for q in range(4):
                            c = half * 4 + q
                            ins = e.transpose(out=ps[bank][:, q * 128:(q + 1) * 128], in_=h_res[:, c, tc * 128:(tc + 1) * 128], identity=ident[:, :])
                        return ins
                    K.op("pe", tr, reads=[hkey(c, b) for c in range(half * 4, half * 4 + 4)] + ["ident"], writes=[("ps", bank)])
                    dst = xs[sl][:, half * 512:(half + 1) * 512]
                    if half == 0:
                        K.op("act", lambda e, dst=dst, bank=bank: e.activation(out=dst, in_=ps[bank][:, :], func=AF.Copy),
                             reads=[("ps", bank)], writes=[("xs", sl, half)])
                    else:
                        K.op("dve", lambda e, dst=dst, bank=bank: e.tensor_copy(out=dst, in_=ps[bank][:, :]),
                             reads=[("ps", bank)], writes=[("xs", sl, half)])
                K.dma("sp", lambda e, tc=tc, sl=sl: e.dma_start(out=y[s, tc * 128:(tc + 1) * 128, :], in_=xs[sl][:]),
                      reads=[("xs", sl, 0), ("xs", sl, 1)], writes=[("yout", tc)], sem=("ys", sl))

        def ffn_phase(s, l, j):
            barrier()
            hn = [ar.alloc("hn", [128, 8, 512], BF16) for _ in range(2)]
            sq = ar.alloc("sq", [128, 8, 512], BF16)
            rstd = ar.alloc("rstd", [128, 512], F32)
            sg = [ar.alloc("sg", [128, 512], F32) for _ in range(2)]
            act = ar.alloc("act", [128, NFF, 512], BF16)
            wgu = [ar.alloc("wgu", [128, 2, 8, 128], BF16) for _ in range(3)]
            wd = [ar.alloc("wd", [128, NFF, 128], BF16) for _ in range(2)]
            gain = g_ffn[:, l * 2 + j, :]
            scr = ("scr", "ffn", l, j)
            gu_units = [(b, m) for b in range(NB) for m in range(NFF)]
            d_units = [(b, mo) for b in range(NB) for mo in range(8)]
            st_gu = Stream(K, "wgu", wgu, gu_units,
                           lambda u, t: (lambda e: e.dma_start(out=t[:], in_=s_gu[(l, j)][u[1]])),
                           lambda u: [scr])
            st_d = Stream(K, "wd", wd, d_units,
                          lambda u, t: (lambda e: e.dma_start(out=t[:], in_=s_d[(l, j)][u[1]])),
                          lambda u: [scr])
            PG, PU, PD, PST = (0, 1), (2, 3), (4, 5), 6

            emit_norm_act(0, sq)
            emit_norm_rest(0, sq, rstd, hn[0], ("hn", 0), gain, PST)
            for b in range(NB):
                hs = b % 2
                for m in range(NFF):
                    wt, wkey = st_gu.get(b * NFF + m)
                    gb, ub = PG[m % 2], PU[m % 2]

                    def mmg(e, wt=wt, gb=gb, hs=hs):
                        for k in range(8):
                            ins = e.matmul(ps[gb][:, :], lhsT=wt[:, 0, k, :], rhs=hn[hs][:, k, :], start=(k == 0), stop=(k == 7))
                        return ins

                    def mmu(e, wt=wt, ub=ub, hs=hs):
                        for k in range(8):
                            ins = e.matmul(ps[ub][:, :], lhsT=wt[:, 1, k, :], rhs=hn[hs][:, k, :], start=(k == 0), stop=(k == 7))
                        return ins
                    K.op("pe", mmg, reads=[wkey, ("hn", hs)], writes=[("ps", gb)])
                    K.op("pe", mmu, reads=[wkey, ("hn", hs)], writes=[("ps", ub)])
                    K.op("act", lambda e, gb=gb, m=m: e.activation(out=sg[m % 2][:, :], in_=ps[gb][:, :], func=AF.Silu),
                         reads=[("ps", gb)], writes=[("sg", m % 2)])
                    K.op("dve", lambda e, ub=ub, m=m: e.tensor_tensor(out=act[:, m, :], in0=ps[ub][:, :], in1=sg[m % 2][:, :], op=ALU.mult),
                         reads=[("ps", ub), ("sg", m % 2)], writes=[("act", m)])
                    if b + 1 < NB and m == 4:
                        emit_norm_act(b + 1, sq)
                    if b + 1 < NB and m == 12:
                        emit_norm_rest(b + 1, sq, rstd, hn[1 - hs], ("hn", 1 - hs), gain, PST)
                for mo in range(8):
                    wt, wkey = st_d.get(b * 8 + mo)
                    db = PD[mo % 2]

                    def mmd(e, wt=wt, db=db):
                        for k in range(NFF):
                            ins = e.matmul(ps[db][:, :], lhsT=wt[:, k, :], rhs=act[:, k, :], start=(k == 0), stop=(k == NFF - 1))
                        return ins
                    K.op("pe", mmd, reads=[wkey] + [("act", k) for k in range(NFF)], writes=[("ps", db)])
                    hv = h_res[:, mo, b * 512:(b + 1) * 512]
                    K.op("dve", lambda e, db=db, hv=hv: e.scalar_tensor_tensor(out=hv, in0=ps[db][:, :], scalar=0.5, in1=hv,
                                                                               op0=ALU.mult, op1=ALU.add),
                         reads=[("ps", db), hkey(mo, b)], writes=[hkey(mo, b)])

        def odd_phase(s, l):
            i = l // 2
            barrier()
            scr = ("scr", "odd", i)
            wv = ar.alloc("wv", [128, 8, 1024], BF16)
            hn = ar.alloc("hn", [128, 8, 512], BF16)
            sqg = ar.alloc("sqg", [128, 8, 512], BF16)
            rstd = ar.alloc("rstd", [128, 512], F32)
            v32 = ar.alloc("v32", [128, 1024], F32)
            ss = ar.alloc("ss", [128, 8], F32)
            vn = ar.alloc("vn", [128, 4, 1024], BF16)
            uf = [ar.alloc("uf", [128, 512], F32) for _ in range(2)]
            wu = [ar.alloc("wu", [128, 8, 128], BF16) for _ in range(3)]
            wo = [ar.alloc("wo", [128, 8, 128], BF16) for _ in range(3)]
            sgwT = ar.alloc("sgwT", [128, 4, 128], BF16)
            sgw_raw = ar.alloc("sgw_raw", [128, 4, 128], F32)
            sgb = ar.alloc("sgb", [128, 512], F32)
            gsg = ar.alloc("gsg", [128, 1024], F32)
            triu = ar.alloc("triu", [128, 128], F32)
            K.dma("sp", lambda e: e.dma_start(out=wv[:], in_=s_owv[i][:, :, :]), reads=[scr], writes=["wv"], sem=("o", 0))
            K.dma("sp", lambda e: e.dma_start(out=sgw_raw[:], in_=sg_w[i].rearrange("g t s -> t g s")), writes=["sgw_raw"], sem=("o", 1))
            K.dma("sp", lambda e: e.dma_start(out=triu[:], in_=c_triu[:, :]), writes=["triu"], sem=("o", 2))
            K.dma("sp", lambda e: e.dma_start(out=sgb[0:1, :], in_=sg_b[i:i + 1].rearrange("o g t -> o (g t)")), writes=["sgb"], sem=("o", 3))
            K.dma("sp", lambda e: e.dma_start(out=gsg[:], in_=sg_norm[i:i + 1, :].broadcast_to([128, 1024])), writes=["gsg"], sem=("o", 4))
            K.op("dve", lambda e: e.tensor_scalar(out=gsg[:], in0=gsg[:], scalar1=32.0, scalar2=None, op0=ALU.mult), reads=["gsg"], writes=["gsg"])
            for g in range(4):
                K.op("pe", lambda e, g=g: e.transpose(out=ps[g % 2][:, 0:128], in_=sgw_raw[:, g, :], identity=ident[:, :]),
                     reads=["sgw_raw", "ident"], writes=[("ps", g % 2)])
                K.op("dve", lambda e, g=g: e.tensor_tensor(out=sgwT[:, g, :], in0=ps[g % 2][:, 0:128], in1=triu[:, :], op=ALU.mult),
                     reads=[("ps", g % 2), "triu"], writes=["sgwT"])
            u_units = [(b, fc) for b in range(NB) for fc in range(8)]
            st_u = Stream(K, "wu", wu, u_units, lambda u, t: (lambda e: e.dma_start(out=t[:], in_=s_owu[i][u[1]])), lambda u: [scr])
            st_o = Stream(K, "wo", wo, u_units, lambda u, t: (lambda e: e.dma_start(out=t[:], in_=s_owo[i][u[1]])), lambda u: [scr])
            gain = g_mix[:, l, :]
            for b in range(NB):
                emit_norm_act(b, sqg)
                emit_norm_rest(b, sqg, rstd, hn, "hn", gain, 6)
                for ch in range(4):
                    for half in range(2):
                        bank = half

                        def mmv(e, ch=ch, half=half, bank=bank):
                            for k in range(8):
                                ins = e.matmul(ps[bank][:, :], lhsT=hn[:, k, ch * 128:(ch + 1) * 128], rhs=wv[:, k, half * 512:(half + 1) * 512],
                                               start=(k == 0), stop=(k == 7))
                            return ins
                        K.op("pe", mmv, reads=["hn", "wv"], writes=[("ps", bank)])
                        K.op("act", lambda e, half=half, bank=bank: e.activation(out=v32[:, half * 512:(half + 1) * 512], in_=ps[bank][:, :], func=AF.Gelu_apprx_tanh),
                             reads=[("ps", bank)], writes=[("v32", half)])
                    K.op("act", lambda e, ch=ch: e.activation(out=vn[:, ch, :], in_=v32[:, :], func=AF.Square, accum_out=ss[:, ch:ch + 1]),
                         reads=[("v32", 0), ("v32", 1)], writes=[("vn", ch), ("ss", ch)])
                    K.op("act", lambda e, ch=ch: e.activation(out=ss[:, ch:ch + 1], in_=ss[:, ch:ch + 1], func=AF.Sqrt, bias=float(D * EPS)),
                         reads=[("ss", ch)], writes=[("ss", ch)])
                    K.op("dve", lambda e, ch=ch: e.reciprocal(out=ss[:, ch:ch + 1], in_=ss[:, ch:ch + 1]), reads=[("ss", ch)], writes=[("ss", ch)])
                    K.op("dve", lambda e, ch=ch: e.scalar_tensor_tensor(out=vn[:, ch, :], in0=v32[:, :], scalar=ss[:, ch:ch + 1], in1=gsg[:, :],
                                                                        op0=ALU.mult, op1=ALU.mult),
                         reads=[("v32", 0), ("v32", 1), ("ss", ch), "gsg"], writes=[("vn", ch)])
                for fc in range(8):
                    g = fc // 2
                    wt, wkey = st_u.get(b * 8 + fc)
                    ub = 2 + fc % 2
                    mb = 4 + fc % 2

                    def mmu(e, wt=wt, ub=ub):
                        for k in range(8):
                            ins = e.matmul(ps[ub][:, :], lhsT=wt[:, k, :], rhs=hn[:, k, :], start=(k == 0), stop=(k == 7))
                        return ins
                    K.op("pe", mmu, reads=[wkey, "hn"], writes=[("ps", ub)])
                    K.op("act", lambda e, ub=ub, fc=fc: e.activation(out=uf[fc % 2][:, :], in_=ps[ub][:, :], func=AF.Gelu_apprx_tanh),
                         reads=[("ps", ub)], writes=[("uf", fc % 2)])

                    def mmx(e, fc=fc, g=g, mb=mb):
                        for ch in range(4):
                            e.matmul(ps[mb][:, ch * 128:(ch + 1) * 128], lhsT=vn[:, ch, fc * 128:(fc + 1) * 128], rhs=sgwT[:, g, :], start=True, stop=False)
                            ins = e.matmul(ps[mb][:, ch * 128:(ch + 1) * 128], lhsT=ones_f[0:1, :], rhs=sgb[0:1, g * 128:(g + 1) * 128], start=False, stop=True)
                        return ins
                    K.op("pe", mmx, reads=[("vn", c) for c in range(4)] + ["sgwT", "sgb", "ones_f"], writes=[("ps", mb)])
                    K.op("dve", lambda e, fc=fc, mb=mb: e.tensor_tensor(out=sqg[:, fc, :], in0=ps[mb][:, :], in1=uf[fc % 2][:, :], op=ALU.mult),
                         reads=[("ps", mb), ("uf", fc % 2)], writes=["sq"])
                for mo in range(8):
                    wt, wkey = st_o.get(b * 8 + mo)
                    db = 6 + mo % 2

                    def mmo(e, wt=wt, db=db):
                        for k in range(8):
                            ins = e.matmul(ps[db][:, :], lhsT=wt[:, k, :], rhs=sqg[:, k, :], start=(k == 0), stop=(k == 7))
                        return ins
                    K.op("pe", mmo, reads=[wkey, "sq"], writes=[("ps", db)])
                    hv = h_res[:, mo, b * 512:(b + 1) * 512]
                    K.op("dve", lambda e, db=db, hv=hv: e.tensor_tensor(out=hv, in0=ps[db][:, :], in1=hv, op=ALU.add),
                         reads=[("ps", db), hkey(mo, b)], writes=[hkey(mo, b)])

        s_kpe = dscr("s_kpe", [32, S], F32)
        s_rl = dscr("s_rl", [2, 512], F32)
        TWO_PI = 2.0 * np.pi
        C1 = 6.28125
        C2 = TWO_PI - C1
        SC = 1.0 - 1e-6

        def tables_phase(s):
            barrier()
            invf = ar.alloc("invf", [128, 1], F32)
            pi_ = ar.alloc("pi_", [128, 512], I32)
            pf = ar.alloc("pf", [128, 512], F32)
            av = ar.alloc("av", [128, 512], F32)
            tf = ar.alloc("tf", [128, 512], F32)
            ki = ar.alloc("ki", [128, 512], I32)
            kf = ar.alloc("kf", [128, 512], F32)
            rr = ar.alloc("rr", [128, 512], F32)
            outt = [ar.alloc("outt", [128, 512], F32) for _ in range(2)]
            R = slice(64, 96)
            K.dma("sp", lambda e: e.dma_start(out=invf[:], in_=c_invfreq[:, :]), writes=["invf"], sem=("t", 0))
            sgn = ar.alloc("sgn", [128, 1], F32)
            hpi = ar.alloc("hpi", [128, 1], F32)
            K.dma("sp", lambda e: e.dma_start(out=sgn[:], in_=c_sgn[:, :]), writes=["sgn"], sem=("t", 4))
            K.op("dve", lambda e: e.memset(hpi[:], float(np.pi / 2 * SC)), writes=["hpi"])
            for b in range(NB):
                bl = slice(b * 512, (b + 1) * 512)
                K.dma("sp", lambda e, bl=bl: e.dma_start(out=pi_[R, :], in_=positions[s:s + 1, bl].broadcast_to([32, 512])), writes=["pi"], sem=("t", 1))
                K.op("dve", lambda e: e.tensor_copy(out=pf[R, :], in_=pi_[R, :]), reads=["pi"], writes=["pf"])
                K.op("dve", lambda e: e.tensor_scalar(out=av[R, :], in0=pf[R, :], scalar1=invf[R, 0:1], scalar2=None, op0=ALU.mult), reads=["pf", "invf"], writes=["av"])
                for which in range(2):
                    off = 0.25 if which == 0 else 0.0
                    K.op("dve", lambda e, off=off: e.tensor_scalar(out=ki[R, :], in0=av[R, :], scalar1=float(1.0 / TWO_PI), scalar2=float(off), op0=ALU.mult, op1=ALU.add),
                         reads=["av"], writes=["ki"])
                    K.op("dve", lambda e: e.tensor_copy(out=kf[R, :], in_=ki[R, :]), reads=["ki"], writes=["kf"])
                    K.op("dve", lambda e: e.scalar_tensor_tensor(out=rr[R, :], in0=kf[R, :], scalar=float(-C1), in1=av[R, :], op0=ALU.mult, op1=ALU.add),
                         reads=["kf", "av"], writes=["rr"])
                    K.op("dve", lambda e: e.scalar_tensor_tensor(out=rr[R, :], in0=kf[R, :], scalar=float(-C2), in1=rr[R, :], op0=ALU.mult, op1=ALU.add),
                         reads=["kf", "rr"], writes=["rr"])
                    if which == 0:
                        K.op("act", lambda e: e.activation(out=outt[0][R, :], in_=rr[R, :], func=AF.Sin, scale=float(SC), bias=hpi[R, 0:1]),
                             reads=["rr", "hpi"], writes=[("outt", 0)])
                    else:
                        K.op("act", lambda e: e.activation(out=outt[1][R, :], in_=rr[R, :], func=AF.Sin, scale=sgn[R, 0:1]),
                             reads=["rr", "sgn"], writes=[("outt", 1)])
                    K.dma("sp", lambda e, which=which, bl=bl: e.dma_start(out=s_cs[s, which, :, bl], in_=outt[which][R, :]),
                          reads=[("outt", which)], writes=[("cs", b, which)], sem=("t", 2 + which))

        def even_phase(s, l):
            i = l // 2
            barrier()
            scr = ("scr", "even", i)
            R = slice(64, 96)
            cqn = ar.alloc("cqn", [128, 2, S], BF16)
            ckvn = ar.alloc("ckvn", [128, S], BF16)
            gq = ar.alloc("gq", [128, 2], F32)
            gkv = ar.alloc("gkv", [128, 1], F32)
            qg = ar.alloc("qg", [128, 1], F32)
            kg = ar.alloc("kg", [128, 1], F32)
            pscale = ar.alloc("pscale", [128, 4], F32)
            K.dma("sp", lambda e: e.dma_start(out=gq[:], in_=q_a_norm[i:i + 1, :].rearrange("o (c p) -> p (o c)", p=128)), writes=["gq"], sem=("e", 0))
            K.dma("sp", lambda e: e.dma_start(out=gkv[:], in_=kv_a_norm[i:i + 1, :].rearrange("o p -> p o")), writes=["gkv"], sem=("e", 1))
            K.dma("sp", lambda e: e.dma_start(out=qg[0:96, :], in_=q_norm[i:i + 1, :].rearrange("o p -> p o")), writes=["qg"], sem=("e", 2))
            K.dma("sp", lambda e: e.dma_start(out=kg[0:96, :], in_=k_norm[i:i + 1, :].rearrange("o p -> p o")), writes=["kg"], sem=("e", 3))
            K.dma("sp", lambda e: e.dma_start(out=pscale[:], in_=pool_scale[i:i + 1, :].rearrange("o (g p) -> p (o g)", p=128)), writes=["pscale"], sem=("e", 4))
            K.op("dve", lambda e: e.tensor_scalar(out=gq[:], in0=gq[:], scalar1=16.0, scalar2=None, op0=ALU.mult), reads=["gq"], writes=["gq"])
            K.op("dve", lambda e: e.tensor_scalar(out=gkv[:], in0=gkv[:], scalar1=float(np.sqrt(128.0)), scalar2=None, op0=ALU.mult), reads=["gkv"], writes=["gkv"])
            K.op("dve", lambda e: e.tensor_scalar(out=kg[0:96, :], in0=kg[0:96, :], scalar1=float(np.sqrt(96.0)), scalar2=None, op0=ALU.mult), reads=["kg"], writes=["kg"])
            mA = ar.mark()
            hn = ar.alloc("hn", [128, 8, 512], BF16)
            sq = ar.alloc("sq", [128, 8, 512], BF16)
            rstd = ar.alloc("rstd", [128, 512], F32)
            rstc = ar.alloc("rstc", [128, 512], F32)
            win = [ar.alloc("win", [128, 8, 128], BF16) for _ in range(3)]
            wkpe = ar.alloc("wkpe", [128, 8, 32], BF16)
            pbuf = ar.alloc("pbuf", [128, 528], F32)
            t1 = ar.alloc("t1", [128, 528], F32)
            t2 = ar.alloc("t2", [128, 528], F32)
            halo = ar.alloc("halo", [128, 4, 16], F32)
            pooled = [ar.alloc("pooled", [128, 512], BF16) for _ in range(2)]
            pfix = ar.alloc("pfix", [128, 16], F32)
            pout = ar.alloc("pout", [128, 4, 512], BF16)
            pw = ar.alloc("pw", [128, 4, 128], BF16)
            invc = ar.alloc("invc", [128, 64], F32)
            wop = [ar.alloc("wop", [128, 4, 128], BF16) for _ in range(3)]
            kst = ar.alloc("kst", [128, 512], F32)
            K.dma("sp", lambda e: e.dma_start(out=wkpe[:], in_=s_ekpe[i][:, :, :]), reads=[scr], writes=["wkpe"], sem=("e", 5))
            K.dma("sp", lambda e: e.dma_start(out=pw[:], in_=s_epw[i][:, :, :]), reads=[scr], writes=["pw"], sem=("e", 6))
            K.dma("sp", lambda e: e.dma_start(out=invc[:], in_=c_invcnt[:, :]), writes=["invc"], sem=("e", 7))
            K.op("dve", lambda e: e.memset(halo[:], 0.0), writes=["halo"])
            in_units = [(b, t) for b in range(NB) for t in range(7)]
            st_in = Stream(K, "win", win, in_units, lambda u, t: (lambda e: e.dma_start(out=t[:], in_=s_ewin[i][u[1]])), lambda u: [scr])
            op_units = [(b, mo) for b in range(NB) for mo in range(8)]
            st_op = Stream(K, "wop", wop, op_units, lambda u, t: (lambda e: e.dma_start(out=t[:], in_=s_ewop[i][u[1]])), lambda u: [scr])
            gain = g_mix[:, l, :]
            WIN = (2, 4, 8, 16)
            for b in range(NB):
                bl = slice(b * 512, (b + 1) * 512)
                emit_norm_act(b, sq)
                emit_norm_rest(b, sq, rstd, hn, "hn", gain, 6)
                banks = [0, 1, 2, 4, 5, 4, 5]
                for t in range(7):
                    wt, wkey = st_in.get(b * 7 + t)
                    bk = banks[t]

                    def mmi(e, wt=wt, bk=bk):
                        for k in range(8):
                            ins = e.matmul(ps[bk][:, :], lhsT=wt[:, k, :], rhs=hn[:, k, :], start=(k == 0), stop=(k == 7))
                        return ins
                    K.op("pe", mmi, reads=[wkey, "hn"], writes=[("ps", bk)])
                    if t == 1:
                        for c in range(2):
                            K.op("act", lambda e, c=c: e.activation(out=sq[:, c, :], in_=ps[c][:, :], func=AF.Square), reads=[("ps", c)], writes=["sq"])

                        def mms(e):
                            e.matmul(ps[6][:, :], lhsT=ones_bf[:, :], rhs=sq[:, 0, :], start=True, stop=False)
                            return e.matmul(ps[6][:, :], lhsT=ones_bf[:, :], rhs=sq[:, 1, :], start=False, stop=True)
                        K.op("pe", mms, reads=["sq", "ones_bf"], writes=[("ps", 6)])
                        K.op("act", lambda e: e.activation(out=rstc[:, :], in_=ps[6][:, :], func=AF.Sqrt, bias=float(256 * EPS)), reads=[("ps", 6)], writes=["rstc"])
                        K.op("dve", lambda e: e.reciprocal(out=rstc[:, :], in_=rstc[:, :]), reads=["rstc"], writes=["rstc"])
                        for c in range(2):
                            K.op("dve", lambda e, c=c, bl=bl: e.scalar_tensor_tensor(out=cqn[:, c, bl], in0=ps[c][:, :], scalar=gq[:, c:c + 1], in1=rstc[:, :],
                                                                                     op0=ALU.mult, op1=ALU.mult),
                                 reads=[("ps", c), "gq", "rstc"], writes=[("cqn", b)])
                    if t == 2:
                        K.op("act", lambda e: e.activation(out=sq[:, 2, :], in_=ps[2][:, :], func=AF.Square), reads=[("ps", 2)], writes=["sq"])
                        K.op("pe", lambda e: e.matmul(ps[6][:, :], lhsT=ones_bf[:, :], rhs=sq[:, 2, :], start=True, stop=True), reads=["sq", "ones_bf"], writes=[("ps", 6)])
                        K.op("act", lambda e: e.activation(out=rstc[:, :], in_=ps[6][:, :], func=AF.Sqrt, bias=float(128 * EPS)), reads=[("ps", 6)], writes=["rstc"])
                        K.op("dve", lambda e: e.reciprocal(out=rstc[:, :], in_=rstc[:, :]), reads=["rstc"], writes=["rstc"])
                        K.op("dve", lambda e, bl=bl: e.scalar_tensor_tensor(out=ckvn[:, bl], in0=ps[2][:, :], scalar=gkv[:, 0:1], in1=rstc[:, :], op0=ALU.mult, op1=ALU.mult),
                             reads=[("ps", 2), "gkv", "rstc"], writes=[("ckvn", b)])

                        def mmk(e):
                            for k in range(8):
                                ins = e.matmul(ps[3][R, :], lhsT=wkpe[:, k, :], rhs=hn[:, k, :], start=(k == 0), stop=(k == 7))
                            return ins
                        K.op("pe", mmk, reads=["wkpe", "hn"], writes=[("ps", 3)])
                        K.op("act", lambda e: e.activation(out=kst[R, :], in_=ps[3][R, :], func=AF.Copy), reads=[("ps", 3)], writes=["kst"])
                        K.dma("sp", lambda e, bl=bl: e.dma_start(out=s_kpe[:, bl], in_=kst[R, :]), reads=["kst"], writes=[("kpe", b)], sem=("e", 8))
                    if t >= 3:
                        g = t - 3
                        w = WIN[g]
                        K.op("act", lambda e, bk=bk: e.activation(out=pbuf[:, 16:528], in_=ps[bk][:, :], func=AF.Copy), reads=[("ps", bk)], writes=["pbuf"])
                        K.op("dve", lambda e, g=g: e.tensor_copy(out=pbuf[:, 0:16], in_=halo[:, g, :]), reads=["halo"], writes=["pbuf"])
                        K.op("dve", lambda e: e.tensor_tensor(out=t1[:, 1:528], in0=pbuf[:, 1:528], in1=pbuf[:, 0:527], op=ALU.add), reads=["pbuf"], writes=["t1"])
                        fin = t1
                        if g >= 1:
                            K.op("dve", lambda e: e.tensor_tensor(out=t2[:, 3:528], in0=t1[:, 3:528], in1=t1[:, 1:526], op=ALU.add), reads=["t1"], writes=["t2"])
                            fin = t2
                        if g >= 2:
                            K.op("dve", lambda e: e.tensor_tensor(out=t1[:, 7:528], in0=t2[:, 7:528], in1=t2[:, 3:524], op=ALU.add), reads=["t2"], writes=["t1"])
                            fin = t1
                        if g >= 3:
                            K.op("dve", lambda e: e.tensor_tensor(out=t2[:, 15:528], in0=t1[:, 15:528], in1=t1[:, 7:520], op=ALU.add), reads=["t1"], writes=["t2"])
                            fin = t2
                        fkey = "t1" if fin is t1 else "t2"
                        pl = pooled[g % 2]
                        K.op("dve", lambda e, fin=fin, pl=pl, w=w: e.scalar_tensor_tensor(out=pl[:, :], in0=fin[:, 16:528], scalar=float(1.0 / w), in1=pbuf[:, 16:528],
                                                                                           op0=ALU.mult, op1=ALU.subtract),
                             reads=[fkey, "pbuf"], writes=[("pooled", g % 2)])
                        if b == 0:
                            K.op("dve", lambda e, fin=fin, g=g: e.tensor_tensor(out=pfix[:, :], in0=fin[:, 16:32], in1=invc[:, g * 16:(g + 1) * 16], op=ALU.mult),
                                 reads=[fkey, "invc"], writes=["pfix"])
                            K.op("dve", lambda e, pl=pl: e.tensor_tensor(out=pl[:, 0:16], in0=pfix[:, :], in1=pbuf[:, 16:32], op=ALU.subtract),
                                 reads=["pfix", "pbuf", ("pooled", g % 2)], writes=[("pooled", g % 2)])
                        K.op("dve", lambda e, g=g: e.tensor_copy(out=halo[:, g, :], in_=pbuf[:, 512:528]), reads=["pbuf"], writes=["halo"])
                        K.op("pe", lambda e, g=g, pl=pl: e.matmul(ps[7][:, :], lhsT=pw[:, g, :], rhs=pl[:, :], start=True, stop=True),
                             reads=["pw", ("pooled", g % 2)], writes=[("ps", 7)])
                        K.op("act", lambda e, g=g: e.activation(out=pout[:, g, :], in_=ps[7][:, :], func=AF.Copy, scale=pscale[:, g:g + 1]),
                             reads=[("ps", 7), "pscale"], writes=[("pout", g)])
                for mo in range(8):
                    wt, wkey = st_op.get(b * 8 + mo)
                    db = mo % 2

                    def mmo(e, wt=wt, db=db):
                        for g in range(4):
                            ins = e.matmul(ps[db][:, :], lhsT=wt[:, g, :], rhs=pout[:, g, :], start=(g == 0), stop=(g == 3))
                        return ins
                    K.op("pe", mmo, reads=[wkey] + [("pout", g) for g in range(4)], writes=[("ps", db)])
                    hv = h_res[:, mo, bl]
                    K.op("dve", lambda e, db=db, hv=hv: e.tensor_tensor(out=hv, in0=ps[db][:, :], in1=hv, op=ALU.add),
                         reads=[("ps", db), hkey(mo, b)], writes=[hkey(mo, b)])

            K.barrier_all()
            K.res.update({k: [v[0], {}] for k, v in scr_tokens.items()})
            ar.reset(mA)
            kh = [ar.alloc("kh", [128, S], BF16) for _ in range(2)]
            vx = ar.alloc("vx", [128, NCH, 65], BF16)
            wuq = [ar.alloc("wuq", [128, 2, 96], BF16) for _ in range(2)]
            wukv = ar.alloc("wukv", [128, 1024], BF16)
            woa = [ar.alloc("woa", [128, 1024], BF16) for _ in range(2)]
            cosT = ar.alloc("cosT", [128, 512], F32)
            sinT = ar.alloc("sinT", [128, 512], F32)
            xk = ar.alloc("xk", [128, 512], F32)
            sqk = ar.alloc("sqk", [128, 512], BF16)
            rstb = ar.alloc("rstb", [128, 512], F32)
            xr = ar.alloc("xr", [128, 512], F32)
            tmp = ar.alloc("tmp", [128, 512], F32)
            qh = [ar.alloc("qh", [128, 512], BF16) for _ in range(2)]
            pT = [ar.alloc("pT", [128, 512], BF16) for _ in range(3)]
            bc = ar.alloc("bc", [128, 512], F32)
            rl = bc
            ao = ar.alloc("ao", [128, 512], BF16)
            rotm = ar.alloc("rotm", [128, 128], F32)
            trb = ar.alloc("trb", [128, 128], BF16)
            idb = ar.alloc("idb", [128, 128], BF16)
            K.dma("sp", lambda e: e.dma_start(out=wukv[:], in_=s_eukv[i][:, :]), reads=[scr], writes=["wukv"], sem=("e", 9))
            K.dma("sp", lambda e: e.dma_start(out=rotm[:], in_=c_rotm[:, :]), writes=["rotm"], sem=("e", 10))
            K.dma("pool", lambda e: e.dma_start(out=trb[:], in_=c_maskneg[:, :]), writes=["trb"], sem=("e", 11))
            K.dma("pool", lambda e: e.dma_start(out=idb[:], in_=c_ident[:, :]), writes=["idb"], sem=("e", 15))
            K.op("dve", lambda e: e.memset(vx[:, :, 64:65], 1.0), writes=["vx1"])
            eps96 = ar.alloc("eps96", [128, 1], F32)
            K.op("dve", lambda e: e.memset(eps96[:], float(96 * EPS)), writes=["eps96"])
            PQ, PST, PDR, PSC, PO = 0, 1, (2, 3), (4, 5), (6, 7)

            def g_norm_rope(src_nope, src_rope, rope_key, gcol, dst, dst_key, j):
                bl = slice(j * 512, (j + 1) * 512)
                K.dma("sp", lambda e, bl=bl: e.dma_start(out=cosT[R, :], in_=s_cs[s, 0, :, bl]), reads=[("cs", j, 0)], writes=["cosT"], sem=("e", 12))
                K.dma("sp", lambda e, bl=bl: e.dma_start(out=sinT[R, :], in_=s_cs[s, 1, :, bl]), reads=[("cs", j, 1)], writes=["sinT"], sem=("e", 13))
                yield
                if src_rope is None:
                    K.op("act", lambda e: e.activation(out=sqk[0:96, :], in_=src_nope[0:96, :], func=AF.Square), reads=[("ps", PQ)], writes=["sqk"])
                    rsrc = src_nope
                else:
                    K.op("act", lambda e: e.activation(out=sqk[0:64, :], in_=src_nope[0:64, :], func=AF.Square), reads=[("ps", PQ)], writes=["sqk"])
                    K.op("act", lambda e: e.activation(out=sqk[R, :], in_=src_rope[R, :], func=AF.Square), reads=[rope_key, "sqk"], writes=["sqk"])
                    rsrc = src_rope
                yield
                K.op("pe", lambda e: e.matmul(ps[PST][0:96, :], lhsT=ones_bf[0:96, 0:96], rhs=sqk[0:96, :], start=True, stop=True), reads=["sqk", "ones_bf"], writes=[("ps", PST)])
                yield
                K.op("act", lambda e: e.activation(out=rstb[0:96, :], in_=ps[PST][0:96, :], func=AF.Ln, bias=eps96[0:96, 0:1]), reads=[("ps", PST), "eps96"], writes=["rstb"])
                yield
                K.op("act", lambda e: e.activation(out=rstb[0:96, :], in_=rstb[0:96, :], func=AF.Exp, scale=-0.5), reads=["rstb"], writes=["rstb"])
                yield
                K.op("dve", lambda e: e.scalar_tensor_tensor(out=dst[0:64, :], in0=src_nope[0:64, :], scalar=gcol[0:64, 0:1], in1=rstb[0:64, :], op0=ALU.mult, op1=ALU.mult),
                     reads=[("ps", PQ), "rstb", "qg", "kg"], writes=[dst_key])
                K.op("dve", lambda e: e.scalar_tensor_tensor(out=xr[R, :], in0=rsrc[R, :], scalar=gcol[R, 0:1], in1=rstb[R, :], op0=ALU.mult, op1=ALU.mult),
                     reads=[("ps", PQ), rope_key, "rstb", "qg", "kg"], writes=["xr"])
                yield
                K.dma("sp", lambda e: e.dma_start(out=tmp[64:80, :], in_=xr[80:96, :]), reads=["xr"], writes=["tmp"], sem=("e", 16))
                K.dma("sp", lambda e: e.dma_start(out=tmp[80:96, :], in_=xr[64:80, :]), reads=["xr"], writes=["tmp2"], sem=("e", 17))
                yield
                K.op("dve", lambda e: e.tensor_tensor(out=tmp[R, :], in0=tmp[R, :], in1=sinT[R, :], op=ALU.mult), reads=["tmp", "tmp2", "sinT"], writes=["tmp", "tmp2"])
                K.op("dve", lambda e: e.tensor_tensor(out=xr[R, :], in0=xr[R, :], in1=cosT[R, :], op=ALU.mult), reads=["xr", "cosT", "tmp", "tmp2"], writes=["xr"])
                yield
                K.op("dve", lambda e: e.tensor_tensor(out=dst[R, :], in0=xr[R, :], in1=tmp[R, :], op=ALU.add), reads=["xr", "tmp", "tmp2"], writes=[dst_key])
                yield

            def g_kblock(h, kb, j):
                bl = slice(j * 512, (j + 1) * 512)
                K.dma("sp", lambda e, bl=bl: e.dma_start(out=xk[R, :], in_=s_kpe[:, bl]), reads=[("kpe", j)], writes=["xk"], sem=("e", 14))
                K.op("pe", lambda e, h=h, bl=bl: e.matmul(ps[PQ][0:64, :], lhsT=wukv[:, h * 128:h * 128 + 64], rhs=ckvn[:, bl], start=True, stop=True),
                     reads=["wukv", ("ckvn", j)], writes=[("ps", PQ)])
                yield
                yield from g_norm_rope(ps[PQ], xk, "xk", kg, kh[kb][:, bl], ("kh", kb, j), j)

            def g_qbuild(hs, j):
                bl = slice(j * 512, (j + 1) * 512)

                def mmq(e, hs=hs, bl=bl):
                    e.matmul(ps[PQ][0:96, :], lhsT=wuq[hs][:, 0, :], rhs=cqn[:, 0, bl], start=True, stop=False)
                    return e.matmul(ps[PQ][0:96, :], lhsT=wuq[hs][:, 1, :], rhs=cqn[:, 1, bl], start=False, stop=True)
                K.op("pe", mmq, reads=[("wuq", hs), ("cqn", j)], writes=[("ps", PQ)])
                yield
                yield from g_norm_rope(ps[PQ], None, ("ps", PQ), qg, qh[j % 2], ("qh", j % 2), j)

            def g_tail(hs, j):
                ob = PO[j % 2]
                bl = slice(j * 512, (j + 1) * 512)
                K.op("act", lambda e: e.activation(out=rl[64:65, :], in_=ps[ob][64:65, :], func=AF.Ln), reads=[("ps", ob)], writes=["rl"])
                yield
                K.op("act", lambda e: e.activation(out=rl[64:65, :], in_=rl[64:65, :], func=AF.Exp, scale=-1.0), reads=["rl"], writes=["rl"])
                yield
                K.dma("sp", lambda e: e.dma_start(out=s_rl[j % 2:j % 2 + 1, :], in_=rl[64:65, :]), reads=["rl"], writes=[("s_rl", j % 2)], sem=("e", 18))
                yield
                K.dma("sp", lambda e: e.dma_start(out=bc[0:64, :], in_=s_rl[j % 2:j % 2 + 1, :].broadcast_to([64, 512])), reads=[("s_rl", j % 2)], writes=["bc"], sem=("e", 19))
                yield
                K.op("dve", lambda e: e.tensor_tensor(out=ao[0:64, :], in0=ps[ob][0:64, :], in1=bc[0:64, :], op=ALU.mult), reads=[("ps", ob), "bc"], writes=["ao"])
                yield
                for mo in range(8):
                    db = PDR[mo % 2]
                    K.op("pe", lambda e, mo=mo, db=db: e.matmul(ps[db][:, :], lhsT=woa[hs][0:64, mo * 128:(mo + 1) * 128], rhs=ao[0:64, :], start=True, stop=True),
                         reads=[("woa", hs), "ao"], writes=[("ps", db)])
                    hv = h_res[:, mo, bl]
                    K.op("dve", lambda e, db=db, hv=hv: e.tensor_tensor(out=hv, in0=ps[db][:, :], in1=hv, op=ALU.add),
                         reads=[("ps", db), hkey(mo, j)], writes=[hkey(mo, j)])
                    yield

            class Lanes:
                def __init__(self):
                    self.tail = None
                    self.cur = None
                    self.qpend = None
                    self.kpend = []

                def _step(self, g):
                    try:
                        next(g)
                        return True
                    except StopIteration:
                        return False

                def step_b(self):
                    while True:
                        if self.cur is None:
                            if self.qpend is not None:
                                self.cur, self.qpend = self.qpend, None
                            elif self.kpend:
                                self.cur = self.kpend.pop(0)
                            else:
                                return False
                        if self._step(self.cur):
                            return True
                        self.cur = None

                def step_a(self):
                    if self.tail is not None:
                        if self._step(self.tail):
                            return True
                        self.tail = None
                    return False

                def pump(self, n):
                    for t in range(n):
                        if t % 2 == 0:
                            if not self.step_a():
                                self.step_b()
                        else:
                            if not self.step_b():
                                self.step_a()

                def finish_tail(self):
                    while self.step_a():
                        pass

                def finish_q(self):
                    while self.cur is not None or self.qpend is not None:
                        if self.cur is None:
                            self.cur, self.qpend = self.qpend, None
                        if not self._step(self.cur):
                            self.cur = None

                def finish_all(self):
                    self.finish_tail()
                    while self.step_b():
                        pass

            def drain(g):
                if g is not None:
                    for _ in g:
                        pass

            def load_head_w(h):
                hs = h % 2
                K.dma("sp", lambda e: e.dma_start(out=wuq[hs][:], in_=s_euq[i][h]), reads=[scr], writes=[("wuq", hs)], sem=("wuq", hs))
                K.dma("sp", lambda e: e.dma_start(out=woa[hs][0:64, :], in_=s_ewoa[i][h * 64:(h + 1) * 64, :]), reads=[scr], writes=[("woa", hs)], sem=("woa", hs))

            sc_i = 0
            load_head_w(0)
            for j in range(NB):
                drain(g_kblock(0, 0, j))
            L = Lanes()
            for h in range(8):
                hs = h % 2
                kb = h % 2
                if h + 1 < 8:
                    load_head_w(h + 1)
                for c0 in range(0, NCH, 8):
                    nq = min(8, NCH - c0)

                    def mmv(e, h=h, c0=c0, nq=nq):
                        for q in range(nq):
                            ins = e.matmul(ps[PDR[1]][:, q * 64:(q + 1) * 64], lhsT=ckvn[:, (c0 + q) * 128:(c0 + q + 1) * 128], rhs=wukv[:, h * 128 + 64:(h + 1) * 128],
                                           start=True, stop=True)
                        return ins
                    K.op("pe", mmv, reads=["wukv"] + [("ckvn", (c0 + q) // 4) for q in range(nq)], writes=[("ps", PDR[1])])
                    K.op("act", lambda e, c0=c0, nq=nq: e.activation(out=vx[:, c0:c0 + nq, 0:64], in_=ps[PDR[1]][:, 0:nq * 64].rearrange("p (q n) -> p q n", q=nq), func=AF.Copy),
                         reads=[("ps", PDR[1])], writes=[("vx", c0 // 8)])
                if h + 1 < 8:
                    L.kpend = [g_kblock(h + 1, 1 - kb, jj) for jj in range(NB)]
                L.qpend = g_qbuild(hs, 0)
                L.finish_q()
                for j in range(NB):
                    if j + 1 < NB:
                        L.qpend = g_qbuild(hs, j + 1)
                    qt = qh[j % 2]
                    ob = PO[j % 2]
                    nkt = 4 * (j + 1)
                    for kt in range(nkt):
                        d = kt - 4 * j
                        lo = max(0, d) * 128
                        sb = PSC[sc_i % 2]
                        pt = pT[sc_i % 3]
                        pkey = ("pT", sc_i % 3)
                        sc_i += 1

                        def mms(e, kt=kt, lo=lo, sb=sb, qt=qt, d=d, kb=kb):
                            ins = e.matmul(ps[sb][:, lo:512], lhsT=kh[kb][0:96, kt * 128:(kt + 1) * 128], rhs=qt[0:96, lo:512], start=True, stop=(d < 0))
                            if d >= 0:
                                ins = e.matmul(ps[sb][:, lo:lo + 128], lhsT=idb[:, :], rhs=trb[:, :], start=False, stop=True)
                            return ins
                        K.op("pe", mms, reads=[("kh", kb, kt // 4), ("qh", j % 2), "trb", "idb"], writes=[("ps", sb)])
                        K.op("act", lambda e, lo=lo, sb=sb, pt=pt: e.activation(out=pt[:, lo:512], in_=ps[sb][:, lo:512], func=AF.Exp), reads=[("ps", sb)], writes=[pkey])
                        K.op("pe", lambda e, kt=kt, lo=lo, pt=pt, nkt=nkt, ob=ob: e.matmul(ps[ob][0:65, lo:512], lhsT=vx[:, kt, 0:65], rhs=pt[:, lo:512], start=(kt == 0), stop=(kt == nkt - 1)),
                             reads=[pkey, ("vx", kt // 8), "vx1"], writes=[("ps", ob)])
                        L.pump(3)
                    L.finish_tail()
                    L.finish_q()
                    L.tail = g_tail(hs, j)
                L.finish_all()

        PHASES = {"ffn": ffn_phase, "odd": odd_phase, "even": even_phase}

        for s in range(NSEQ):
            load_seq(s)
            if need_even:
                tables_phase(s)
            for (l, p) in plan:
                if p == "ffn1":
                    ffn_phase(s, l, 0)
                elif p == "ffn2":
                    ffn_phase(s, l, 1)
                elif l % 2 == 1:
                    PHASES["odd"](s, l)
                else:
                    PHASES["even"](s, l)
            store_seq(s)
        K.barrier_all()
        with nc.allow_non_contiguous_dma(reason="small constant loads"):
            K.emit()
    return nc


FULL_PLAN = [(l, p) for l in range(DEPTH) for p in ("ffn1", "mix", "ffn2")]


def make_consts():
    ident = np.eye(128, dtype=np.float32)
    triu = np.triu(np.ones((128, 128), np.float32))
    rotm = np.zeros((128, 128), np.float32)
    for i in range(16):
        rotm[64 + i + 16, i] = -1.0
        rotm[64 + i, i + 16] = 1.0
    invf = np.zeros((128, 1), np.float32)
    f = (10000.0 ** (-np.arange(0, 32, 2, dtype=np.float32) / 32)).astype(np.float32)
    invf[64:80, 0] = f
    invf[80:96, 0] = f
    invcnt = np.zeros((128, 64), np.float32)
    for g, w in enumerate((2, 4, 8, 16)):
        for t in range(16):
            invcnt[:, g * 16 + t] = 1.0 / min(t + 1, w)
    maskneg = ((1.0 - triu) * -30000.0).astype(np.float32)
    sgn = np.full((128, 1), 1.0 - 1e-6, np.float32)
    sgn[64:80] = -(1.0 - 1e-6)
    return {"c_ident": ident, "c_triu": triu, "c_maskneg": maskneg, "c_sgn": sgn, "c_rotm": rotm, "c_invfreq": invf, "c_invcnt": invcnt}


_CACHE = {}


def kernel(**inputs):
    n = 8
    S = inputs["x"].shape[1]
    B = inputs["x"].shape[0]
    nseq = B // n
    key = (S, nseq)
    if key not in _CACHE:
        _CACHE[key] = build(S, nseq, FULL_PLAN)
    nc = _CACHE[key]
    consts = make_consts()
    in_maps = []
    for c in range(n):
        m = {k: np.ascontiguousarray(v) for k, v in inputs.items() if k not in ("x", "positions")}
        m["x"] = np.ascontiguousarray(inputs["x"][c * nseq:(c + 1) * nseq])
        m["positions"] = np.ascontiguousarray(inputs["positions"][c * nseq:(c + 1) * nseq]).astype(np.int32)
        m.update(consts)
        in_maps.append(m)
    res = run_bass_kernel_spmd(nc, in_maps, core_ids=list(range(n)))
    return np.concatenate([np.asarray(r["y"]) for r in res.results], axis=0).astype(np.float32)
```

```python
import contextlib
import numpy as np
import concourse.bass as bass
import concourse.mybir as mybir
from concourse.bass_utils import run_bass_kernel_spmd

F32 = mybir.dt.float32
BF16 = mybir.dt.bfloat16
I32 = mybir.dt.int32
AF = mybir.ActivationFunctionType
ALU = mybir.AluOpType

D = 1024
DFF = 2816
NFF = 22
DEPTH = 4
EPS = 1e-6
SB_BASE = 16512
SB_END = 229376
HRES_BYTES = 8 * 4096 * 4
ENGS = ("pe", "act", "dve", "pool", "sp")


def _dsize(dt):
    return 2 if dt == BF16 else 4


class Sched:
    def __init__(self, nc, stack):
        self.nc = nc
        self.stack = stack
        self.q = {e: [] for e in ENGS}
        self.tok = {}
        self.sem = {}
        self.seen = {}
        self.res = {}

    def _semh(self, key):
        if key not in self.sem:
            name = "s_" + "_".join(str(x) for x in key)
            self.sem[key] = self.stack.enter_context(self.nc.semaphore(name))
        return self.sem[key]

    def _collect(self, eng, reads, writes):
        need = {}

        def add(t):
            if t is not None:
                k, v = t
                if need.get(k, 0) < v:
                    need[k] = v

        for r in reads:
            e = self.res.get(r)
            if e is not None:
                add(e[0])
        for w in writes:
            e = self.res.get(w)
            if e is not None:
                add(e[0])
                for k, v in e[1].items():
                    add((k, v))
        waits = []
        for k, v in need.items():
            if eng == "pe" and k == ("eng", "pe"):
                continue
            if self.seen.get((eng, k), 0) < v:
                waits.append((k, v))
                self.seen[(eng, k)] = v
        return waits

    def _commit(self, reads, writes, token):
        k, v = token
        for r in reads:
            e = self.res.setdefault(r, [None, {}])
            if e[1].get(k, 0) < v:
                e[1][k] = v
        for w in writes:
            self.res[w] = [token, {}]

    def op(self, eng, fn, reads=(), writes=()):
        waits = self._collect(eng, reads, writes)
        key = ("eng", eng)
        self._semh(key)
        self.tok[key] = self.tok.get(key, 0) + 1
        self._commit(reads, writes, (key, self.tok[key]))
        self.q[eng].append((waits, fn, (key, 1)))

    def dma(self, queue, fn, reads=(), writes=(), sem=None, token_val=None):
        waits = self._collect(queue, reads, writes)
        key = ("dma",) + tuple(sem)
        self._semh(key)
        self.tok[key] = self.tok.get(key, 0) + 16
        tv = self.tok[key] if token_val is None else token_val
        self._commit(reads, writes, (key, tv))
        self.q[queue].append((waits, fn, (key, 16)))

    def barrier_all(self):
        waits = []
        for k, v in self.tok.items():
            if k[0] == "dma" and k[1] != "cast" and self.seen.get(("sp", k), 0) < v:
                waits.append((k, v))
                self.seen[("sp", k)] = v
        for k, v in self.tok.items():
            if k[0] == "eng" and k != ("eng", "sp") and self.seen.get(("sp", k), 0) < v:
                waits.append((k, v))
                self.seen[("sp", k)] = v
        key = ("eng", "sp")
        self._semh(key)
        self.tok[key] = self.tok.get(key, 0) + 1
        self.q["sp"].append((waits, (lambda e: e.nop()), (key, 1)))
        for e in ENGS:
            if e == "sp":
                continue
            ws = []
            for k, v in self.tok.items():
                if k[0] == "eng" and k != ("eng", e) and self.seen.get((e, k), 0) < v:
                    ws.append((k, v))
                    self.seen[(e, k)] = v
            for k, v in self.tok.items():
                if k[0] == "dma" and k[1] != "cast":
                    self.seen[(e, k)] = max(self.seen.get((e, k), 0), v)
            own = ("eng", e)
            if own in self.tok:
                ws.append((own, self.tok[own]))
                self.seen[(e, own)] = self.tok[own]
            if ws:
                self.q[e].append((ws, None, None))
        self.res = {}

    def emit(self):
        nc = self.nc
        with nc.Block() as block:
            decos = {"pe": block.tensor, "act": block.scalar, "dve": block.vector,
                     "pool": block.gpsimd, "sp": block.sync}
            for eng in ENGS:
                def body(e, eng=eng):
                    for waits, fn, inc in self.q[eng]:
                        if fn is None:
                            for k, v in waits:
                                e.wait_ge(self.sem[k], v)
                            continue
                        for k, v in waits[:-1]:
                            e.wait_ge(self.sem[k], v)
                        att = (self.sem[waits[-1][0]], waits[-1][1]) if waits else None
                        ins = fn(_EngProxy(e, att))
                        ins.then_inc(self.sem[inc[0]], inc[1])
                decos[eng](body)


class _EngProxy:
    def __init__(self, e, att):
        self._e = e
        self._att = att

    def __getattr__(self, name):
        f = getattr(self._e, name)

        def call(*a, **kw):
            ins = f(*a, **kw)
            if self._att is not None:
                ins._wait_ge(self._att[0], self._att[1])
                self._att = None
            return ins
        return call


class Arena:
    def __init__(self, nc, lo, hi):
        self.nc = nc
        self.lo = lo
        self.hi = hi
        self.cur = lo
        self.n = 0

    def alloc(self, name, shape, dt):
        nbytes = int(np.prod(shape[1:])) * _dsize(dt)
        nbytes = (nbytes + 31) // 32 * 32
        assert self.cur + nbytes <= self.hi, (name, self.cur, nbytes, self.hi)
        self.n += 1
        t = self.nc.alloc_sbuf_tensor_at("%s_%d" % (name, self.n), list(shape), dt, offset=self.cur)
        self.cur += nbytes
        return t

    def mark(self):
        return self.cur

    def reset(self, m):
        self.cur = m


class Stream:
    def __init__(self, K, name, slots, units, loader, src_keys):
        self.K = K
        self.name = name
        self.slots = slots
        self.units = units
        self.loader = loader
        self.src_keys = src_keys
        self.issued = 0

    def key(self, i):
        return ("ws", self.name, i % len(self.slots))

    def _issue_upto(self, n):
        while self.issued < min(n, len(self.units)):
            i = self.issued
            tile = self.slots[i % len(self.slots)]
            fn = self.loader(self.units[i], tile)
            self.K.dma("sp", fn, reads=self.src_keys(self.units[i]), writes=[self.key(i)],
                       sem=(self.name, i % len(self.slots)))
            self.issued += 1

    def get(self, i):
        self._issue_upto(i + len(self.slots))
        return self.slots[i % len(self.slots)], self.key(i)


def build(S, NSEQ, plan):
    assert S % 512 == 0
    NB = S // 512
    NCH = S // 128
    nc = bass.Bass("TRN2", target_bir_lowering=False)

    def din(name, shape, dt=F32):
        return nc.dram_tensor(name, list(shape), dt, kind="ExternalInput").ap()

    def dscr(name, shape, dt=BF16):
        return nc.dram_tensor(name, list(shape), dt, kind="Internal").ap()

    x = din("x", [NSEQ, S, D])
    positions = din("positions", [NSEQ, S], I32)
    ffn_norm = din("ffn_norm", [DEPTH, 2, D])
    ffn_w_gate = din("ffn_w_gate", [DEPTH, 2, D, DFF])
    ffn_w_up = din("ffn_w_up", [DEPTH, 2, D, DFF])
    ffn_w_down = din("ffn_w_down", [DEPTH, 2, DFF, D])
    mix_norm = din("mix_norm", [DEPTH, D])
    even_w_in = din("even_w_in", [2, D, 928])
    q_a_norm = din("q_a_norm", [2, 256])
    kv_a_norm = din("kv_a_norm", [2, 128])
    w_uq = din("w_uq", [2, 256, 768])
    w_ukv = din("w_ukv", [2, 128, 1024])
    q_norm = din("q_norm", [2, 96])
    k_norm = din("k_norm", [2, 96])
    pool_w = din("pool_w", [2, 4, 128, 128])
    pool_scale = din("pool_scale", [2, 512])
    even_w_out = din("even_w_out", [2, D, D])
    odd_w_in = din("odd_w_in", [2, D, 2048])
    sg_norm = din("sg_norm", [2, D])
    sg_w = din("sg_w", [2, 4, 128, 128])
    sg_b = din("sg_b", [2, 4, 128])
    odd_w_out = din("odd_w_out", [2, D, D])
    c_ident = din("c_ident", [128, 128])
    c_triu = din("c_triu", [128, 128])
    c_maskneg = din("c_maskneg", [128, 128])
    c_sgn = din("c_sgn", [128, 1])
    c_rotm = din("c_rotm", [128, 128])
    c_invfreq = din("c_invfreq", [128, 1])
    c_invcnt = din("c_invcnt", [128, 64])
    y = nc.dram_tensor("y", [NSEQ, S, D], F32, kind="ExternalOutput").ap()

    need_ffn = sorted({(l, j) for (l, p) in plan for j in ((0,) if p == "ffn1" else (1,) if p == "ffn2" else ())})
    need_even = sorted({l // 2 for (l, p) in plan if p == "mix" and l % 2 == 0})
    need_odd = sorted({l // 2 for (l, p) in plan if p == "mix" and l % 2 == 1})

    s_gu = {lj: dscr("s_gu_%d_%d" % lj, [NFF, 128, 2, 8, 128]) for lj in need_ffn}
    s_d = {lj: dscr("s_d_%d_%d" % lj, [8, 128, NFF, 128]) for lj in need_ffn}
    s_ewin = {i: dscr("s_ewin_%d" % i, [7, 128, 8, 128]) for i in need_even}
    s_ekpe = {i: dscr("s_ekpe_%d" % i, [128, 8, 32]) for i in need_even}
    s_euq = {i: dscr("s_euq_%d" % i, [8, 128, 2, 96]) for i in need_even}
    s_eukv = {i: dscr("s_eukv_%d" % i, [128, 1024]) for i in need_even}
    s_epw = {i: dscr("s_epw_%d" % i, [128, 4, 128]) for i in need_even}
    s_ewoa = {i: dscr("s_ewoa_%d" % i, [512, 1024]) for i in need_even}
    s_ewop = {i: dscr("s_ewop_%d" % i, [8, 128, 4, 128]) for i in need_even}
    s_owu = {i: dscr("s_owu_%d" % i, [8, 128, 8, 128]) for i in need_odd}
    s_owv = {i: dscr("s_owv_%d" % i, [128, 8, 1024]) for i in need_odd}
    s_owo = {i: dscr("s_owo_%d" % i, [8, 128, 8, 128]) for i in need_odd}
    s_cs = dscr("s_cs", [NSEQ, 2, 32, S], F32)

    stack = contextlib.ExitStack()
    with stack:
        K = Sched(nc, stack)
        h_res = nc.alloc_sbuf_tensor_at("h_res", [128, 8, S], F32, offset=SB_BASE)
        ar = Arena(nc, SB_BASE + HRES_BYTES, SB_END)
        ps = [stack.enter_context(nc.psum_tensor("ps%d" % i, [128, 512], F32)) for i in range(8)]

        ones_bf = ar.alloc("ones_bf", [128, 128], BF16)
        ones_f = ar.alloc("ones_f", [128, 128], F32)
        ident = ar.alloc("ident", [128, 128], F32)
        g_ffn = ar.alloc("g_ffn", [128, 8, 8], F32)
        g_mix = ar.alloc("g_mix", [128, 4, 8], F32)
        K.op("dve", lambda e: e.memset(ones_bf[:], 1.0), writes=["ones_bf"])
        K.op("dve", lambda e: e.memset(ones_f[:], 1.0), writes=["ones_f"])
        K.dma("sp", lambda e: e.dma_start(out=ident[:], in_=c_ident[:, :]), writes=["ident"], sem=("c", 0))
        g_raw = ar.alloc("g_raw", [128, 128], F32)
        K.dma("sp", lambda e: e.dma_start(out=g_raw[0:64, :], in_=ffn_norm.rearrange("l j (c p) -> (l j c) p", p=128)),
              writes=["g_raw0"], sem=("c", 1))
        K.dma("sp", lambda e: e.dma_start(out=g_raw[64:96, :], in_=mix_norm.rearrange("l (c p) -> (l c) p", p=128)),
              writes=["g_raw1"], sem=("c", 2))
        K.op("pe", lambda e: e.transpose(out=ps[7][:, 0:96], in_=g_raw[0:96, :], identity=ident[0:96, 0:96]),
             reads=["g_raw0", "g_raw1", "ident"], writes=[("ps", 7)])
        K.op("dve", lambda e: e.tensor_scalar(out=g_ffn[:].rearrange("p a c -> p (a c)"), in0=ps[7][:, 0:64], scalar1=32.0, scalar2=None, op0=ALU.mult),
             reads=[("ps", 7)], writes=["g_ffn"])
        K.op("dve", lambda e: e.tensor_scalar(out=g_mix[:].rearrange("p a c -> p (a c)"), in0=ps[7][:, 64:96], scalar1=32.0, scalar2=None, op0=ALU.mult),
             reads=[("ps", 7)], writes=["g_mix"])
        ar_base = ar.mark()

        def cast_group(grp, items):
            n = len(items)
            for idx, (dst, src) in enumerate(items):
                K.dma("pool", (lambda e, dst=dst, src=src: e.dma_start(out=dst, in_=src)),
                      writes=[("scrp", idx) + tuple(grp)], sem=("cast",) + tuple(grp), token_val=16 * n)
            K.res[("scr",) + tuple(grp)] = [(("dma", "cast") + tuple(grp), 16 * n), {}]

        def cast_ffn(l, j):
            items = []
            for m in range(NFF):
                items.append((s_gu[(l, j)][m, :, 0], ffn_w_gate[l, j, :, m * 128:(m + 1) * 128].rearrange("(k p) n -> p k n", p=128)))
                items.append((s_gu[(l, j)][m, :, 1], ffn_w_up[l, j, :, m * 128:(m + 1) * 128].rearrange("(k p) n -> p k n", p=128)))
            for mo in range(8):
                items.append((s_d[(l, j)][mo], ffn_w_down[l, j, :, mo * 128:(mo + 1) * 128].rearrange("(k p) n -> p k n", p=128)))
            cast_group(("ffn", l, j), items)

        def cast_odd(i):
            items = []
            for fc in range(8):
                items.append((s_owu[i][fc], odd_w_in[i, :, fc * 128:(fc + 1) * 128].rearrange("(k p) n -> p k n", p=128)))
            items.append((s_owv[i][:, :, :], odd_w_in[i, :, 1024:2048].rearrange("(k p) n -> p k n", p=128)))
            for mo in range(8):
                items.append((s_owo[i][mo], odd_w_out[i, :, mo * 128:(mo + 1) * 128].rearrange("(k p) n -> p k n", p=128)))
            cast_group(("odd", i), items)

        def cast_even(i):
            items = []
            cols = [0, 128, 256, 416, 544, 672, 800]
            for t, c0 in enumerate(cols):
                items.append((s_ewin[i][t], even_w_in[i, :, c0:c0 + 128].rearrange("(k p) n -> p k n", p=128)))
            items.append((s_ekpe[i][:, :, :], even_w_in[i, :, 384:416].rearrange("(k p) n -> p k n", p=128)))
            for h in range(8):
                items.append((s_euq[i][h], w_uq[i, :, h * 96:(h + 1) * 96].rearrange("(k p) n -> p k n", p=128)))
            items.append((s_eukv[i][:, :], w_ukv[i, :, :]))
            items.append((s_epw[i][:, :, :], pool_w[i].rearrange("g c d -> c g d")))
            items.append((s_ewoa[i][:, :], even_w_out[i, 0:512, :]))
            for mo in range(8):
                items.append((s_ewop[i][mo], even_w_out[i, 512:1024, mo * 128:(mo + 1) * 128].rearrange("(g p) n -> p g n", p=128)))
            cast_group(("even", i), items)

        done_cast = set()
        for (l, p) in plan:
            if p in ("ffn1", "ffn2"):
                key = ("ffn", l, 0 if p == "ffn1" else 1)
                if key not in done_cast:
                    cast_ffn(l, key[2])
            elif l % 2 == 0:
                key = ("even", l // 2)
                if key not in done_cast:
                    cast_even(l // 2)
            else:
                key = ("odd", l // 2)
                if key not in done_cast:
                    cast_odd(l // 2)
            done_cast.add(key)
        scr_tokens = {k: v for k, v in K.res.items() if k[0] == "scr"}

        def barrier():
            K.barrier_all()
            K.res.update({k: [v[0], {}] for k, v in scr_tokens.items()})
            ar.reset(ar_base)

        def hkey(c, b):
            return ("h", c, b)

        def emit_norm_act(b, sq, nchunk=8, c0=0):
            K.op("act", lambda e: e.activation(out=sq[:, 0:nchunk, :], in_=h_res[:, c0:c0 + nchunk, b * 512:(b + 1) * 512], func=AF.Square),
                 reads=[hkey(c, b) for c in range(c0, c0 + nchunk)], writes=["sq"])

        def emit_norm_rest(b, sq, rstd, hn, hn_key, gain, pstat):
            def mm(e):
                for c in range(8):
                    ins = e.matmul(ps[pstat][:, :], lhsT=ones_bf[:, :], rhs=sq[:, c, :], start=(c == 0), stop=(c == 7))
                return ins
            K.op("pe", mm, reads=["sq", "ones_bf"], writes=[("ps", pstat)])
            K.op("act", lambda e: e.activation(out=rstd[:, :], in_=ps[pstat][:, :], func=AF.Sqrt, bias=float(D * EPS)),
                 reads=[("ps", pstat)], writes=["rstd"])
            K.op("dve", lambda e: e.reciprocal(out=rstd[:, :], in_=rstd[:, :]), reads=["rstd"], writes=["rstd"])
            for c in range(8):
                K.op("dve", lambda e, c=c: e.scalar_tensor_tensor(out=hn[:, c, :], in0=h_res[:, c, b * 512:(b + 1) * 512],
                                                                  scalar=gain[:, c:c + 1], in1=rstd[:, :],
                                                                  op0=ALU.mult, op1=ALU.mult),
                     reads=[hkey(c, b), "rstd", "g_ffn", "g_mix"], writes=[hn_key])

        def load_seq(s):
            barrier()
            xs = [ar.alloc("xs", [128, D], F32) for _ in range(2)]
            for tc in range(NCH):
                sl = tc % 2
                K.dma("sp", lambda e, tc=tc, sl=sl: e.dma_start(out=xs[sl][:], in_=x[s, tc * 128:(tc + 1) * 128, :]),
                      writes=[("xs", sl)], sem=("xs", sl))
                for half in range(2):
                    bank = (tc * 2 + half) % 4

                    def tr(e, tc=tc, sl=sl, half=half, bank=bank):
                        for q in range(4):
                            c = half * 4 + q
                            ins = e.transpose(out=ps[bank][:, q * 128:(q + 1) * 128], in_=xs[sl][:, c * 128:(c + 1) * 128], identity=ident[:, :])
                        return ins
                    K.op("pe", tr, reads=[("xs", sl), "ident"], writes=[("ps", bank)])
                    b = tc // 4
                    eng = "act" if half == 0 else "dve"
                    dst = h_res[:, half * 4:half * 4 + 4, tc * 128:(tc + 1) * 128]
                    src = ps[bank][:, :].rearrange("p (q n) -> p q n", q=4)
                    if eng == "act":
                        K.op("act", lambda e, dst=dst, src=src: e.activation(out=dst, in_=src, func=AF.Copy),
                             reads=[("ps", bank)], writes=[hkey(c, b) for c in range(half * 4, half * 4 + 4)])
                    else:
                        K.op("dve", lambda e, dst=dst, src=src: e.tensor_copy(out=dst, in_=src),
                             reads=[("ps", bank)], writes=[hkey(c, b) for c in range(half * 4, half * 4 + 4)])

        def store_seq(s):
            barrier()
            xs = [ar.alloc("xs", [128, D], F32) for _ in range(2)]
            for tc in range(NCH):
                sl = tc % 2
                b = tc // 4
                for half in range(2):
                    bank = (tc * 2 + half) % 4

                    def tr(e, tc=tc, half=half, bank=bank):
                        for q in range(4):
                            c = half * 4 + q
                            ins = e.transpose(out=ps[bank][:, q * 128:(q + 1) * 128], in_=h_res[:, c, tc * 128:(tc + 1) * 128], identity=ident[:, :])
                        return ins
                    K.op("pe", tr, reads=[hkey(c, b) for c in range(half * 4, half * 4 + 4)] + ["ident"], writes=[("ps", bank)])
                    dst = xs[sl][:, half * 512:(half + 1) * 512]
                    if half == 0:
                        K.op("act", lambda e, dst=dst, bank=bank: e.activation(out=dst, in_=ps[bank][:, :], func=AF.Copy),
                             reads=[("ps", bank)], writes=[("xs", sl, half)])
                    else:
                        K.op("dve", lambda e, dst=dst, bank=bank: e.tensor_copy(out=dst, in_=ps[bank][:, :]),
                             reads=[("ps", bank)], writes=[("xs", sl, half)])
                K.dma("sp", lambda e, tc=tc, sl=sl: e.dma_start(out=y[s, tc * 128:(tc + 1) * 128, :], in_=xs[sl][:]),
                      reads=[("xs", sl, 0), ("xs", sl, 1)], writes=[("yout", tc)], sem=("ys", sl))

        def ffn_phase(s, l, j):
            barrier()
            hn = [ar.alloc("hn", [128, 8, 512], BF16) for _ in range(2)]
            sq = ar.alloc("sq", [128, 8, 512], BF16)
            rstd = ar.alloc("rstd", [128, 512], F32)
            sg = [ar.alloc("sg", [128, 512], F32) for _ in range(2)]
            act = ar.alloc("act", [128, NFF, 512], BF16)
            wgu = [ar.alloc("wgu", [128, 2, 8, 128], BF16) for _ in range(3)]
            wd = [ar.alloc("wd", [128, NFF, 128], BF16) for _ in range(2)]
            gain = g_ffn[:, l * 2 + j, :]
            scr = ("scr", "ffn", l, j)
            gu_units = [(b, m) for b in range(NB) for m in range(NFF)]
            d_units = [(b, mo) for b in range(NB) for mo in range(8)]
            st_gu = Stream(K, "wgu", wgu, gu_units,
                           lambda u, t: (lambda e: e.dma_start(out=t[:], in_=s_gu[(l, j)][u[1]])),
                           lambda u: [scr])
            st_d = Stream(K, "wd", wd, d_units,
                          lambda u, t: (lambda e: e.dma_start(out=t[:], in_=s_d[(l, j)][u[1]])),
                          lambda u: [scr])
            PG, PU, PD, PST = (0, 1), (2, 3), (4, 5), 6

            emit_norm_act(0, sq)
            emit_norm_rest(0, sq, rstd, hn[0], ("hn", 0), gain, PST)
            for b in range(NB):
                hs = b % 2
                for m in range(NFF):
                    wt, wkey = st_gu.get(b * NFF + m)
                    gb, ub = PG[m % 2], PU[m % 2]

                    def mmg(e, wt=wt, gb=gb, hs=hs):
                        for k in range(8):
                            ins = e.matmul(ps[gb][:, :], lhsT=wt[:, 0, k, :], rhs=hn[hs][:, k, :], start=(k == 0), stop=(k == 7))
                        return ins

                    def mmu(e, wt=wt, ub=ub, hs=hs):
                        for k in range(8):
                            ins = e.matmul(ps[ub][:, :], lhsT=wt[:, 1, k, :], rhs=hn[hs][:, k, :], start=(k == 0), stop=(k == 7))
                        return ins
                    K.op("pe", mmg, reads=[wkey, ("hn", hs)], writes=[("ps", gb)])
                    K.op("pe", mmu, reads=[wkey, ("hn", hs)], writes=[("ps", ub)])
                    K.op("act", lambda e, gb=gb, m=m: e.activation(out=sg[m % 2][:, :], in_=ps[gb][:, :], func=AF.Silu),
                         reads=[("ps", gb)], writes=[("sg", m % 2)])
                    K.op("dve", lambda e, ub=ub, m=m: e.tensor_tensor(out=act[:, m, :], in0=ps[ub][:, :], in1=sg[m % 2][:, :], op=ALU.mult),
                         reads=[("ps", ub), ("sg", m % 2)], writes=[("act", m)])
                    if b + 1 < NB and m == 4:
                        emit_norm_act(b + 1, sq)
                    if b + 1 < NB and m == 12:
                        emit_norm_rest(b + 1, sq, rstd, hn[1 - hs], ("hn", 1 - hs), gain, PST)
                for mo in range(8):
                    wt, wkey = st_d.get(b * 8 + mo)
                    db = PD[mo % 2]

                    def mmd(e, wt=wt, db=db):
                        for k in range(NFF):
                            ins = e.matmul(ps[db][:, :], lhsT=wt[:, k, :], rhs=act[:, k, :], start=(k == 0), stop=(k == NFF - 1))
                        return ins
                    K.op("pe", mmd, reads=[wkey] + [("act", k) for k in range(NFF)], writes=[("ps", db)])
                    hv = h_res[:, mo, b * 512:(b + 1) * 512]
                    K.op("dve", lambda e, db=db, hv=hv: e.scalar_tensor_tensor(out=hv, in0=ps[db][:, :], scalar=0.5, in1=hv,
                                                                               op0=ALU.mult, op1=ALU.add),
                         reads=[("ps", db), hkey(mo, b)], writes=[hkey(mo, b)])

        def odd_phase(s, l):
            i = l // 2
            barrier()
            scr = ("scr", "odd", i)
            wv = ar.alloc("wv", [128, 8, 1024], BF16)
            hn = ar.alloc("hn", [128, 8, 512], BF16)
            sqg = ar.alloc("sqg", [128, 8, 512], BF16)
            rstd = ar.alloc("rstd", [128, 512], F32)
            v32 = ar.alloc("v32", [128, 1024], F32)
            ss = ar.alloc("ss", [128, 8], F32)
            vn = ar.alloc("vn", [128, 4, 1024], BF16)
            uf = [ar.alloc("uf", [128, 512], F32) for _ in range(2)]
            wu = [ar.alloc("wu", [128, 8, 128], BF16) for _ in range(3)]
            wo = [ar.alloc("wo", [128, 8, 128], BF16) for _ in range(3)]
            sgwT = ar.alloc("sgwT", [128, 4, 128], BF16)
            sgw_raw = ar.alloc("sgw_raw", [128, 4, 128], F32)
            sgb = ar.alloc("sgb", [128, 512], F32)
            gsg = ar.alloc("gsg", [128, 1024], F32)
            triu = ar.alloc("triu", [128, 128], F32)
            K.dma("sp", lambda e: e.dma_start(out=wv[:], in_=s_owv[i][:, :, :]), reads=[scr], writes=["wv"], sem=("o", 0))
            K.dma("sp", lambda e: e.dma_start(out=sgw_raw[:], in_=sg_w[i].rearrange("g t s -> t g s")), writes=["sgw_raw"], sem=("o", 1))
            K.dma("sp", lambda e: e.dma_start(out=triu[:], in_=c_triu[:, :]), writes=["triu"], sem=("o", 2))
            K.dma("sp", lambda e: e.dma_start(out=sgb[0:1, :], in_=sg_b[i:i + 1].rearrange("o g t -> o (g t)")), writes=["sgb"], sem=("o", 3))
            K.dma("sp", lambda e: e.dma_start(out=gsg[:], in_=sg_norm[i:i + 1, :].broadcast_to([128, 1024])), writes=["gsg"], sem=("o", 4))
            K.op("dve", lambda e: e.tensor_scalar(out=gsg[:], in0=gsg[:], scalar1=32.0, scalar2=None, op0=ALU.mult), reads=["gsg"], writes=["gsg"])
            for g in range(4):
                K.op("pe", lambda e, g=g: e.transpose(out=ps[g % 2][:, 0:128], in_=sgw_raw[:, g, :], identity=ident[:, :]),
                     reads=["sgw_raw", "ident"], writes=[("ps", g % 2)])
                K.op("dve", lambda e, g=g: e.tensor_tensor(out=sgwT[:, g, :], in0=ps[g % 2][:, 0:128], in1=triu[:, :], op=ALU.mult),
                     reads=[("ps", g % 2), "triu"], writes=["sgwT"])
            u_units = [(b, fc) for b in range(NB) for fc in range(8)]
            st_u = Stream(K, "wu", wu, u_units, lambda u, t: (lambda e: e.dma_start(out=t[:], in_=s_owu[i][u[1]])), lambda u: [scr])
            st_o = Stream(K, "wo", wo, u_units, lambda u, t: (lambda e: e.dma_start(out=t[:], in_=s_owo[i][u[1]])), lambda u: [scr])
            gain = g_mix[:, l, :]
            for b in range(NB):
                emit_norm_act(b, sqg)
                emit_norm_rest(b, sqg, rstd, hn, "hn", gain, 6)
                for ch in range(4):
                    for half in range(2):
                        bank = half

                        def mmv(e, ch=ch, half=half, bank=bank):
                            for k in range(8):
                                ins = e.matmul(ps[bank][:, :], lhsT=hn[:, k, ch * 128:(ch + 1) * 128], rhs=wv[:, k, half * 512:(half + 1) * 512],
                                               start=(k == 0), stop=(k == 7))
                            return ins
                        K.op("pe", mmv, reads=["hn", "wv"], writes=[("ps", bank)])
                        K.op("act", lambda e, half=half, bank=bank: e.activation(out=v32[:, half * 512:(half + 1) * 512], in_=ps[bank][:, :], func=AF.Gelu_apprx_tanh),
                             reads=[("ps", bank)], writes=[("v32", half)])
                    K.op("act", lambda e, ch=ch: e.activation(out=vn[:, ch, :], in_=v32[:, :], func=AF.Square, accum_out=ss[:, ch:ch + 1]),
                         reads=[("v32", 0), ("v32", 1)], writes=[("vn", ch), ("ss", ch)])
                    K.op("act", lambda e, ch=ch: e.activation(out=ss[:, ch:ch + 1], in_=ss[:, ch:ch + 1], func=AF.Sqrt, bias=float(D * EPS)),
                         reads=[("ss", ch)], writes=[("ss", ch)])
                    K.op("dve", lambda e, ch=ch: e.reciprocal(out=ss[:, ch:ch + 1], in_=ss[:, ch:ch + 1]), reads=[("ss", ch)], writes=[("ss", ch)])
                    K.op("dve", lambda e, ch=ch: e.scalar_tensor_tensor(out=vn[:, ch, :], in0=v32[:, :], scalar=ss[:, ch:ch + 1], in1=gsg[:, :],
                                                                        op0=ALU.mult, op1=ALU.mult),
                         reads=[("v32", 0), ("v32", 1), ("ss", ch), "gsg"], writes=[("vn", ch)])
                for fc in range(8):
                    g = fc // 2
                    wt, wkey = st_u.get(b * 8 + fc)
                    ub = 2 + fc % 2
                    mb = 4 + fc % 2

                    def mmu(e, wt=wt, ub=ub):
                        for k in range(8):
                            ins = e.matmul(ps[ub][:, :], lhsT=wt[:, k, :], rhs=hn[:, k, :], start=(k == 0), stop=(k == 7))
                        return ins
                    K.op("pe", mmu, reads=[wkey, "hn"], writes=[("ps", ub)])
                    K.op("act", lambda e, ub=ub, fc=fc: e.activation(out=uf[fc % 2][:, :], in_=ps[ub][:, :], func=AF.Gelu_apprx_tanh),
                         reads=[("ps", ub)], writes=[("uf", fc % 2)])

                    def mmx(e, fc=fc, g=g, mb=mb):
                        for ch in range(4):
                            e.matmul(ps[mb][:, ch * 128:(ch + 1) * 128], lhsT=vn[:, ch, fc * 128:(fc + 1) * 128], rhs=sgwT[:, g, :], start=True, stop=False)
                            ins = e.matmul(ps[mb][:, ch * 128:(ch + 1) * 128], lhsT=ones_f[0:1, :], rhs=sgb[0:1, g * 128:(g + 1) * 128], start=False, stop=True)
                        return ins
                    K.op("pe", mmx, reads=[("vn", c) for c in range(4)] + ["sgwT", "sgb", "ones_f"], writes=[("ps", mb)])
                    K.op("dve", lambda e, fc=fc, mb=mb: e.tensor_tensor(out=sqg[:, fc, :], in0=ps[mb][:, :], in1=uf[fc % 2][:, :], op=ALU.mult),
                         reads=[("ps", mb), ("uf", fc % 2)], writes=["sq"])
                for mo in range(8):
                    wt, wkey = st_o.get(b * 8 + mo)
                    db = 6 + mo % 2

                    def mmo(e, wt=wt, db=db):
                        for k in range(8):
                            ins = e.matmul(ps[db][:, :], lhsT=wt[:, k, :], rhs=sqg[:, k, :], start=(k == 0), stop=(k == 7))
                        return ins
                    K.op("pe", mmo, reads=[wkey, "sq"], writes=[("ps", db)])
                    hv = h_res[:, mo, b * 512:(b + 1) * 512]
                    K.op("dve", lambda e, db=db, hv=hv: e.tensor_tensor(out=hv, in0=ps[db][:, :], in1=hv, op=ALU.add),
                         reads=[("ps", db), hkey(mo, b)], writes=[hkey(mo, b)])

        s_kpe = dscr("s_kpe", [32, S], F32)
        s_rl = dscr("s_rl", [2, 512], F32)
        TWO_PI = 2.0 * np.pi
        C1 = 6.28125
        C2 = TWO_PI - C1
        SC = 1.0 - 1e-6

        def tables_phase(s):
            barrier()
            invf = ar.alloc("invf", [128, 1], F32)
            pi_ = ar.alloc("pi_", [128, 512], I32)
            pf = ar.alloc("pf", [128, 512], F32)
            av = ar.alloc("av", [128, 512], F32)
            tf = ar.alloc("tf", [128, 512], F32)
            ki = ar.alloc("ki", [128, 512], I32)
            kf = ar.alloc("kf", [128, 512], F32)
            rr = ar.alloc("rr", [128, 512], F32)
            outt = [ar.alloc("outt", [128, 512], F32) for _ in range(2)]
            R = slice(64, 96)
            K.dma("sp", lambda e: e.dma_start(out=invf[:], in_=c_invfreq[:, :]), writes=["invf"], sem=("t", 0))
            sgn = ar.alloc("sgn", [128, 1], F32)
            hpi = ar.alloc("hpi", [128, 1], F32)
            K.dma("sp", lambda e: e.dma_start(out=sgn[:], in_=c_sgn[:, :]), writes=["sgn"], sem=("t", 4))
            K.op("dve", lambda e: e.memset(hpi[:], float(np.pi / 2 * SC)), writes=["hpi"])
            for b in range(NB):
                bl = slice(b * 512, (b + 1) * 512)
                K.dma("sp", lambda e, bl=bl: e.dma_start(out=pi_[R, :], in_=positions[s:s + 1, bl].broadcast_to([32, 512])), writes=["pi"], sem=("t", 1))
                K.op("dve", lambda e: e.tensor_copy(out=pf[R, :], in_=pi_[R, :]), reads=["pi"], writes=["pf"])
                K.op("dve", lambda e: e.tensor_scalar(out=av[R, :], in0=pf[R, :], scalar1=invf[R, 0:1], scalar2=None, op0=ALU.mult), reads=["pf", "invf"], writes=["av"])
                for which in range(2):
                    off = 0.25 if which == 0 else 0.0
                    K.op("dve", lambda e, off=off: e.tensor_scalar(out=ki[R, :], in0=av[R, :], scalar1=float(1.0 / TWO_PI), scalar2=float(off), op0=ALU.mult, op1=ALU.add),
                         reads=["av"], writes=["ki"])
                    K.op("dve", lambda e: e.tensor_copy(out=kf[R, :], in_=ki[R, :]), reads=["ki"], writes=["kf"])
                    K.op("dve", lambda e: e.scalar_tensor_tensor(out=rr[R, :], in0=kf[R, :], scalar=float(-C1), in1=av[R, :], op0=ALU.mult, op1=ALU.add),
                         reads=["kf", "av"], writes=["rr"])
                    K.op("dve", lambda e: e.scalar_tensor_tensor(out=rr[R, :], in0=kf[R, :], scalar=float(-C2), in1=rr[R, :], op0=ALU.mult, op1=ALU.add),
                         reads=["kf", "rr"], writes=["rr"])
                    if which == 0:
                        K.op("act", lambda e: e.activation(out=outt[0][R, :], in_=rr[R, :], func=AF.Sin, scale=float(SC), bias=hpi[R, 0:1]),
                             reads=["rr", "hpi"], writes=[("outt", 0)])
                    else:
                        K.op("act", lambda e: e.activation(out=outt[1][R, :], in_=rr[R, :], func=AF.Sin, scale=sgn[R, 0:1]),
                             reads=["rr", "sgn"], writes=[("outt", 1)])
                    K.dma("sp", lambda e, which=which, bl=bl: e.dma_start(out=s_cs[s, which, :, bl], in_=outt[which][R, :]),
                          reads=[("outt", which)], writes=[("cs", b, which)], sem=("t", 2 + which))

        def even_phase(s, l):
            i = l // 2
            barrier()
            scr = ("scr", "even", i)
            R = slice(64, 96)
            cqn = ar.alloc("cqn", [128, 2, S], BF16)
            ckvn = ar.alloc("ckvn", [128, S], BF16)
            gq = ar.alloc("gq", [128, 2], F32)
            gkv = ar.alloc("gkv", [128, 1], F32)
            qg = ar.alloc("qg", [128, 1], F32)
            kg = ar.alloc("kg", [128, 1], F32)
            pscale = ar.alloc("pscale", [128, 4], F32)
            K.dma("sp", lambda e: e.dma_start(out=gq[:], in_=q_a_norm[i:i + 1, :].rearrange("o (c p) -> p (o c)", p=128)), writes=["gq"], sem=("e", 0))
            K.dma("sp", lambda e: e.dma_start(out=gkv[:], in_=kv_a_norm[i:i + 1, :].rearrange("o p -> p o")), writes=["gkv"], sem=("e", 1))
            K.dma("sp", lambda e: e.dma_start(out=qg[0:96, :], in_=q_norm[i:i + 1, :].rearrange("o p -> p o")), writes=["qg"], sem=("e", 2))
            K.dma("sp", lambda e: e.dma_start(out=kg[0:96, :], in_=k_norm[i:i + 1, :].rearrange("o p -> p o")), writes=["kg"], sem=("e", 3))
            K.dma("sp", lambda e: e.dma_start(out=pscale[:], in_=pool_scale[i:i + 1, :].rearrange("o (g p) -> p (o g)", p=128)), writes=["pscale"], sem=("e", 4))
            K.op("dve", lambda e: e.tensor_scalar(out=gq[:], in0=gq[:], scalar1=16.0, scalar2=None, op0=ALU.mult), reads=["gq"], writes=["gq"])
            K.op("dve", lambda e: e.tensor_scalar(out=gkv[:], in0=gkv[:], scalar1=float(np.sqrt(128.0)), scalar2=None, op0=ALU.mult), reads=["gkv"], writes=["gkv"])
            K.op("dve", lambda e: e.tensor_scalar(out=kg[0:96, :], in0=kg[0:96, :], scalar1=float(np.sqrt(96.0)), scalar2=None, op0=ALU.mult), reads=["kg"], writes=["kg"])
            mA = ar.mark()
            hn = ar.alloc("hn", [128, 8, 512], BF16)
            sq = ar.alloc("sq", [128, 8, 512], BF16)
            rstd = ar.alloc("rstd", [128, 512], F32)
            rstc = ar.alloc("rstc", [128, 512], F32)
            win = [ar.alloc("win", [128, 8, 128], BF16) for _ in range(3)]
            wkpe = ar.alloc("wkpe", [128, 8, 32], BF16)
            pbuf = ar.alloc("pbuf", [128, 528], F32)
            t1 = ar.alloc("t1", [128, 528], F32)
            t2 = ar.alloc("t2", [128, 528], F32)
            halo = ar.alloc("halo", [128, 4, 16], F32)
            pooled = [ar.alloc("pooled", [128, 512], BF16) for _ in range(2)]
            pfix = ar.alloc("pfix", [128, 16], F32)
            pout = ar.alloc("pout", [128, 4, 512], BF16)
            pw = ar.alloc("pw", [128, 4, 128], BF16)
            invc = ar.alloc("invc", [128, 64], F32)
            wop = [ar.alloc("wop", [128, 4, 128], BF16) for _ in range(3)]
            kst = ar.alloc("kst", [128, 512], F32)
            K.dma("sp", lambda e: e.dma_start(out=wkpe[:], in_=s_ekpe[i][:, :, :]), reads=[scr], writes=["wkpe"], sem=("e", 5))
            K.dma("sp", lambda e: e.dma_start(out=pw[:], in_=s_epw[i][:, :, :]), reads=[scr], writes=["pw"], sem=("e", 6))
            K.dma("sp", lambda e: e.dma_start(out=invc[:], in_=c_invcnt[:, :]), writes=["invc"], sem=("e", 7))
            K.op("dve", lambda e: e.memset(halo[:], 0.0), writes=["halo"])
            in_units = [(b, t) for b in range(NB) for t in range(7)]
            st_in = Stream(K, "win", win, in_units, lambda u, t: (lambda e: e.dma_start(out=t[:], in_=s_ewin[i][u[1]])), lambda u: [scr])
            op_units = [(b, mo) for b in range(NB) for mo in range(8)]
            st_op = Stream(K, "wop", wop, op_units, lambda u, t: (lambda e: e.dma_start(out=t[:], in_=s_ewop[i][u[1]])), lambda u: [scr])
            gain = g_mix[:, l, :]
            WIN = (2, 4, 8, 16)
            for b in range(NB):
                bl = slice(b * 512, (b + 1) * 512)
                emit_norm_act(b, sq)
                emit_norm_rest(b, sq, rstd, hn, "hn", gain, 6)
                banks = [0, 1, 2, 4, 5, 4, 5]
                for t in range(7):
                    wt, wkey = st_in.get(b * 7 + t)
                    bk = banks[t]

                    def mmi(e, wt=wt, bk=bk):
                        for k in range(8):
                            ins = e.matmul(ps[bk][:, :], lhsT=wt[:, k, :], rhs=hn[:, k, :], start=(k == 0), stop=(k == 7))
                        return ins
                    K.op("pe", mmi, reads=[wkey, "hn"], writes=[("ps", bk)])
                    if t == 1:
                        for c in range(2):
                            K.op("act", lambda e, c=c: e.activation(out=sq[:, c, :], in_=ps[c][:, :], func=AF.Square), reads=[("ps", c)], writes=["sq"])

                        def mms(e):
                            e.matmul(ps[6][:, :], lhsT=ones_bf[:, :], rhs=sq[:, 0, :], start=True, stop=False)
                            return e.matmul(ps[6][:, :], lhsT=ones_bf[:, :], rhs=sq[:, 1, :], start=False, stop=True)
                        K.op("pe", mms, reads=["sq", "ones_bf"], writes=[("ps", 6)])
                        K.op("act", lambda e: e.activation(out=rstc[:, :], in_=ps[6][:, :], func=AF.Sqrt, bias=float(256 * EPS)), reads=[("ps", 6)], writes=["rstc"])
                        K.op("dve", lambda e: e.reciprocal(out=rstc[:, :], in_=rstc[:, :]), reads=["rstc"], writes=["rstc"])
                        for c in range(2):
                            K.op("dve", lambda e, c=c, bl=bl: e.scalar_tensor_tensor(out=cqn[:, c, bl], in0=ps[c][:, :], scalar=gq[:, c:c + 1], in1=rstc[:, :],
                                                                                     op0=ALU.mult, op1=ALU.mult),
                                 reads=[("ps", c), "gq", "rstc"], writes=[("cqn", b)])
                    if t == 2:
                        K.op("act", lambda e: e.activation(out=sq[:, 2, :], in_=ps[2][:, :], func=AF.Square), reads=[("ps", 2)], writes=["sq"])
                        K.op("pe", lambda e: e.matmul(ps[6][:, :], lhsT=ones_bf[:, :], rhs=sq[:, 2, :], start=True, stop=True), reads=["sq", "ones_bf"], writes=[("ps", 6)])
                        K.op("act", lambda e: e.activation(out=rstc[:, :], in_=ps[6][:, :], func=AF.Sqrt, bias=float(128 * EPS)), reads=[("ps", 6)], writes=["rstc"])
                        K.op("dve", lambda e: e.reciprocal(out=rstc[:, :], in_=rstc[:, :]), reads=["rstc"], writes=["rstc"])
                        K.op("dve", lambda e, bl=bl: e.scalar_tensor_tensor(out=ckvn[:, bl], in0=ps[2][:, :], scalar=gkv[:, 0:1], in1=rstc[:, :], op0=ALU.mult, op1=ALU.mult),
                             reads=[("ps", 2), "gkv", "rstc"], writes=[("ckvn", b)])

                        def mmk(e):
                            for k in range(8):
                                ins = e.matmul(ps[3][R, :], lhsT=wkpe[:, k, :], rhs=hn[:, k, :], start=(k == 0), stop=(k == 7))
                            return ins
                        K.op("pe", mmk, reads=["wkpe", "hn"], writes=[("ps", 3)])
                        K.op("act", lambda e: e.activation(out=kst[R, :], in_=ps[3][R, :], func=AF.Copy), reads=[("ps", 3)], writes=["kst"])
                        K.dma("sp", lambda e, bl=bl: e.dma_start(out=s_kpe[:, bl], in_=kst[R, :]), reads=["kst"], writes=[("kpe", b)], sem=("e", 8))
                    if t >= 3:
                        g = t - 3
                        w = WIN[g]
                        K.op("act", lambda e, bk=bk: e.activation(out=pbuf[:, 16:528], in_=ps[bk][:, :], func=AF.Copy), reads=[("ps", bk)], writes=["pbuf"])
                        K.op("dve", lambda e, g=g: e.tensor_copy(out=pbuf[:, 0:16], in_=halo[:, g, :]), reads=["halo"], writes=["pbuf"])
                        K.op("dve", lambda e: e.tensor_tensor(out=t1[:, 1:528], in0=pbuf[:, 1:528], in1=pbuf[:, 0:527], op=ALU.add), reads=["pbuf"], writes=["t1"])
                        fin = t1
                        if g >= 1:
                            K.op("dve", lambda e: e.tensor_tensor(out=t2[:, 3:528], in0=t1[:, 3:528], in1=t1[:, 1:526], op=ALU.add), reads=["t1"], writes=["t2"])
                            fin = t2
                        if g >= 2:
                            K.op("dve", lambda e: e.tensor_tensor(out=t1[:, 7:528], in0=t2[:, 7:528], in1=t2[:, 3:524], op=ALU.add), reads=["t2"], writes=["t1"])
                            fin = t1
                        if g >= 3:
                            K.op("dve", lambda e: e.tensor_tensor(out=t2[:, 15:528], in0=t1[:, 15:528], in1=t1[:, 7:520], op=ALU.add), reads=["t1"], writes=["t2"])
                            fin = t2
                        fkey = "t1" if fin is t1 else "t2"
                        pl = pooled[g % 2]
                        K.op("dve", lambda e, fin=fin, pl=pl, w=w: e.scalar_tensor_tensor(out=pl[:, :], in0=fin[:, 16:528], scalar=float(1.0 / w), in1=pbuf[:, 16:528],
                                                                                           op0=ALU.mult, op1=ALU.subtract),
                             reads=[fkey, "pbuf"], writes=[("pooled", g % 2)])
                        if b == 0:
                            K.op("dve", lambda e, fin=fin, g=g: e.tensor_tensor(out=pfix[:, :], in0=fin[:, 16:32], in1=invc[:, g * 16:(g + 1) * 16], op=ALU.mult),
                                 reads=[fkey, "invc"], writes=["pfix"])
                            K.op("dve", lambda e, pl=pl: e.tensor_tensor(out=pl[:, 0:16], in0=pfix[:, :], in1=pbuf[:, 16:32], op=ALU.subtract),
                                 reads=["pfix", "pbuf", ("pooled", g % 2)], writes=[("pooled", g % 2)])
                        K.op("dve", lambda e, g=g: e.tensor_copy(out=halo[:, g, :], in_=pbuf[:, 512:528]), reads=["pbuf"], writes=["halo"])
                        K.op("pe", lambda e, g=g, pl=pl: e.matmul(ps[7][:, :], lhsT=pw[:, g, :], rhs=pl[:, :], start=True, stop=True),
                             reads=["pw", ("pooled", g % 2)], writes=[("ps", 7)])
                        K.op("act", lambda e, g=g: e.activation(out=pout[:, g, :], in_=ps[7][:, :], func=AF.Copy, scale=pscale[:, g:g + 1]),
                             reads=[("ps", 7), "pscale"], writes=[("pout", g)])
                for mo in range(8):
                    wt, wkey = st_op.get(b * 8 + mo)
                    db = mo % 2

                    def mmo(e, wt=wt, db=db):
                        for g in range(4):
                            ins = e.matmul(ps[db][:, :], lhsT=wt[:, g, :], rhs=pout[:, g, :], start=(g == 0), stop=(g == 3))
                        return ins
                    K.op("pe", mmo, reads=[wkey] + [("pout", g) for g in range(4)], writes=[("ps", db)])
                    hv = h_res[:, mo, bl]
                    K.op("dve", lambda e, db=db, hv=hv: e.tensor_tensor(out=hv, in0=ps[db][:, :], in1=hv, op=ALU.add),
                         reads=[("ps", db), hkey(mo, b)], writes=[hkey(mo, b)])

            K.barrier_all()
            K.res.update({k: [v[0], {}] for k, v in scr_tokens.items()})
            ar.reset(mA)
            kh = [ar.alloc("kh", [128, S], BF16) for _ in range(2)]
            vx = ar.alloc("vx", [128, NCH, 65], BF16)
            wuq = [ar.alloc("wuq", [128, 2, 96], BF16) for _ in range(2)]
            wukv = ar.alloc("wukv", [128, 1024], BF16)
            woa = [ar.alloc("woa", [128, 1024], BF16) for _ in range(2)]
            cosT = ar.alloc("cosT", [128, 512], F32)
            sinT = ar.alloc("sinT", [128, 512], F32)
            xk = ar.alloc("xk", [128, 512], F32)
            sqk = ar.alloc("sqk", [128, 512], BF16)
            rstb = ar.alloc("rstb", [128, 512], F32)
            xr = ar.alloc("xr", [128, 512], F32)
            tmp = ar.alloc("tmp", [128, 512], F32)
            qh = [ar.alloc("qh", [128, 512], BF16) for _ in range(2)]
            pT = [ar.alloc("pT", [128, 512], BF16) for _ in range(3)]
            bc = ar.alloc("bc", [128, 512], F32)
            rl = bc
            ao = ar.alloc("ao", [128, 512], BF16)
            rotm = ar.alloc("rotm", [128, 128], F32)
            trb = ar.alloc("trb", [128, 128], BF16)
            idb = ar.alloc("idb", [128, 128], BF16)
            K.dma("sp", lambda e: e.dma_start(out=wukv[:], in_=s_eukv[i][:, :]), reads=[scr], writes=["wukv"], sem=("e", 9))
            K.dma("sp", lambda e: e.dma_start(out=rotm[:], in_=c_rotm[:, :]), writes=["rotm"], sem=("e", 10))
            K.dma("pool", lambda e: e.dma_start(out=trb[:], in_=c_maskneg[:, :]), writes=["trb"], sem=("e", 11))
            K.dma("pool", lambda e: e.dma_start(out=idb[:], in_=c_ident[:, :]), writes=["idb"], sem=("e", 15))
            K.op("dve", lambda e: e.memset(vx[:, :, 64:65], 1.0), writes=["vx1"])
            eps96 = ar.alloc("eps96", [128, 1], F32)
            K.op("dve", lambda e: e.memset(eps96[:], float(96 * EPS)), writes=["eps96"])
            PQ, PST, PDR, PSC, PO = 0, 1, (2, 3), (4, 5), (6, 7)

            def g_norm_rope(src_nope, src_rope, rope_key, gcol, dst, dst_key, j):
                bl = slice(j * 512, (j + 1) * 512)
                K.dma("sp", lambda e, bl=bl: e.dma_start(out=cosT[R, :], in_=s_cs[s, 0, :, bl]), reads=[("cs", j, 0)], writes=["cosT"], sem=("e", 12))
                K.dma("sp", lambda e, bl=bl: e.dma_start(out=sinT[R, :], in_=s_cs[s, 1, :, bl]), reads=[("cs", j, 1)], writes=["sinT"], sem=("e", 13))
                yield
                if src_rope is None:
                    K.op("act", lambda e: e.activation(out=sqk[0:96, :], in_=src_nope[0:96, :], func=AF.Square), reads=[("ps", PQ)], writes=["sqk"])
                    rsrc = src_nope
                else:
                    K.op("act", lambda e: e.activation(out=sqk[0:64, :], in_=src_nope[0:64, :], func=AF.Square), reads=[("ps", PQ)], writes=["sqk"])
                    K.op("act", lambda e: e.activation(out=sqk[R, :], in_=src_rope[R, :], func=AF.Square), reads=[rope_key, "sqk"], writes=["sqk"])
                    rsrc = src_rope
                yield
                K.op("pe", lambda e: e.matmul(ps[PST][0:96, :], lhsT=ones_bf[0:96, 0:96], rhs=sqk[0:96, :], start=True, stop=True), reads=["sqk", "ones_bf"], writes=[("ps", PST)])
                yield
                K.op("act", lambda e: e.activation(out=rstb[0:96, :], in_=ps[PST][0:96, :], func=AF.Ln, bias=eps96[0:96, 0:1]), reads=[("ps", PST), "eps96"], writes=["rstb"])
                yield
                K.op("act", lambda e: e.activation(out=rstb[0:96, :], in_=rstb[0:96, :], func=AF.Exp, scale=-0.5), reads=["rstb"], writes=["rstb"])
                yield
                K.op("dve", lambda e: e.scalar_tensor_tensor(out=dst[0:64, :], in0=src_nope[0:64, :], scalar=gcol[0:64, 0:1], in1=rstb[0:64, :], op0=ALU.mult, op1=ALU.mult),
                     reads=[("ps", PQ), "rstb", "qg", "kg"], writes=[dst_key])
                K.op("dve", lambda e: e.scalar_tensor_tensor(out=xr[R, :], in0=rsrc[R, :], scalar=gcol[R, 0:1], in1=rstb[R, :], op0=ALU.mult, op1=ALU.mult),
                     reads=[("ps", PQ), rope_key, "rstb", "qg", "kg"], writes=["xr"])
                yield
                K.op("pe", lambda e: e.matmul(ps[PST][R, :], lhsT=rotm[R, 0:32], rhs=xr[R, :], start=True, stop=True), reads=["xr", "rotm"], writes=[("ps", PST)])
                yield
                K.op("dve", lambda e: e.tensor_tensor(out=tmp[R, :], in0=ps[PST][R, :], in1=sinT[R, :], op=ALU.mult), reads=[("ps", PST), "sinT"], writes=["tmp"])
                K.op("dve", lambda e: e.tensor_tensor(out=xr[R, :], in0=xr[R, :], in1=cosT[R, :], op=ALU.mult), reads=["xr", "cosT", ("ps", PST)], writes=["xr"])
                yield
                K.op("dve", lambda e: e.tensor_tensor(out=dst[R, :], in0=xr[R, :], in1=tmp[R, :], op=ALU.add), reads=["xr", "tmp"], writes=[dst_key])
                yield

            def g_kblock(h, kb, j):
                bl = slice(j * 512, (j + 1) * 512)
                K.dma("sp", lambda e, bl=bl: e.dma_start(out=xk[R, :], in_=s_kpe[:, bl]), reads=[("kpe", j)], writes=["xk"], sem=("e", 14))
                K.op("pe", lambda e, h=h, bl=bl: e.matmul(ps[PQ][0:64, :], lhsT=wukv[:, h * 128:h * 128 + 64], rhs=ckvn[:, bl], start=True, stop=True),
                     reads=["wukv", ("ckvn", j)], writes=[("ps", PQ)])
                yield
                yield from g_norm_rope(ps[PQ], xk, "xk", kg, kh[kb][:, bl], ("kh", kb, j), j)

            def g_qbuild(hs, j):
                bl = slice(j * 512, (j + 1) * 512)

                def mmq(e, hs=hs, bl=bl):
                    e.matmul(ps[PQ][0:96, :], lhsT=wuq[hs][:, 0, :], rhs=cqn[:, 0, bl], start=True, stop=False)
                    return e.matmul(ps[PQ][0:96, :], lhsT=wuq[hs][:, 1, :], rhs=cqn[:, 1, bl], start=False, stop=True)
                K.op("pe", mmq, reads=[("wuq", hs), ("cqn", j)], writes=[("ps", PQ)])
                yield
                yield from g_norm_rope(ps[PQ], None, ("ps", PQ), qg, qh[j % 2], ("qh", j % 2), j)

            def g_tail(hs, j):
                ob = PO[j % 2]
                bl = slice(j * 512, (j + 1) * 512)
                K.op("act", lambda e: e.activation(out=rl[64:65, :], in_=ps[ob][64:65, :], func=AF.Copy), reads=[("ps", ob)], writes=["rl"])
                yield
                K.op("pe", lambda e: e.matmul(ps[PDR[0]][0:64, :], lhsT=ones_f[64:65, 0:64], rhs=rl[64:65, :], start=True, stop=True), reads=["rl", "ones_f"], writes=[("ps", PDR[0])])
                yield
                K.op("act", lambda e: e.activation(out=bc[0:64, :], in_=ps[PDR[0]][0:64, :], func=AF.Ln), reads=[("ps", PDR[0])], writes=["bc"])
                yield
                K.op("act", lambda e: e.activation(out=bc[0:64, :], in_=bc[0:64, :], func=AF.Exp, scale=-1.0), reads=["bc"], writes=["bc"])
                yield
                K.op("dve", lambda e: e.tensor_tensor(out=ao[0:64, :], in0=ps[ob][0:64, :], in1=bc[0:64, :], op=ALU.mult), reads=[("ps", ob), "bc"], writes=["ao"])
                yield
                for mo in range(8):
                    db = PDR[mo % 2]
                    K.op("pe", lambda e, mo=mo, db=db: e.matmul(ps[db][:, :], lhsT=woa[hs][0:64, mo * 128:(mo + 1) * 128], rhs=ao[0:64, :], start=True, stop=True),
                         reads=[("woa", hs), "ao"], writes=[("ps", db)])
                    hv = h_res[:, mo, bl]
                    K.op("dve", lambda e, db=db, hv=hv: e.tensor_tensor(out=hv, in0=ps[db][:, :], in1=hv, op=ALU.add),
                         reads=[("ps", db), hkey(mo, j)], writes=[hkey(mo, j)])
                    yield

            class Lanes:
                def __init__(self):
                    self.tail = None
                    self.cur = None
                    self.qpend = None
                    self.kpend = []

                def _step(self, g):
                    try:
                        next(g)
                        return True
                    except StopIteration:
                        return False

                def step_b(self):
                    while True:
                        if self.cur is None:
                            if self.qpend is not None:
                                self.cur, self.qpend = self.qpend, None
                            elif self.kpend:
                                self.cur = self.kpend.pop(0)
                            else:
                                return False
                        if self._step(self.cur):
                            return True
                        self.cur = None

                def step_a(self):
                    if self.tail is not None:
                        if self._step(self.tail):
                            return True
                        self.tail = None
                    return False

                def pump(self, n):
                    for t in range(n):
                        if t % 2 == 0:
                            if not self.step_a():
                                self.step_b()
                        else:
                            if not self.step_b():
                                self.step_a()

                def finish_tail(self):
                    while self.step_a():
                        pass

                def finish_q(self):
                    while self.cur is not None or self.qpend is not None:
                        if self.cur is None:
                            self.cur, self.qpend = self.qpend, None
                        if not self._step(self.cur):
                            self.cur = None

                def finish_all(self):
                    self.finish_tail()
                    while self.step_b():
                        pass

            def drain(g):
                if g is not None:
                    for _ in g:
                        pass

            def load_head_w(h):
                hs = h % 2
                K.dma("sp", lambda e: e.dma_start(out=wuq[hs][:], in_=s_euq[i][h]), reads=[scr], writes=[("wuq", hs)], sem=("wuq", hs))
                K.dma("sp", lambda e: e.dma_start(out=woa[hs][0:64, :], in_=s_ewoa[i][h * 64:(h + 1) * 64, :]), reads=[scr], writes=[("woa", hs)], sem=("woa", hs))

            sc_i = 0
            load_head_w(0)
            for j in range(NB):
                drain(g_kblock(0, 0, j))
            L = Lanes()
            for h in range(8):
                hs = h % 2
                kb = h % 2
                if h + 1 < 8:
                    load_head_w(h + 1)
                for c0 in range(0, NCH, 8):
                    nq = min(8, NCH - c0)

                    def mmv(e, h=h, c0=c0, nq=nq):
                        for q in range(nq):
                            ins = e.matmul(ps[PDR[1]][:, q * 64:(q + 1) * 64], lhsT=ckvn[:, (c0 + q) * 128:(c0 + q + 1) * 128], rhs=wukv[:, h * 128 + 64:(h + 1) * 128],
                                           start=True, stop=True)
                        return ins
                    K.op("pe", mmv, reads=["wukv"] + [("ckvn", (c0 + q) // 4) for q in range(nq)], writes=[("ps", PDR[1])])
                    K.op("act", lambda e, c0=c0, nq=nq: e.activation(out=vx[:, c0:c0 + nq, 0:64], in_=ps[PDR[1]][:, 0:nq * 64].rearrange("p (q n) -> p q n", q=nq), func=AF.Copy),
                         reads=[("ps", PDR[1])], writes=[("vx", c0 // 8)])
                if h + 1 < 8:
                    L.kpend = [g_kblock(h + 1, 1 - kb, jj) for jj in range(NB)]
                L.qpend = g_qbuild(hs, 0)
                L.finish_q()
                for j in range(NB):
                    if j + 1 < NB:
                        L.qpend = g_qbuild(hs, j + 1)
                    qt = qh[j % 2]
                    ob = PO[j % 2]
                    nkt = 4 * (j + 1)
                    for kt in range(nkt):
                        d = kt - 4 * j
                        lo = max(0, d) * 128
                        sb = PSC[sc_i % 2]
                        pt = pT[sc_i % 3]
                        pkey = ("pT", sc_i % 3)
                        sc_i += 1

                        def mms(e, kt=kt, lo=lo, sb=sb, qt=qt, d=d, kb=kb):
                            ins = e.matmul(ps[sb][:, lo:512], lhsT=kh[kb][0:96, kt * 128:(kt + 1) * 128], rhs=qt[0:96, lo:512], start=True, stop=(d < 0))
                            if d >= 0:
                                ins = e.matmul(ps[sb][:, lo:lo + 128], lhsT=idb[:, :], rhs=trb[:, :], start=False, stop=True)
                            return ins
                        K.op("pe", mms, reads=[("kh", kb, kt // 4), ("qh", j % 2), "trb", "idb"], writes=[("ps", sb)])
                        K.op("act", lambda e, lo=lo, sb=sb, pt=pt: e.activation(out=pt[:, lo:512], in_=ps[sb][:, lo:512], func=AF.Exp), reads=[("ps", sb)], writes=[pkey])
                        K.op("pe", lambda e, kt=kt, lo=lo, pt=pt, nkt=nkt, ob=ob: e.matmul(ps[ob][0:65, lo:512], lhsT=vx[:, kt, 0:65], rhs=pt[:, lo:512], start=(kt == 0), stop=(kt == nkt - 1)),
                             reads=[pkey, ("vx", kt // 8), "vx1"], writes=[("ps", ob)])
                        L.pump(3)
                    L.finish_tail()
                    L.finish_q()
                    L.tail = g_tail(hs, j)
                L.finish_all()

        PHASES = {"ffn": ffn_phase, "odd": odd_phase, "even": even_phase}

        for s in range(NSEQ):
            load_seq(s)
            if need_even:
                tables_phase(s)
            for (l, p) in plan:
                if p == "ffn1":
                    ffn_phase(s, l, 0)
                elif p == "ffn2":
                    ffn_phase(s, l, 1)
                elif l % 2 == 1:
                    PHASES["odd"](s, l)
                else:
                    PHASES["even"](s, l)
            store_seq(s)
        K.barrier_all()
        with nc.allow_non_contiguous_dma(reason="small constant loads"):
            K.emit()
    return nc


FULL_PLAN = [(l, p) for l in range(DEPTH) for p in ("ffn1", "mix", "ffn2")]


def make_consts():
    ident = np.eye(128, dtype=np.float32)
    triu = np.triu(np.ones((128, 128), np.float32))
    rotm = np.zeros((128, 128), np.float32)
    for i in range(16):
        rotm[64 + i + 16, i] = 1.0
        rotm[64 + i, i + 16] = 1.0
    invf = np.zeros((128, 1), np.float32)
    f = (10000.0 ** (-np.arange(0, 32, 2, dtype=np.float32) / 32)).astype(np.float32)
    invf[64:80, 0] = f
    invf[80:96, 0] = f
    invcnt = np.zeros((128, 64), np.float32)
    for g, w in enumerate((2, 4, 8, 16)):
        for t in range(16):
            invcnt[:, g * 16 + t] = 1.0 / min(t + 1, w)
    maskneg = ((1.0 - triu) * -30000.0).astype(np.float32)
    sgn = np.full((128, 1), 1.0 - 1e-6, np.float32)
    sgn[64:80] = -(1.0 - 1e-6)
    return {"c_ident": ident, "c_triu": triu, "c_maskneg": maskneg, "c_sgn": sgn, "c_rotm": rotm, "c_invfreq": invf, "c_invcnt": invcnt}


_CACHE = {}


def kernel(**inputs):
    n = 8
    S = inputs["x"].shape[1]
    B = inputs["x"].shape[0]
    nseq = B // n
    key = (S, nseq)
    if key not in _CACHE:
        _CACHE[key] = build(S, nseq, FULL_PLAN)
    nc = _CACHE[key]
    consts = make_consts()
    in_maps = []
    for c in range(n):
        m = {k: np.ascontiguousarray(v) for k, v in inputs.items() if k not in ("x", "positions")}
        m["x"] = np.ascontiguousarray(inputs["x"][c * nseq:(c + 1) * nseq])
        m["positions"] = np.ascontiguousarray(inputs["positions"][c * nseq:(c + 1) * nseq]).astype(np.int32)
        m.update(consts)
        in_maps.append(m)
    res = run_bass_kernel_spmd(nc, in_maps, core_ids=list(range(n)))
    return np.concatenate([np.asarray(r["y"]) for r in res.results], axis=0).astype(np.float32)
```

```python
import contextlib
import numpy as np
import concourse.bass as bass
import concourse.mybir as mybir
from concourse.bass_utils import run_bass_kernel_spmd

F32 = mybir.dt.float32
BF16 = mybir.dt.bfloat16
I32 = mybir.dt.int32
AF = mybir.ActivationFunctionType
ALU = mybir.AluOpType

D = 1024
DFF = 2816
NFF = 22
DEPTH = 4
EPS = 1e-6
SB_BASE = 16512
SB_END = 229376
HRES_BYTES = 8 * 4096 * 4
ENGS = ("pe", "act", "dve", "pool", "sp")


def _dsize(dt):
    return 2 if dt == BF16 else 4


class Sched:
    def __init__(self, nc, stack):
        self.nc = nc
        self.stack = stack
        self.q = {e: [] for e in ENGS}
        self.tok = {}
        self.sem = {}
        self.seen = {}
        self.res = {}

    def _semh(self, key):
        if key not in self.sem:
            name = "s_" + "_".join(str(x) for x in key)
            self.sem[key] = self.stack.enter_context(self.nc.semaphore(name))
        return self.sem[key]

    def _collect(self, eng, reads, writes):
        need = {}

        def add(t):
            if t is not None:
                k, v = t
                if need.get(k, 0) < v:
                    need[k] = v

        for r in reads:
            e = self.res.get(r)
            if e is not None:
                add(e[0])
        for w in writes:
            e = self.res.get(w)
            if e is not None:
                add(e[0])
                for k, v in e[1].items():
                    add((k, v))
        waits = []
        for k, v in need.items():
            if eng == "pe" and k == ("eng", "pe"):
                continue
            if self.seen.get((eng, k), 0) < v:
                waits.append((k, v))
                self.seen[(eng, k)] = v
        return waits

    def _commit(self, reads, writes, token):
        k, v = token
        for r in reads:
            e = self.res.setdefault(r, [None, {}])
            if e[1].get(k, 0) < v:
                e[1][k] = v
        for w in writes:
            self.res[w] = [token, {}]

    def op(self, eng, fn, reads=(), writes=()):
        waits = self._collect(eng, reads, writes)
        key = ("eng", eng)
        self._semh(key)
        self.tok[key] = self.tok.get(key, 0) + 1
        self._commit(reads, writes, (key, self.tok[key]))
        self.q[eng].append((waits, fn, (key, 1)))

    def dma(self, queue, fn, reads=(), writes=(), sem=None, token_val=None):
        waits = self._collect(queue, reads, writes)
        key = ("dma",) + tuple(sem)
        self._semh(key)
        self.tok[key] = self.tok.get(key, 0) + 16
        tv = self.tok[key] if token_val is None else token_val
        self._commit(reads, writes, (key, tv))
        self.q[queue].append((waits, fn, (key, 16)))

    def barrier_all(self):
        waits = []
        for k, v in self.tok.items():
            if k[0] == "dma" and k[1] != "cast" and self.seen.get(("sp", k), 0) < v:
                waits.append((k, v))
                self.seen[("sp", k)] = v
        for k, v in self.tok.items():
            if k[0] == "eng" and k != ("eng", "sp") and self.seen.get(("sp", k), 0) < v:
                waits.append((k, v))
                self.seen[("sp", k)] = v
        key = ("eng", "sp")
        self._semh(key)
        self.tok[key] = self.tok.get(key, 0) + 1
        self.q["sp"].append((waits, (lambda e: e.nop()), (key, 1)))
        for e in ENGS:
            if e == "sp":
                continue
            ws = []
            for k, v in self.tok.items():
                if k[0] == "eng" and k != ("eng", e) and self.seen.get((e, k), 0) < v:
                    ws.append((k, v))
                    self.seen[(e, k)] = v
            for k, v in self.tok.items():
                if k[0] == "dma" and k[1] != "cast":
                    self.seen[(e, k)] = max(self.seen.get((e, k), 0), v)
            own = ("eng", e)
            if own in self.tok:
                ws.append((own, self.tok[own]))
                self.seen[(e, own)] = self.tok[own]
            if ws:
                self.q[e].append((ws, None, None))
        self.res = {}

    def emit(self):
        nc = self.nc
        with nc.Block() as block:
            decos = {"pe": block.tensor, "act": block.scalar, "dve": block.vector,
                     "pool": block.gpsimd, "sp": block.sync}
            for eng in ENGS:
                def body(e, eng=eng):
                    for waits, fn, inc in self.q[eng]:
                        if fn is None:
                            for k, v in waits:
                                e.wait_ge(self.sem[k], v)
                            continue
                        for k, v in waits[:-1]:
                            e.wait_ge(self.sem[k], v)
                        att = (self.sem[waits[-1][0]], waits[-1][1]) if waits else None
                        ins = fn(_EngProxy(e, att))
                        ins.then_inc(self.sem[inc[0]], inc[1])
                decos[eng](body)


class _EngProxy:
    def __init__(self, e, att):
        self._e = e
        self._att = att

    def __getattr__(self, name):
        f = getattr(self._e, name)

        def call(*a, **kw):
            ins = f(*a, **kw)
            if self._att is not None:
                ins._wait_ge(self._att[0], self._att[1])
                self._att = None
            return ins
        return call


class Arena:
    def __init__(self, nc, lo, hi):
        self.nc = nc
        self.lo = lo
        self.hi = hi
        self.cur = lo
        self.n = 0

    def alloc(self, name, shape, dt):
        nbytes = int(np.prod(shape[1:])) * _dsize(dt)
        nbytes = (nbytes + 31) // 32 * 32
        assert self.cur + nbytes <= self.hi, (name, self.cur, nbytes, self.hi)
        self.n += 1
        t = self.nc.alloc_sbuf_tensor_at("%s_%d" % (name, self.n), list(shape), dt, offset=self.cur)
        self.cur += nbytes
        return t

    def mark(self):
        return self.cur

    def reset(self, m):
        self.cur = m


class Stream:
    def __init__(self, K, name, slots, units, loader, src_keys):
        self.K = K
        self.name = name
        self.slots = slots
        self.units = units
        self.loader = loader
        self.src_keys = src_keys
        self.issued = 0

    def key(self, i):
        return ("ws", self.name, i % len(self.slots))

    def _issue_upto(self, n):
        while self.issued < min(n, len(self.units)):
            i = self.issued
            tile = self.slots[i % len(self.slots)]
            fn = self.loader(self.units[i], tile)
            self.K.dma("sp", fn, reads=self.src_keys(self.units[i]), writes=[self.key(i)],
                       sem=(self.name, i % len(self.slots)))
            self.issued += 1

    def get(self, i):
        self._issue_upto(i + len(self.slots))
        return self.slots[i % len(self.slots)], self.key(i)


def build(S, NSEQ, plan):
    assert S % 512 == 0
    NB = S // 512
    NCH = S // 128
    nc = bass.Bass("TRN2", target_bir_lowering=False)

    def din(name, shape, dt=F32):
        return nc.dram_tensor(name, list(shape), dt, kind="ExternalInput").ap()

    def dscr(name, shape, dt=BF16):
        return nc.dram_tensor(name, list(shape), dt, kind="Internal").ap()

    x = din("x", [NSEQ, S, D])
    positions = din("positions", [NSEQ, S], I32)
    ffn_norm = din("ffn_norm", [DEPTH, 2, D])
    ffn_w_gate = din("ffn_w_gate", [DEPTH, 2, D, DFF])
    ffn_w_up = din("ffn_w_up", [DEPTH, 2, D, DFF])
    ffn_w_down = din("ffn_w_down", [DEPTH, 2, DFF, D])
    mix_norm = din("mix_norm", [DEPTH, D])
    even_w_in = din("even_w_in", [2, D, 928])
    q_a_norm = din("q_a_norm", [2, 256])
    kv_a_norm = din("kv_a_norm", [2, 128])
    w_uq = din("w_uq", [2, 256, 768])
    w_ukv = din("w_ukv", [2, 128, 1024])
    q_norm = din("q_norm", [2, 96])
    k_norm = din("k_norm", [2, 96])
    pool_w = din("pool_w", [2, 4, 128, 128])
    pool_scale = din("pool_scale", [2, 512])
    even_w_out = din("even_w_out", [2, D, D])
    odd_w_in = din("odd_w_in", [2, D, 2048])
    sg_norm = din("sg_norm", [2, D])
    sg_w = din("sg_w", [2, 4, 128, 128])
    sg_b = din("sg_b", [2, 4, 128])
    odd_w_out = din("odd_w_out", [2, D, D])
    c_ident = din("c_ident", [128, 128])
    c_triu = din("c_triu", [128, 128])
    c_maskneg = din("c_maskneg", [128, 128])
    c_sgn = din("c_sgn", [128, 1])
    c_rotm = din("c_rotm", [128, 128])
    c_invfreq = din("c_invfreq", [128, 1])
    c_invcnt = din("c_invcnt", [128, 64])
    y = nc.dram_tensor("y", [NSEQ, S, D], F32, kind="ExternalOutput").ap()

    need_ffn = sorted({(l, j) for (l, p) in plan for j in ((0,) if p == "ffn1" else (1,) if p == "ffn2" else ())})
    need_even = sorted({l // 2 for (l, p) in plan if p == "mix" and l % 2 == 0})
    need_odd = sorted({l // 2 for (l, p) in plan if p == "mix" and l % 2 == 1})

    s_gu = {lj: dscr("s_gu_%d_%d" % lj, [NFF, 128, 2, 8, 128]) for lj in need_ffn}
    s_d = {lj: dscr("s_d_%d_%d" % lj, [8, 128, NFF, 128]) for lj in need_ffn}
    s_ewin = {i: dscr("s_ewin_%d" % i, [7, 128, 8, 128]) for i in need_even}
    s_ekpe = {i: dscr("s_ekpe_%d" % i, [128, 8, 32]) for i in need_even}
    s_euq = {i: dscr("s_euq_%d" % i, [8, 128, 2, 96]) for i in need_even}
    s_eukv = {i: dscr("s_eukv_%d" % i, [128, 1024]) for i in need_even}
    s_epw = {i: dscr("s_epw_%d" % i, [128, 4, 128]) for i in need_even}
    s_ewoa = {i: dscr("s_ewoa_%d" % i, [512, 1024]) for i in need_even}
    s_ewop = {i: dscr("s_ewop_%d" % i, [8, 128, 4, 128]) for i in need_even}
    s_owu = {i: dscr("s_owu_%d" % i, [8, 128, 8, 128]) for i in need_odd}
    s_owv = {i: dscr("s_owv_%d" % i, [128, 8, 1024]) for i in need_odd}
    s_owo = {i: dscr("s_owo_%d" % i, [8, 128, 8, 128]) for i in need_odd}
    s_cs = dscr("s_cs", [NSEQ, 2, 32, S], F32)

    stack = contextlib.ExitStack()
    with stack:
        K = Sched(nc, stack)
        h_res = nc.alloc_sbuf_tensor_at("h_res", [128, 8, S], F32, offset=SB_BASE)
        ar = Arena(nc, SB_BASE + HRES_BYTES, SB_END)
        ps = [stack.enter_context(nc.psum_tensor("ps%d" % i, [128, 512], F32)) for i in range(8)]

        ones_bf = ar.alloc("ones_bf", [128, 128], BF16)
        ones_f = ar.alloc("ones_f", [128, 128], F32)
        ident = ar.alloc("ident", [128, 128], F32)
        g_ffn = ar.alloc("g_ffn", [128, 8, 8], F32)
        g_mix = ar.alloc("g_mix", [128, 4, 8], F32)
        K.op("dve", lambda e: e.memset(ones_bf[:], 1.0), writes=["ones_bf"])
        K.op("dve", lambda e: e.memset(ones_f[:], 1.0), writes=["ones_f"])
        K.dma("sp", lambda e: e.dma_start(out=ident[:], in_=c_ident[:, :]), writes=["ident"], sem=("c", 0))
        g_raw = ar.alloc("g_raw", [128, 128], F32)
        K.dma("sp", lambda e: e.dma_start(out=g_raw[0:64, :], in_=ffn_norm.rearrange("l j (c p) -> (l j c) p", p=128)),
              writes=["g_raw0"], sem=("c", 1))
        K.dma("sp", lambda e: e.dma_start(out=g_raw[64:96, :], in_=mix_norm.rearrange("l (c p) -> (l c) p", p=128)),
              writes=["g_raw1"], sem=("c", 2))
        K.op("pe", lambda e: e.transpose(out=ps[7][:, 0:96], in_=g_raw[0:96, :], identity=ident[0:96, 0:96]),
             reads=["g_raw0", "g_raw1", "ident"], writes=[("ps", 7)])
        K.op("dve", lambda e: e.tensor_scalar(out=g_ffn[:].rearrange("p a c -> p (a c)"), in0=ps[7][:, 0:64], scalar1=32.0, scalar2=None, op0=ALU.mult),
             reads=[("ps", 7)], writes=["g_ffn"])
        K.op("dve", lambda e: e.tensor_scalar(out=g_mix[:].rearrange("p a c -> p (a c)"), in0=ps[7][:, 64:96], scalar1=32.0, scalar2=None, op0=ALU.mult),
             reads=[("ps", 7)], writes=["g_mix"])
        ar_base = ar.mark()

        def cast_group(grp, items):
            n = len(items)
            for idx, (dst, src) in enumerate(items):
                K.dma("pool", (lambda e, dst=dst, src=src: e.dma_start(out=dst, in_=src)),
                      writes=[("scrp", idx) + tuple(grp)], sem=("cast",) + tuple(grp), token_val=16 * n)
            K.res[("scr",) + tuple(grp)] = [(("dma", "cast") + tuple(grp), 16 * n), {}]

        def cast_ffn(l, j):
            items = []
            for m in range(NFF):
                items.append((s_gu[(l, j)][m, :, 0], ffn_w_gate[l, j, :, m * 128:(m + 1) * 128].rearrange("(k p) n -> p k n", p=128)))
                items.append((s_gu[(l, j)][m, :, 1], ffn_w_up[l, j, :, m * 128:(m + 1) * 128].rearrange("(k p) n -> p k n", p=128)))
            for mo in range(8):
                items.append((s_d[(l, j)][mo], ffn_w_down[l, j, :, mo * 128:(mo + 1) * 128].rearrange("(k p) n -> p k n", p=128)))
            cast_group(("ffn", l, j), items)

        def cast_odd(i):
            items = []
            for fc in range(8):
                items.append((s_owu[i][fc], odd_w_in[i, :, fc * 128:(fc + 1) * 128].rearrange("(k p) n -> p k n", p=128)))
            items.append((s_owv[i][:, :, :], odd_w_in[i, :, 1024:2048].rearrange("(k p) n -> p k n", p=128)))
            for mo in range(8):
                items.append((s_owo[i][mo], odd_w_out[i, :, mo * 128:(mo + 1) * 128].rearrange("(k p) n -> p k n", p=128)))
            cast_group(("odd", i), items)

        def cast_even(i):
            items = []
            cols = [0, 128, 256, 416, 544, 672, 800]
            for t, c0 in enumerate(cols):
                items.append((s_ewin[i][t], even_w_in[i, :, c0:c0 + 128].rearrange("(k p) n -> p k n", p=128)))
            items.append((s_ekpe[i][:, :, :], even_w_in[i, :, 384:416].rearrange("(k p) n -> p k n", p=128)))
            for h in range(8):
                items.append((s_euq[i][h], w_uq[i, :, h * 96:(h + 1) * 96].rearrange("(k p) n -> p k n", p=128)))
            items.append((s_eukv[i][:, :], w_ukv[i, :, :]))
            items.append((s_epw[i][:, :, :], pool_w[i].rearrange("g c d -> c g d")))
            items.append((s_ewoa[i][:, :], even_w_out[i, 0:512, :]))
            for mo in range(8):
                items.append((s_ewop[i][mo], even_w_out[i, 512:1024, mo * 128:(mo + 1) * 128].rearrange("(g p) n -> p g n", p=128)))
            cast_group(("even", i), items)

        done_cast = set()
        for (l, p) in plan:
            if p in ("ffn1", "ffn2"):
                key = ("ffn", l, 0 if p == "ffn1" else 1)
                if key not in done_cast:
                    cast_ffn(l, key[2])
            elif l % 2 == 0:
                key = ("even", l // 2)
                if key not in done_cast:
                    cast_even(l // 2)
            else:
                key = ("odd", l // 2)
                if key not in done_cast:
                    cast_odd(l // 2)
            done_cast.add(key)
        scr_tokens = {k: v for k, v in K.res.items() if k[0] == "scr"}

        def barrier():
            K.barrier_all()
            K.res.update({k: [v[0], {}] for k, v in scr_tokens.items()})
            ar.reset(ar_base)

        def hkey(c, b):
            return ("h", c, b)

        def emit_norm_act(b, sq, nchunk=8, c0=0):
            K.op("act", lambda e: e.activation(out=sq[:, 0:nchunk, :], in_=h_res[:, c0:c0 + nchunk, b * 512:(b + 1) * 512], func=AF.Square),
                 reads=[hkey(c, b) for c in range(c0, c0 + nchunk)], writes=["sq"])

        def emit_norm_rest(b, sq, rstd, hn, hn_key, gain, pstat):
            def mm(e):
                for c in range(8):
                    ins = e.matmul(ps[pstat][:, :], lhsT=ones_bf[:, :], rhs=sq[:, c, :], start=(c == 0), stop=(c == 7))
                return ins
            K.op("pe", mm, reads=["sq", "ones_bf"], writes=[("ps", pstat)])
            K.op("act", lambda e: e.activation(out=rstd[:, :], in_=ps[pstat][:, :], func=AF.Sqrt, bias=float(D * EPS)),
                 reads=[("ps", pstat)], writes=["rstd"])
            K.op("dve", lambda e: e.reciprocal(out=rstd[:, :], in_=rstd[:, :]), reads=["rstd"], writes=["rstd"])
            for c in range(8):
                K.op("dve", lambda e, c=c: e.scalar_tensor_tensor(out=hn[:, c, :], in0=h_res[:, c, b * 512:(b + 1) * 512],
                                                                  scalar=gain[:, c:c + 1], in1=rstd[:, :],
                                                                  op0=ALU.mult, op1=ALU.mult),
                     reads=[hkey(c, b), "rstd", "g_ffn", "g_mix"], writes=[hn_key])

        def load_seq(s):
            barrier()
            xs = [ar.alloc("xs", [128, D], F32) for _ in range(2)]
            for tc in range(NCH):
                sl = tc % 2
                K.dma("sp", lambda e, tc=tc, sl=sl: e.dma_start(out=xs[sl][:], in_=x[s, tc * 128:(tc + 1) * 128, :]),
                      writes=[("xs", sl)], sem=("xs", sl))
                for half in range(2):
                    bank = (tc * 2 + half) % 4

                    def tr(e, tc=tc, sl=sl, half=half, bank=bank):
                        for q in range(4):
                            c = half * 4 + q
                            ins = e.transpose(out=ps[bank][:, q * 128:(q + 1) * 128], in_=xs[sl][:, c * 128:(c + 1) * 128], identity=ident[:, :])
                        return ins
                    K.op("pe", tr, reads=[("xs", sl), "ident"], writes=[("ps", bank)])
                    b = tc // 4
                    eng = "act" if half == 0 else "dve"
                    dst = h_res[:, half * 4:half * 4 + 4, tc * 128:(tc + 1) * 128]
                    src = ps[bank][:, :].rearrange("p (q n) -> p q n", q=4)
                    if eng == "act":
                        K.op("act", lambda e, dst=dst, src=src: e.activation(out=dst, in_=src, func=AF.Copy),
                             reads=[("ps", bank)], writes=[hkey(c, b) for c in range(half * 4, half * 4 + 4)])
                    else:
                        K.op("dve", lambda e, dst=dst, src=src: e.tensor_copy(out=dst, in_=src),
                             reads=[("ps", bank)], writes=[hkey(c, b) for c in range(half * 4, half * 4 + 4)])

        def store_seq(s):
            barrier()
            xs = [ar.alloc("xs", [128, D], F32) for _ in range(2)]
            for tc in range(NCH):
                sl = tc % 2
                b = tc // 4
                for half in range(2):
                    bank = (tc * 2 + half) % 4

                    def tr(e, tc=tc, half=half, bank=bank):
                        for q in range(4):
                            c = half * 4 + q
                            ins = e.transpose(out=ps[bank][:, q * 128:(q + 1) * 128], in_=h_res[:, c, tc * 128:(tc + 1) * 128], identity=ident[:, :])
                        return ins
                    K.op("pe", tr, reads=[hkey(c, b) for c in range(half * 4, half * 4 + 4)] + ["ident"], writes=[("ps", bank)])
                    dst = xs[sl][:, half * 512:(half + 1) * 512]
                    if half == 0:
                        K.op("act", lambda e, dst=dst, bank=bank: e.activation(out=dst, in_=ps[bank][:, :], func=AF.Copy),
                             reads=[("ps", bank)], writes=[("xs", sl, half)])
                    else:
                        K.op("dve", lambda e, dst=dst, bank=bank: e.tensor_copy(out=dst, in_=ps[bank][:, :]),
                             reads=[("ps", bank)], writes=[("xs", sl, half)])
                K.dma("sp", lambda e, tc=tc, sl=sl: e.dma_start(out=y[s, tc * 128:(tc + 1) * 128, :], in_=xs[sl][:]),
                      reads=[("xs", sl, 0), ("xs", sl, 1)], writes=[("yout", tc)], sem=("ys", sl))

        def ffn_phase(s, l, j):
            barrier()
            hn = [ar.alloc("hn", [128, 8, 512], BF16) for _ in range(2)]
            sq = ar.alloc("sq", [128, 8, 512], BF16)
            rstd = ar.alloc("rstd", [128, 512], F32)
            sg = [ar.alloc("sg", [128, 512], F32) for _ in range(2)]
            act = ar.alloc("act", [128, NFF, 512], BF16)
            wgu = [ar.alloc("wgu", [128, 2, 8, 128], BF16) for _ in range(3)]
            wd = [ar.alloc("wd", [128, NFF, 128], BF16) for _ in range(2)]
            gain = g_ffn[:, l * 2 + j, :]
            scr = ("scr", "ffn", l, j)
            gu_units = [(b, m) for b in range(NB) for m in range(NFF)]
            d_units = [(b, mo) for b in range(NB) for mo in range(8)]
            st_gu = Stream(K, "wgu", wgu, gu_units,
                           lambda u, t: (lambda e: e.dma_start(out=t[:], in_=s_gu[(l, j)][u[1]])),
                           lambda u: [scr])
            st_d = Stream(K, "wd", wd, d_units,
                          lambda u, t: (lambda e: e.dma_start(out=t[:], in_=s_d[(l, j)][u[1]])),
                          lambda u: [scr])
            PG, PU, PD, PST = (0, 1), (2, 3), (4, 5), 6

            emit_norm_act(0, sq)
            emit_norm_rest(0, sq, rstd, hn[0], ("hn", 0), gain, PST)
            for b in range(NB):
                hs = b % 2
                for m in range(NFF):
                    wt, wkey = st_gu.get(b * NFF + m)
                    gb, ub = PG[m % 2], PU[m % 2]

                    def mmg(e, wt=wt, gb=gb, hs=hs):
                        for k in range(8):
                            ins = e.matmul(ps[gb][:, :], lhsT=wt[:, 0, k, :], rhs=hn[hs][:, k, :], start=(k == 0), stop=(k == 7))
                        return ins

                    def mmu(e, wt=wt, ub=ub, hs=hs):
                        for k in range(8):
                            ins = e.matmul(ps[ub][:, :], lhsT=wt[:, 1, k, :], rhs=hn[hs][:, k, :], start=(k == 0), stop=(k == 7))
                        return ins
                    K.op("pe", mmg, reads=[wkey, ("hn", hs)], writes=[("ps", gb)])
                    K.op("pe", mmu, reads=[wkey, ("hn", hs)], writes=[("ps", ub)])
                    K.op("act", lambda e, gb=gb, m=m: e.activation(out=sg[m % 2][:, :], in_=ps[gb][:, :], func=AF.Silu),
                         reads=[("ps", gb)], writes=[("sg", m % 2)])
                    K.op("dve", lambda e, ub=ub, m=m: e.tensor_tensor(out=act[:, m, :], in0=ps[ub][:, :], in1=sg[m % 2][:, :], op=ALU.mult),
                         reads=[("ps", ub), ("sg", m % 2)], writes=[("act", m)])
                    if b + 1 < NB and m == 4:
                        emit_norm_act(b + 1, sq)
                    if b + 1 < NB and m == 12:
                        emit_norm_rest(b + 1, sq, rstd, hn[1 - hs], ("hn", 1 - hs), gain, PST)
                for mo in range(8):
                    wt, wkey = st_d.get(b * 8 + mo)
                    db = PD[mo % 2]

                    def mmd(e, wt=wt, db=db):
                        for k in range(NFF):
                            ins = e.matmul(ps[db][:, :], lhsT=wt[:, k, :], rhs=act[:, k, :], start=(k == 0), stop=(k == NFF - 1))
                        return ins
                    K.op("pe", mmd, reads=[wkey] + [("act", k) for k in range(NFF)], writes=[("ps", db)])
                    hv = h_res[:, mo, b * 512:(b + 1) * 512]
                    K.op("dve", lambda e, db=db, hv=hv: e.scalar_tensor_tensor(out=hv, in0=ps[db][:, :], scalar=0.5, in1=hv,
                                                                               op0=ALU.mult, op1=ALU.add),
                         reads=[("ps", db), hkey(mo, b)], writes=[hkey(mo, b)])

        def odd_phase(s, l):
            i = l // 2
            barrier()
            scr = ("scr", "odd", i)
            wv = ar.alloc("wv", [128, 8, 1024], BF16)
            hn = ar.alloc("hn", [128, 8, 512], BF16)
            sqg = ar.alloc("sqg", [128, 8, 512], BF16)
            rstd = ar.alloc("rstd", [128, 512], F32)
            v32 = ar.alloc("v32", [128, 1024], F32)
            ss = ar.alloc("ss", [128, 8], F32)
            vn = ar.alloc("vn", [128, 4, 1024], BF16)
            uf = [ar.alloc("uf", [128, 512], F32) for _ in range(2)]
            wu = [ar.alloc("wu", [128, 8, 128], BF16) for _ in range(3)]
            wo = [ar.alloc("wo", [128, 8, 128], BF16) for _ in range(3)]
            sgwT = ar.alloc("sgwT", [128, 4, 128], BF16)
            sgw_raw = ar.alloc("sgw_raw", [128, 4, 128], F32)
            sgb = ar.alloc("sgb", [128, 512], F32)
            gsg = ar.alloc("gsg", [128, 1024], F32)
            triu = ar.alloc("triu", [128, 128], F32)
            K.dma("sp", lambda e: e.dma_start(out=wv[:], in_=s_owv[i][:, :, :]), reads=[scr], writes=["wv"], sem=("o", 0))
            K.dma("sp", lambda e: e.dma_start(out=sgw_raw[:], in_=sg_w[i].rearrange("g t s -> t g s")), writes=["sgw_raw"], sem=("o", 1))
            K.dma("sp", lambda e: e.dma_start(out=triu[:], in_=c_triu[:, :]), writes=["triu"], sem=("o", 2))
            K.dma("sp", lambda e: e.dma_start(out=sgb[0:1, :], in_=sg_b[i:i + 1].rearrange("o g t -> o (g t)")), writes=["sgb"], sem=("o", 3))
            K.dma("sp", lambda e: e.dma_start(out=gsg[:], in_=sg_norm[i:i + 1, :].broadcast_to([128, 1024])), writes=["gsg"], sem=("o", 4))
            K.op("dve", lambda e: e.tensor_scalar(out=gsg[:], in0=gsg[:], scalar1=32.0, scalar2=None, op0=ALU.mult), reads=["gsg"], writes=["gsg"])
            for g in range(4):
                K.op("pe", lambda e, g=g: e.transpose(out=ps[g % 2][:, 0:128], in_=sgw_raw[:, g, :], identity=ident[:, :]),
                     reads=["sgw_raw", "ident"], writes=[("ps", g % 2)])
                K.op("dve", lambda e, g=g: e.tensor_tensor(out=sgwT[:, g, :], in0=ps[g % 2][:, 0:128], in1=triu[:, :], op=ALU.mult),
                     reads=[("ps", g % 2), "triu"], writes=["sgwT"])
            u_units = [(b, fc) for b in range(NB) for fc in range(8)]
            st_u = Stream(K, "wu", wu, u_units, lambda u, t: (lambda e: e.dma_start(out=t[:], in_=s_owu[i][u[1]])), lambda u: [scr])
            st_o = Stream(K, "wo", wo, u_units, lambda u, t: (lambda e: e.dma_start(out=t[:], in_=s_owo[i][u[1]])), lambda u: [scr])
            gain = g_mix[:, l, :]
            for b in range(NB):
                emit_norm_act(b, sqg)
                emit_norm_rest(b, sqg, rstd, hn, "hn", gain, 6)
                for ch in range(4):
                    for half in range(2):
                        bank = half

                        def mmv(e, ch=ch, half=half, bank=bank):
                            for k in range(8):
                                ins = e.matmul(ps[bank][:, :], lhsT=hn[:, k, ch * 128:(ch + 1) * 128], rhs=wv[:, k, half * 512:(half + 1) * 512],
                                               start=(k == 0), stop=(k == 7))
                            return ins
                        K.op("pe", mmv, reads=["hn", "wv"], writes=[("ps", bank)])
                        K.op("act", lambda e, half=half, bank=bank: e.activation(out=v32[:, half * 512:(half + 1) * 512], in_=ps[bank][:, :], func=AF.Gelu_apprx_tanh),
                             reads=[("ps", bank)], writes=[("v32", half)])
                    K.op("act", lambda e, ch=ch: e.activation(out=vn[:, ch, :], in_=v32[:, :], func=AF.Square, accum_out=ss[:, ch:ch + 1]),
                         reads=[("v32", 0), ("v32", 1)], writes=[("vn", ch), ("ss", ch)])
                    K.op("act", lambda e, ch=ch: e.activation(out=ss[:, ch:ch + 1], in_=ss[:, ch:ch + 1], func=AF.Sqrt, bias=float(D * EPS)),
                         reads=[("ss", ch)], writes=[("ss", ch)])
                    K.op("dve", lambda e, ch=ch: e.reciprocal(out=ss[:, ch:ch + 1], in_=ss[:, ch:ch + 1]), reads=[("ss", ch)], writes=[("ss", ch)])
                    K.op("dve", lambda e, ch=ch: e.scalar_tensor_tensor(out=vn[:, ch, :], in0=v32[:, :], scalar=ss[:, ch:ch + 1], in1=gsg[:, :],
                                                                        op0=ALU.mult, op1=ALU.mult),
                         reads=[("v32", 0), ("v32", 1), ("ss", ch), "gsg"], writes=[("vn", ch)])
                for fc in range(8):
                    g = fc // 2
                    wt, wkey = st_u.get(b * 8 + fc)
                    ub = 2 + fc % 2
                    mb = 4 + fc % 2

                    def mmu(e, wt=wt, ub=ub):
                        for k in range(8):
                            ins = e.matmul(ps[ub][:, :], lhsT=wt[:, k, :], rhs=hn[:, k, :], start=(k == 0), stop=(k == 7))
                        return ins
                    K.op("pe", mmu, reads=[wkey, "hn"], writes=[("ps", ub)])
                    K.op("act", lambda e, ub=ub, fc=fc: e.activation(out=uf[fc % 2][:, :], in_=ps[ub][:, :], func=AF.Gelu_apprx_tanh),
                         reads=[("ps", ub)], writes=[("uf", fc % 2)])

                    def mmx(e, fc=fc, g=g, mb=mb):
                        for ch in range(4):
                            e.matmul(ps[mb][:, ch * 128:(ch + 1) * 128], lhsT=vn[:, ch, fc * 128:(fc + 1) * 128], rhs=sgwT[:, g, :], start=True, stop=False)
                            ins = e.matmul(ps[mb][:, ch * 128:(ch + 1) * 128], lhsT=ones_f[0:1, :], rhs=sgb[0:1, g * 128:(g + 1) * 128], start=False, stop=True)
                        return ins
                    K.op("pe", mmx, reads=[("vn", c) for c in range(4)] + ["sgwT", "sgb", "ones_f"], writes=[("ps", mb)])
                    K.op("dve", lambda e, fc=fc, mb=mb: e.tensor_tensor(out=sqg[:, fc, :], in0=ps[mb][:, :], in1=uf[fc % 2][:, :], op=ALU.mult),
                         reads=[("ps", mb), ("uf", fc % 2)], writes=["sq"])
                for mo in range(8):
                    wt, wkey = st_o.get(b * 8 + mo)
                    db = 6 + mo % 2

                    def mmo(e, wt=wt, db=db):
                        for k in range(8):
                            ins = e.matmul(ps[db][:, :], lhsT=wt[:, k, :], rhs=sqg[:, k, :], start=(k == 0), stop=(k == 7))
                        return ins
                    K.op("pe", mmo, reads=[wkey, "sq"], writes=[("ps", db)])
                    hv = h_res[:, mo, b * 512:(b + 1) * 512]
                    K.op("dve", lambda e, db=db, hv=hv: e.tensor_tensor(out=hv, in0=ps[db][:, :], in1=hv, op=ALU.add),
                         reads=[("ps", db), hkey(mo, b)], writes=[hkey(mo, b)])

        s_kpe = dscr("s_kpe", [32, S], F32)
        s_rl = dscr("s_rl", [2, 512], F32)
        TWO_PI = 2.0 * np.pi
        C1 = 6.28125
        C2 = TWO_PI - C1
        SC = 1.0 - 1e-6

        def tables_phase(s):
            barrier()
            invf = ar.alloc("invf", [128, 1], F32)
            pi_ = ar.alloc("pi_", [128, 512], I32)
            pf = ar.alloc("pf", [128, 512], F32)
            av = ar.alloc("av", [128, 512], F32)
            tf = ar.alloc("tf", [128, 512], F32)
            ki = ar.alloc("ki", [128, 512], I32)
            kf = ar.alloc("kf", [128, 512], F32)
            rr = ar.alloc("rr", [128, 512], F32)
            outt = [ar.alloc("outt", [128, 512], F32) for _ in range(2)]
            R = slice(64, 96)
            K.dma("sp", lambda e: e.dma_start(out=invf[:], in_=c_invfreq[:, :]), writes=["invf"], sem=("t", 0))
            sgn = ar.alloc("sgn", [128, 1], F32)
            hpi = ar.alloc("hpi", [128, 1], F32)
            K.dma("sp", lambda e: e.dma_start(out=sgn[:], in_=c_sgn[:, :]), writes=["sgn"], sem=("t", 4))
            K.op("dve", lambda e: e.memset(hpi[:], float(np.pi / 2 * SC)), writes=["hpi"])
            for b in range(NB):
                bl = slice(b * 512, (b + 1) * 512)
                K.dma("sp", lambda e, bl=bl: e.dma_start(out=pi_[R, :], in_=positions[s:s + 1, bl].broadcast_to([32, 512])), writes=["pi"], sem=("t", 1))
                K.op("dve", lambda e: e.tensor_copy(out=pf[R, :], in_=pi_[R, :]), reads=["pi"], writes=["pf"])
                K.op("dve", lambda e: e.tensor_scalar(out=av[R, :], in0=pf[R, :], scalar1=invf[R, 0:1], scalar2=None, op0=ALU.mult), reads=["pf", "invf"], writes=["av"])
                for which in range(2):
                    off = 0.25 if which == 0 else 0.0
                    K.op("dve", lambda e, off=off: e.tensor_scalar(out=ki[R, :], in0=av[R, :], scalar1=float(1.0 / TWO_PI), scalar2=float(off), op0=ALU.mult, op1=ALU.add),
                         reads=["av"], writes=["ki"])
                    K.op("dve", lambda e: e.tensor_copy(out=kf[R, :], in_=ki[R, :]), reads=["ki"], writes=["kf"])
                    K.op("dve", lambda e: e.scalar_tensor_tensor(out=rr[R, :], in0=kf[R, :], scalar=float(-C1), in1=av[R, :], op0=ALU.mult, op1=ALU.add),
                         reads=["kf", "av"], writes=["rr"])
                    K.op("dve", lambda e: e.scalar_tensor_tensor(out=rr[R, :], in0=kf[R, :], scalar=float(-C2), in1=rr[R, :], op0=ALU.mult, op1=ALU.add),
                         reads=["kf", "rr"], writes=["rr"])
                    if which == 0:
                        K.op("act", lambda e: e.activation(out=outt[0][R, :], in_=rr[R, :], func=AF.Sin, scale=float(SC), bias=hpi[R, 0:1]),
                             reads=["rr", "hpi"], writes=[("outt", 0)])
                    else:
                        K.op("act", lambda e: e.activation(out=outt[1][R, :], in_=rr[R, :], func=AF.Sin, scale=sgn[R, 0:1]),
                             reads=["rr", "sgn"], writes=[("outt", 1)])
                    K.dma("sp", lambda e, which=which, bl=bl: e.dma_start(out=s_cs[s, which, :, bl], in_=outt[which][R, :]),
                          reads=[("outt", which)], writes=[("cs", b, which)], sem=("t", 2 + which))

        def even_phase(s, l):
            i = l // 2
            barrier()
            scr = ("scr", "even", i)
            R = slice(64, 96)
            cqn = ar.alloc("cqn", [128, 2, S], BF16)
            ckvn = ar.alloc("ckvn", [128, S], BF16)
            gq = ar.alloc("gq", [128, 2], F32)
            gkv = ar.alloc("gkv", [128, 1], F32)
            qg = ar.alloc("qg", [128, 1], F32)
            kg = ar.alloc("kg", [128, 1], F32)
            pscale = ar.alloc("pscale", [128, 4], F32)
            K.dma("sp", lambda e: e.dma_start(out=gq[:], in_=q_a_norm[i:i + 1, :].rearrange("o (c p) -> p (o c)", p=128)), writes=["gq"], sem=("e", 0))
            K.dma("sp", lambda e: e.dma_start(out=gkv[:], in_=kv_a_norm[i:i + 1, :].rearrange("o p -> p o")), writes=["gkv"], sem=("e", 1))
            K.dma("sp", lambda e: e.dma_start(out=qg[0:96, :], in_=q_norm[i:i + 1, :].rearrange("o p -> p o")), writes=["qg"], sem=("e", 2))
            K.dma("sp", lambda e: e.dma_start(out=kg[0:96, :], in_=k_norm[i:i + 1, :].rearrange("o p -> p o")), writes=["kg"], sem=("e", 3))
            K.dma("sp", lambda e: e.dma_start(out=pscale[:], in_=pool_scale[i:i + 1, :].rearrange("o (g p) -> p (o g)", p=128)), writes=["pscale"], sem=("e", 4))
            K.op("dve", lambda e: e.tensor_scalar(out=gq[:], in0=gq[:], scalar1=16.0, scalar2=None, op0=ALU.mult), reads=["gq"], writes=["gq"])
            K.op("dve", lambda e: e.tensor_scalar(out=gkv[:], in0=gkv[:], scalar1=float(np.sqrt(128.0)), scalar2=None, op0=ALU.mult), reads=["gkv"], writes=["gkv"])
            K.op("dve", lambda e: e.tensor_scalar(out=kg[0:96, :], in0=kg[0:96, :], scalar1=float(np.sqrt(96.0)), scalar2=None, op0=ALU.mult), reads=["kg"], writes=["kg"])
            mA = ar.mark()
            hn = ar.alloc("hn", [128, 8, 512], BF16)
            sq = ar.alloc("sq", [128, 8, 512], BF16)
            rstd = ar.alloc("rstd", [128, 512], F32)
            rstc = ar.alloc("rstc", [128, 512], F32)
            win = [ar.alloc("win", [128, 8, 128], BF16) for _ in range(3)]
            wkpe = ar.alloc("wkpe", [128, 8, 32], BF16)
            pbuf = ar.alloc("pbuf", [128, 528], F32)
            t1 = ar.alloc("t1", [128, 528], F32)
            t2 = ar.alloc("t2", [128, 528], F32)
            halo = ar.alloc("halo", [128, 4, 16], F32)
            pooled = [ar.alloc("pooled", [128, 512], BF16) for _ in range(2)]
            pfix = ar.alloc("pfix", [128, 16], F32)
            pout = ar.alloc("pout", [128, 4, 512], BF16)
            pw = ar.alloc("pw", [128, 4, 128], BF16)
            invc = ar.alloc("invc", [128, 64], F32)
            wop = [ar.alloc("wop", [128, 4, 128], BF16) for _ in range(3)]
            kst = ar.alloc("kst", [128, 512], F32)
            K.dma("sp", lambda e: e.dma_start(out=wkpe[:], in_=s_ekpe[i][:, :, :]), reads=[scr], writes=["wkpe"], sem=("e", 5))
            K.dma("sp", lambda e: e.dma_start(out=pw[:], in_=s_epw[i][:, :, :]), reads=[scr], writes=["pw"], sem=("e", 6))
            K.dma("sp", lambda e: e.dma_start(out=invc[:], in_=c_invcnt[:, :]), writes=["invc"], sem=("e", 7))
            K.op("dve", lambda e: e.memset(halo[:], 0.0), writes=["halo"])
            in_units = [(b, t) for b in range(NB) for t in range(7)]
            st_in = Stream(K, "win", win, in_units, lambda u, t: (lambda e: e.dma_start(out=t[:], in_=s_ewin[i][u[1]])), lambda u: [scr])
            op_units = [(b, mo) for b in range(NB) for mo in range(8)]
            st_op = Stream(K, "wop", wop, op_units, lambda u, t: (lambda e: e.dma_start(out=t[:], in_=s_ewop[i][u[1]])), lambda u: [scr])
            gain = g_mix[:, l, :]
            WIN = (2, 4, 8, 16)
            for b in range(NB):
                bl = slice(b * 512, (b + 1) * 512)
                emit_norm_act(b, sq)
                emit_norm_rest(b, sq, rstd, hn, "hn", gain, 6)
                banks = [0, 1, 2, 4, 5, 4, 5]
                for t in range(7):
                    wt, wkey = st_in.get(b * 7 + t)
                    bk = banks[t]

                    def mmi(e, wt=wt, bk=bk):
                        for k in range(8):
                            ins = e.matmul(ps[bk][:, :], lhsT=wt[:, k, :], rhs=hn[:, k, :], start=(k == 0), stop=(k == 7))
                        return ins
                    K.op("pe", mmi, reads=[wkey, "hn"], writes=[("ps", bk)])
                    if t == 1:
                        for c in range(2):
                            K.op("act", lambda e, c=c: e.activation(out=sq[:, c, :], in_=ps[c][:, :], func=AF.Square), reads=[("ps", c)], writes=["sq"])

                        def mms(e):
                            e.matmul(ps[6][:, :], lhsT=ones_bf[:, :], rhs=sq[:, 0, :], start=True, stop=False)
                            return e.matmul(ps[6][:, :], lhsT=ones_bf[:, :], rhs=sq[:, 1, :], start=False, stop=True)
                        K.op("pe", mms, reads=["sq", "ones_bf"], writes=[("ps", 6)])
                        K.op("act", lambda e: e.activation(out=rstc[:, :], in_=ps[6][:, :], func=AF.Sqrt, bias=float(256 * EPS)), reads=[("ps", 6)], writes=["rstc"])
                        K.op("dve", lambda e: e.reciprocal(out=rstc[:, :], in_=rstc[:, :]), reads=["rstc"], writes=["rstc"])
                        for c in range(2):
                            K.op("dve", lambda e, c=c, bl=bl: e.scalar_tensor_tensor(out=cqn[:, c, bl], in0=ps[c][:, :], scalar=gq[:, c:c + 1], in1=rstc[:, :],
                                                                                     op0=ALU.mult, op1=ALU.mult),
                                 reads=[("ps", c), "gq", "rstc"], writes=[("cqn", b)])
                    if t == 2:
                        K.op("act", lambda e: e.activation(out=sq[:, 2, :], in_=ps[2][:, :], func=AF.Square), reads=[("ps", 2)], writes=["sq"])
                        K.op("pe", lambda e: e.matmul(ps[6][:, :], lhsT=ones_bf[:, :], rhs=sq[:, 2, :], start=True, stop=True), reads=["sq", "ones_bf"], writes=[("ps", 6)])
                        K.op("act", lambda e: e.activation(out=rstc[:, :], in_=ps[6][:, :], func=AF.Sqrt, bias=float(128 * EPS)), reads=[("ps", 6)], writes=["rstc"])
                        K.op("dve", lambda e: e.reciprocal(out=rstc[:, :], in_=rstc[:, :]), reads=["rstc"], writes=["rstc"])
                        K.op("dve", lambda e, bl=bl: e.scalar_tensor_tensor(out=ckvn[:, bl], in0=ps[2][:, :], scalar=gkv[:, 0:1], in1=rstc[:, :], op0=ALU.mult, op1=ALU.mult),
                             reads=[("ps", 2), "gkv", "rstc"], writes=[("ckvn", b)])

                        def mmk(e):
                            for k in range(8):
                                ins = e.matmul(ps[3][R, :], lhsT=wkpe[:, k, :], rhs=hn[:, k, :], start=(k == 0), stop=(k == 7))
                            return ins
                        K.op("pe", mmk, reads=["wkpe", "hn"], writes=[("ps", 3)])
                        K.op("act", lambda e: e.activation(out=kst[R, :], in_=ps[3][R, :], func=AF.Copy), reads=[("ps", 3)], writes=["kst"])
                        K.dma("sp", lambda e, bl=bl: e.dma_start(out=s_kpe[:, bl], in_=kst[R, :]), reads=["kst"], writes=[("kpe", b)], sem=("e", 8))
                    if t >= 3:
                        g = t - 3
                        w = WIN[g]
                        K.op("act", lambda e, bk=bk: e.activation(out=pbuf[:, 16:528], in_=ps[bk][:, :], func=AF.Copy), reads=[("ps", bk)], writes=["pbuf"])
                        K.op("dve", lambda e, g=g: e.tensor_copy(out=pbuf[:, 0:16], in_=halo[:, g, :]), reads=["halo"], writes=["pbuf"])
                        K.op("dve", lambda e: e.tensor_tensor(out=t1[:, 1:528], in0=pbuf[:, 1:528], in1=pbuf[:, 0:527], op=ALU.add), reads=["pbuf"], writes=["t1"])
                        fin = t1
                        if g >= 1:
                            K.op("dve", lambda e: e.tensor_tensor(out=t2[:, 3:528], in0=t1[:, 3:528], in1=t1[:, 1:526], op=ALU.add), reads=["t1"], writes=["t2"])
                            fin = t2
                        if g >= 2:
                            K.op("dve", lambda e: e.tensor_tensor(out=t1[:, 7:528], in0=t2[:, 7:528], in1=t2[:, 3:524], op=ALU.add), reads=["t2"], writes=["t1"])
                            fin = t1
                        if g >= 3:
                            K.op("dve", lambda e: e.tensor_tensor(out=t2[:, 15:528], in0=t1[:, 15:528], in1=t1[:, 7:520], op=ALU.add), reads=["t1"], writes=["t2"])
                            fin = t2
                        fkey = "t1" if fin is t1 else "t2"
                        pl = pooled[g % 2]
                        K.op("dve", lambda e, fin=fin, pl=pl, w=w: e.scalar_tensor_tensor(out=pl[:, :], in0=fin[:, 16:528], scalar=float(1.0 / w), in1=pbuf[:, 16:528],
                                                                                           op0=ALU.mult, op1=ALU.subtract),
                             reads=[fkey, "pbuf"], writes=[("pooled", g % 2)])
                        if b == 0:
                            K.op("dve", lambda e, fin=fin, g=g: e.tensor_tensor(out=pfix[:, :], in0=fin[:, 16:32], in1=invc[:, g * 16:(g + 1) * 16], op=ALU.mult),
                                 reads=[fkey, "invc"], writes=["pfix"])
                            K.op("dve", lambda e, pl=pl: e.tensor_tensor(out=pl[:, 0:16], in0=pfix[:, :], in1=pbuf[:, 16:32], op=ALU.subtract),
                                 reads=["pfix", "pbuf", ("pooled", g % 2)], writes=[("pooled", g % 2)])
                        K.op("dve", lambda e, g=g: e.tensor_copy(out=halo[:, g, :], in_=pbuf[:, 512:528]), reads=["pbuf"], writes=["halo"])
                        K.op("pe", lambda e, g=g, pl=pl: e.matmul(ps[7][:, :], lhsT=pw[:, g, :], rhs=pl[:, :], start=True, stop=True),
                             reads=["pw", ("pooled", g % 2)], writes=[("ps", 7)])
                        K.op("act", lambda e, g=g: e.activation(out=pout[:, g, :], in_=ps[7][:, :], func=AF.Copy, scale=pscale[:, g:g + 1]),
                             reads=[("ps", 7), "pscale"], writes=[("pout", g)])
                for mo in range(8):
                    wt, wkey = st_op.get(b * 8 + mo)
                    db = mo % 2

                    def mmo(e, wt=wt, db=db):
                        for g in range(4):
                            ins = e.matmul(ps[db][:, :], lhsT=wt[:, g, :], rhs=pout[:, g, :], start=(g == 0), stop=(g == 3))
                        return ins
                    K.op("pe", mmo, reads=[wkey] + [("pout", g) for g in range(4)], writes=[("ps", db)])
                    hv = h_res[:, mo, bl]
                    K.op("dve", lambda e, db=db, hv=hv: e.tensor_tensor(out=hv, in0=ps[db][:, :], in1=hv, op=ALU.add),
                         reads=[("ps", db), hkey(mo, b)], writes=[hkey(mo, b)])

            K.barrier_all()
            K.res.update({k: [v[0], {}] for k, v in scr_tokens.items()})
            ar.reset(mA)
            kh = [ar.alloc("kh", [128, S], BF16) for _ in range(2)]
            vx = ar.alloc("vx", [128, NCH, 65], BF16)
            wuq = [ar.alloc("wuq", [128, 2, 96], BF16) for _ in range(2)]
            wukv = ar.alloc("wukv", [128, 1024], BF16)
            woa = [ar.alloc("woa", [128, 1024], BF16) for _ in range(2)]
            cosT = ar.alloc("cosT", [128, 512], F32)
            sinT = ar.alloc("sinT", [128, 512], F32)
            xk = ar.alloc("xk", [128, 512], F32)
            sqk = ar.alloc("sqk", [128, 512], BF16)
            rstb = ar.alloc("rstb", [128, 512], F32)
            xr = ar.alloc("xr", [128, 512], F32)
            tmp = ar.alloc("tmp", [128, 512], F32)
            qh = [ar.alloc("qh", [128, 512], BF16) for _ in range(2)]
            pT = [ar.alloc("pT", [128, 512], BF16) for _ in range(3)]
            bc = ar.alloc("bc", [128, 512], F32)
            rl = bc
            ao = ar.alloc("ao", [128, 512], BF16)
            rotm = ar.alloc("rotm", [128, 128], F32)
            trb = ar.alloc("trb", [128, 128], BF16)
            idb = ar.alloc("idb", [128, 128], BF16)
            K.dma("sp", lambda e: e.dma_start(out=wukv[:], in_=s_eukv[i][:, :]), reads=[scr], writes=["wukv"], sem=("e", 9))
            K.dma("sp", lambda e: e.dma_start(out=rotm[:], in_=c_rotm[:, :]), writes=["rotm"], sem=("e", 10))
            K.dma("pool", lambda e: e.dma_start(out=trb[:], in_=c_maskneg[:, :]), writes=["trb"], sem=("e", 11))
            K.dma("pool", lambda e: e.dma_start(out=idb[:], in_=c_ident[:, :]), writes=["idb"], sem=("e", 15))
            K.op("dve", lambda e: e.memset(vx[:, :, 64:65], 1.0), writes=["vx1"])
            eps96 = ar.alloc("eps96", [128, 1], F32)
            K.op("dve", lambda e: e.memset(eps96[:], float(96 * EPS)), writes=["eps96"])
            PQ, PST, PDR, PSC, PO = 0, 1, (2, 3), (4, 5), (6, 7)

            def g_norm_rope(src_nope, src_rope, rope_key, gcol, dst, dst_key, j):
                bl = slice(j * 512, (j + 1) * 512)
                K.dma("sp", lambda e, bl=bl: e.dma_start(out=cosT[R, :], in_=s_cs[s, 0, :, bl]), reads=[("cs", j, 0)], writes=["cosT"], sem=("e", 12))
                K.dma("sp", lambda e, bl=bl: e.dma_start(out=sinT[R, :], in_=s_cs[s, 1, :, bl]), reads=[("cs", j, 1)], writes=["sinT"], sem=("e", 13))
                yield
                if src_rope is None:
                    K.op("act", lambda e: e.activation(out=sqk[0:96, :], in_=src_nope[0:96, :], func=AF.Square), reads=[("ps", PQ)], writes=["sqk"])
                    rsrc = src_nope
                else:
                    K.op("act", lambda e: e.activation(out=sqk[0:64, :], in_=src_nope[0:64, :], func=AF.Square), reads=[("ps", PQ)], writes=["sqk"])
                    K.op("act", lambda e: e.activation(out=sqk[R, :], in_=src_rope[R, :], func=AF.Square), reads=[rope_key, "sqk"], writes=["sqk"])
                    rsrc = src_rope
                yield
                K.op("pe", lambda e: e.matmul(ps[PST][0:96, :], lhsT=ones_bf[0:96, 0:96], rhs=sqk[0:96, :], start=True, stop=True), reads=["sqk", "ones_bf"], writes=[("ps", PST)])
                yield
                K.op("act", lambda e: e.activation(out=rstb[0:96, :], in_=ps[PST][0:96, :], func=AF.Ln, bias=eps96[0:96, 0:1]), reads=[("ps", PST), "eps96"], writes=["rstb"])
                yield
                K.op("act", lambda e: e.activation(out=rstb[0:96, :], in_=rstb[0:96, :], func=AF.Exp, scale=-0.5), reads=["rstb"], writes=["rstb"])
                yield
                K.op("dve", lambda e: e.scalar_tensor_tensor(out=dst[0:64, :], in0=src_nope[0:64, :], scalar=gcol[0:64, 0:1], in1=rstb[0:64, :], op0=ALU.mult, op1=ALU.mult),
                     reads=[("ps", PQ), "rstb", "qg", "kg"], writes=[dst_key])
                K.op("dve", lambda e: e.scalar_tensor_tensor(out=xr[R, :], in0=rsrc[R, :], scalar=gcol[R, 0:1], in1=rstb[R, :], op0=ALU.mult, op1=ALU.mult),
                     reads=[("ps", PQ), rope_key, "rstb", "qg", "kg"], writes=["xr"])
                yield
                K.op("pe", lambda e: e.matmul(ps[PST][R, :], lhsT=rotm[R, 0:32], rhs=xr[R, :], start=True, stop=True), reads=["xr", "rotm"], writes=[("ps", PST)])
                yield
                K.op("dve", lambda e: e.tensor_tensor(out=tmp[R, :], in0=ps[PST][R, :], in1=sinT[R, :], op=ALU.mult), reads=[("ps", PST), "sinT"], writes=["tmp"])
                K.op("dve", lambda e: e.tensor_tensor(out=xr[R, :], in0=xr[R, :], in1=cosT[R, :], op=ALU.mult), reads=["xr", "cosT", ("ps", PST)], writes=["xr"])
                yield
                K.op("dve", lambda e: e.tensor_tensor(out=dst[R, :], in0=xr[R, :], in1=tmp[R, :], op=ALU.add), reads=["xr", "tmp"], writes=[dst_key])
                yield

            def g_kblock(h, kb, j):
                bl = slice(j * 512, (j + 1) * 512)
                K.dma("sp", lambda e, bl=bl: e.dma_start(out=xk[R, :], in_=s_kpe[:, bl]), reads=[("kpe", j)], writes=["xk"], sem=("e", 14))
                K.op("pe", lambda e, h=h, bl=bl: e.matmul(ps[PQ][0:64, :], lhsT=wukv[:, h * 128:h * 128 + 64], rhs=ckvn[:, bl], start=True, stop=True),
                     reads=["wukv", ("ckvn", j)], writes=[("ps", PQ)])
                yield
                yield from g_norm_rope(ps[PQ], xk, "xk", kg, kh[kb][:, bl], ("kh", kb, j), j)

            def g_qbuild(hs, j):
                bl = slice(j * 512, (j + 1) * 512)

                def mmq(e, hs=hs, bl=bl):
                    e.matmul(ps[PQ][0:96, :], lhsT=wuq[hs][:, 0, :], rhs=cqn[:, 0, bl], start=True, stop=False)
                    return e.matmul(ps[PQ][0:96, :], lhsT=wuq[hs][:, 1, :], rhs=cqn[:, 1, bl], start=False, stop=True)
                K.op("pe", mmq, reads=[("wuq", hs), ("cqn", j)], writes=[("ps", PQ)])
                yield
                yield from g_norm_rope(ps[PQ], None, ("ps", PQ), qg, qh[j % 2], ("qh", j % 2), j)

            def g_tail(hs, j):
                ob = PO[j % 2]
                bl = slice(j * 512, (j + 1) * 512)
                K.op("act", lambda e: e.activation(out=rl[64:65, :], in_=ps[ob][64:65, :], func=AF.Copy), reads=[("ps", ob)], writes=["rl"])
                yield
                K.op("pe", lambda e: e.matmul(ps[PDR[0]][0:64, :], lhsT=ones_f[64:65, 0:64], rhs=rl[64:65, :], start=True, stop=True), reads=["rl", "ones_f"], writes=[("ps", PDR[0])])
                yield
                K.op("act", lambda e: e.activation(out=bc[0:64, :], in_=ps[PDR[0]][0:64, :], func=AF.Ln), reads=[("ps", PDR[0])], writes=["bc"])
                yield
                K.op("act", lambda e: e.activation(out=bc[0:64, :], in_=bc[0:64, :], func=AF.Exp, scale=-1.0), reads=["bc"], writes=["bc"])
                yield
                K.op("dve", lambda e: e.tensor_tensor(out=ao[0:64, :], in0=ps[ob][0:64, :], in1=bc[0:64, :], op=ALU.mult), reads=[("ps", ob), "bc"], writes=["ao"])
                yield
                for mo in range(8):
                    db = PDR[mo % 2]
                    K.op("pe", lambda e, mo=mo, db=db: e.matmul(ps[db][:, :], lhsT=woa[hs][0:64, mo * 128:(mo + 1) * 128], rhs=ao[0:64, :], start=True, stop=True),
                         reads=[("woa", hs), "ao"], writes=[("ps", db)])
                    hv = h_res[:, mo, bl]
                    K.op("dve", lambda e, db=db, hv=hv: e.tensor_tensor(out=hv, in0=ps[db][:, :], in1=hv, op=ALU.add),
                         reads=[("ps", db), hkey(mo, j)], writes=[hkey(mo, j)])
                    yield

            class Lanes:
                def __init__(self):
                    self.tail = None
                    self.cur = None
                    self.qpend = None
                    self.kpend = []

                def _step(self, g):
                    try:
                        next(g)
                        return True
                    except StopIteration:
                        return False

                def step_b(self):
                    while True:
                        if self.cur is None:
                            if self.qpend is not None:
                                self.cur, self.qpend = self.qpend, None
                            elif self.kpend:
                                self.cur = self.kpend.pop(0)
                            else:
                                return False
                        if self._step(self.cur):
                            return True
                        self.cur = None

                def step_a(self):
                    if self.tail is not None:
                        if self._step(self.tail):
                            return True
                        self.tail = None
                    return False

                def pump(self, n):
                    for t in range(n):
                        if t % 2 == 0:
                            if not self.step_a():
                                self.step_b()
                        else:
                            if not self.step_b():
                                self.step_a()

                def finish_tail(self):
                    while self.step_a():
                        pass

                def finish_q(self):
                    while self.cur is not None or self.qpend is not None:
                        if self.cur is None:
                            self.cur, self.qpend = self.qpend, None
                        if not self._step(self.cur):
                            self.cur = None

                def finish_all(self):
                    self.finish_tail()
                    while self.step_b():
                        pass

            def drain(g):
                if g is not None:
                    for _ in g:
                        pass

            def load_head_w(h):
                hs = h % 2
                K.dma("sp", lambda e: e.dma_start(out=wuq[hs][:], in_=s_euq[i][h]), reads=[scr], writes=[("wuq", hs)], sem=("wuq", hs))
                K.dma("sp", lambda e: e.dma_start(out=woa[hs][0:64, :], in_=s_ewoa[i][h * 64:(h + 1) * 64, :]), reads=[scr], writes=[("woa", hs)], sem=("woa", hs))

            sc_i = 0
            load_head_w(0)
            for j in range(NB):
                drain(g_kblock(0, 0, j))
            L = Lanes()
            for h in range(8):
                hs = h % 2
                kb = h % 2
                if h + 1 < 8:
                    load_head_w(h + 1)
                for c0 in range(0, NCH, 8):
                    nq = min(8, NCH - c0)

                    def mmv(e, h=h, c0=c0, nq=nq):
                        for q in range(nq):
                            ins = e.matmul(ps[PDR[1]][:, q * 64:(q + 1) * 64], lhsT=ckvn[:, (c0 + q) * 128:(c0 + q + 1) * 128], rhs=wukv[:, h * 128 + 64:(h + 1) * 128],
                                           start=True, stop=True)
                        return ins
                    K.op("pe", mmv, reads=["wukv"] + [("ckvn", (c0 + q) // 4) for q in range(nq)], writes=[("ps", PDR[1])])
                    K.op("act", lambda e, c0=c0, nq=nq: e.activation(out=vx[:, c0:c0 + nq, 0:64], in_=ps[PDR[1]][:, 0:nq * 64].rearrange("p (q n) -> p q n", q=nq), func=AF.Copy),
                         reads=[("ps", PDR[1])], writes=[("vx", c0 // 8)])
                if h + 1 < 8:
                    L.kpend = [g_kblock(h + 1, 1 - kb, jj) for jj in range(NB)]
                L.qpend = g_qbuild(hs, 0)
                L.finish_q()
                for j in range(NB):
                    if j + 1 < NB:
                        L.qpend = g_qbuild(hs, j + 1)
                    qt = qh[j % 2]
                    ob = PO[j % 2]
                    nkt = 4 * (j + 1)
                    pend_pv = None
                    for kt in range(nkt):
                        d = kt - 4 * j
                        lo = max(0, d) * 128
                        sb = PSC[sc_i % 2]
                        pt = pT[sc_i % 3]
                        pkey = ("pT", sc_i % 3)
                        sc_i += 1

                        def mms(e, kt=kt, lo=lo, sb=sb, qt=qt, d=d, kb=kb):
                            ins = e.matmul(ps[sb][:, lo:512], lhsT=kh[kb][0:96, kt * 128:(kt + 1) * 128], rhs=qt[0:96, lo:512], start=True, stop=(d < 0))
                            if d >= 0:
                                ins = e.matmul(ps[sb][:, lo:lo + 128], lhsT=idb[:, :], rhs=trb[:, :], start=False, stop=True)
                            return ins
                        K.op("pe", mms, reads=[("kh", kb, kt // 4), ("qh", j % 2), "trb", "idb"], writes=[("ps", sb)])
                        K.op("act", lambda e, lo=lo, sb=sb, pt=pt: e.activation(out=pt[:, lo:512], in_=ps[sb][:, lo:512], func=AF.Exp), reads=[("ps", sb)], writes=[pkey])
                        if pend_pv is not None:
                            pend_pv()
                        pend_pv = (lambda kt=kt, lo=lo, pt=pt, nkt=nkt, ob=ob, pkey=pkey: K.op(
                            "pe", lambda e: e.matmul(ps[ob][0:65, lo:512], lhsT=vx[:, kt, 0:65], rhs=pt[:, lo:512], start=(kt == 0), stop=(kt == nkt - 1)),
                            reads=[pkey, ("vx", kt // 8), "vx1"], writes=[("ps", ob)]))
                        L.pump(3)
                    pend_pv()
                    pend_pv = None
                    L.finish_tail()
                    L.finish_q()
                    L.tail = g_tail(hs, j)
                L.finish_all()

        PHASES = {"ffn": ffn_phase, "odd": odd_phase, "even": even_phase}

        for s in range(NSEQ):
            load_seq(s)
            if need_even:
                tables_phase(s)
            for (l, p) in plan:
                if p == "ffn1":
                    ffn_phase(s, l, 0)
                elif p == "ffn2":
                    ffn_phase(s, l, 1)
                elif l % 2 == 1:
                    PHASES["odd"](s, l)
                else:
                    PHASES["even"](s, l)
            store_seq(s)
        K.barrier_all()
        with nc.allow_non_contiguous_dma(reason="small constant loads"):
            K.emit()
    return nc


FULL_PLAN = [(l, p) for l in range(DEPTH) for p in ("ffn1", "mix", "ffn2")]


def make_consts():
    ident = np.eye(128, dtype=np.float32)
    triu = np.triu(np.ones((128, 128), np.float32))
    rotm = np.zeros((128, 128), np.float32)
    for i in range(16):
        rotm[64 + i + 16, i] = 1.0
        rotm[64 + i, i + 16] = 1.0
    invf = np.zeros((128, 1), np.float32)
    f = (10000.0 ** (-np.arange(0, 32, 2, dtype=np.float32) / 32)).astype(np.float32)
    invf[64:80, 0] = f
    invf[80:96, 0] = f
    invcnt = np.zeros((128, 64), np.float32)
    for g, w in enumerate((2, 4, 8, 16)):
        for t in range(16):
            invcnt[:, g * 16 + t] = 1.0 / min(t + 1, w)
    maskneg = ((1.0 - triu) * -30000.0).astype(np.float32)
    sgn = np.full((128, 1), 1.0 - 1e-6, np.float32)
    sgn[64:80] = -(1.0 - 1e-6)
    return {"c_ident": ident, "c_triu": triu, "c_maskneg": maskneg, "c_sgn": sgn, "c_rotm": rotm, "c_invfreq": invf, "c_invcnt": invcnt}


_CACHE = {}


def kernel(**inputs):
    n = 8
    S = inputs["x"].shape[1]
    B = inputs["x"].shape[0]
    nseq = B // n
    key = (S, nseq)
    if key not in _CACHE:
        _CACHE[key] = build(S, nseq, FULL_PLAN)
    nc = _CACHE[key]
    consts = make_consts()
    in_maps = []
    for c in range(n):
        m = {k: np.ascontiguousarray(v) for k, v in inputs.items() if k not in ("x", "positions")}
        m["x"] = np.ascontiguousarray(inputs["x"][c * nseq:(c + 1) * nseq])
        m["positions"] = np.ascontiguousarray(inputs["positions"][c * nseq:(c + 1) * nseq]).astype(np.int32)
        m.update(consts)
        in_maps.append(m)
    res = run_bass_kernel_spmd(nc, in_maps, core_ids=list(range(n)))
    return np.concatenate([np.asarray(r["y"]) for r in res.results], axis=0).astype(np.float32)
```

```python
import contextlib
import numpy as np
import concourse.bass as bass
import concourse.mybir as mybir
from concourse.bass_utils import run_bass_kernel_spmd

F32 = mybir.dt.float32
BF16 = mybir.dt.bfloat16
I32 = mybir.dt.int32
AF = mybir.ActivationFunctionType
ALU = mybir.AluOpType

D = 1024
DFF = 2816
NFF = 22
DEPTH = 4
EPS = 1e-6
SB_BASE = 16512
SB_END = 229376
HRES_BYTES = 8 * 4096 * 4
ENGS = ("pe", "act", "dve", "pool", "sp")


def _dsize(dt):
    return 2 if dt == BF16 else 4


class Sched:
    def __init__(self, nc, stack):
        self.nc = nc
        self.stack = stack
        self.q = {e: [] for e in ENGS}
        self.tok = {}
        self.sem = {}
        self.seen = {}
        self.res = {}

    def _semh(self, key):
        if key not in self.sem:
            name = "s_" + "_".join(str(x) for x in key)
            self.sem[key] = self.stack.enter_context(self.nc.semaphore(name))
        return self.sem[key]

    def _collect(self, eng, reads, writes):
        need = {}

        def add(t):
            if t is not None:
                k, v = t
                if need.get(k, 0) < v:
                    need[k] = v

        for r in reads:
            e = self.res.get(r)
            if e is not None:
                add(e[0])
        for w in writes:
            e = self.res.get(w)
            if e is not None:
                add(e[0])
                for k, v in e[1].items():
                    add((k, v))
        waits = []
        for k, v in need.items():
            if eng == "pe" and k == ("eng", "pe"):
                continue
            if self.seen.get((eng, k), 0) < v:
                waits.append((k, v))
                self.seen[(eng, k)] = v
        return waits

    def _commit(self, reads, writes, token):
        k, v = token
        for r in reads:
            e = self.res.setdefault(r, [None, {}])
            if e[1].get(k, 0) < v:
                e[1][k] = v
        for w in writes:
            self.res[w] = [token, {}]

    def op(self, eng, fn, reads=(), writes=()):
        waits = self._collect(eng, reads, writes)
        key = ("eng", eng)
        self._semh(key)
        self.tok[key] = self.tok.get(key, 0) + 1
        self._commit(reads, writes, (key, self.tok[key]))
        self.q[eng].append((waits, fn, (key, 1)))

    def dma(self, queue, fn, reads=(), writes=(), sem=None, token_val=None):
        waits = self._collect(queue, reads, writes)
        key = ("dma",) + tuple(sem)
        self._semh(key)
        self.tok[key] = self.tok.get(key, 0) + 16
        tv = self.tok[key] if token_val is None else token_val
        self._commit(reads, writes, (key, tv))
        self.q[queue].append((waits, fn, (key, 16)))

    def barrier_all(self):
        waits = []
        for k, v in self.tok.items():
            if k[0] == "dma" and k[1] != "cast" and self.seen.get(("sp", k), 0) < v:
                waits.append((k, v))
                self.seen[("sp", k)] = v
        for k, v in self.tok.items():
            if k[0] == "eng" and k != ("eng", "sp") and self.seen.get(("sp", k), 0) < v:
                waits.append((k, v))
                self.seen[("sp", k)] = v
        key = ("eng", "sp")
        self._semh(key)
        self.tok[key] = self.tok.get(key, 0) + 1
        self.q["sp"].append((waits, (lambda e: e.nop()), (key, 1)))
        for e in ENGS:
            if e == "sp":
                continue
            ws = []
            for k, v in self.tok.items():
                if k[0] == "eng" and k != ("eng", e) and self.seen.get((e, k), 0) < v:
                    ws.append((k, v))
                    self.seen[(e, k)] = v
            for k, v in self.tok.items():
                if k[0] == "dma" and k[1] != "cast":
                    self.seen[(e, k)] = max(self.seen.get((e, k), 0), v)
            own = ("eng", e)
            if own in self.tok:
                ws.append((own, self.tok[own]))
                self.seen[(e, own)] = self.tok[own]
            if ws:
                self.q[e].append((ws, None, None))
        self.res = {}

    def emit(self):
        nc = self.nc
        with nc.Block() as block:
            decos = {"pe": block.tensor, "act": block.scalar, "dve": block.vector,
                     "pool": block.gpsimd, "sp": block.sync}
            for eng in ENGS:
                def body(e, eng=eng):
                    for waits, fn, inc in self.q[eng]:
                        if fn is None:
                            for k, v in waits:
                                e.wait_ge(self.sem[k], v)
                            continue
                        for k, v in waits[:-1]:
                            e.wait_ge(self.sem[k], v)
                        att = (self.sem[waits[-1][0]], waits[-1][1]) if waits else None
                        ins = fn(_EngProxy(e, att))
                        ins.then_inc(self.sem[inc[0]], inc[1])
                decos[eng](body)


class _EngProxy:
    def __init__(self, e, att):
        self._e = e
        self._att = att

    def __getattr__(self, name):
        f = getattr(self._e, name)

        def call(*a, **kw):
            ins = f(*a, **kw)
            if self._att is not None:
                ins._wait_ge(self._att[0], self._att[1])
                self._att = None
            return ins
        return call


class Arena:
    def __init__(self, nc, lo, hi):
        self.nc = nc
        self.lo = lo
        self.hi = hi
        self.cur = lo
        self.n = 0

    def alloc(self, name, shape, dt):
        nbytes = int(np.prod(shape[1:])) * _dsize(dt)
        nbytes = (nbytes + 31) // 32 * 32
        assert self.cur + nbytes <= self.hi, (name, self.cur, nbytes, self.hi)
        self.n += 1
        t = self.nc.alloc_sbuf_tensor_at("%s_%d" % (name, self.n), list(shape), dt, offset=self.cur)
        self.cur += nbytes
        return t

    def mark(self):
        return self.cur

    def reset(self, m):
        self.cur = m


class Stream:
    def __init__(self, K, name, slots, units, loader, src_keys):
        self.K = K
        self.name = name
        self.slots = slots
        self.units = units
        self.loader = loader
        self.src_keys = src_keys
        self.issued = 0

    def key(self, i):
        return ("ws", self.name, i % len(self.slots))

    def _issue_upto(self, n):
        while self.issued < min(n, len(self.units)):
            i = self.issued
            tile = self.slots[i % len(self.slots)]
            fn = self.loader(self.units[i], tile)
            self.K.dma("sp", fn, reads=self.src_keys(self.units[i]), writes=[self.key(i)],
                       sem=(self.name, i % len(self.slots)))
            self.issued += 1

    def get(self, i):
        self._issue_upto(i + len(self.slots))
        return self.slots[i % len(self.slots)], self.key(i)


def build(S, NSEQ, plan):
    assert S % 512 == 0
    NB = S // 512
    NCH = S // 128
    nc = bass.Bass("TRN2", target_bir_lowering=False)

    def din(name, shape, dt=F32):
        return nc.dram_tensor(name, list(shape), dt, kind="ExternalInput").ap()

    def dscr(name, shape, dt=BF16):
        return nc.dram_tensor(name, list(shape), dt, kind="Internal").ap()

    x = din("x", [NSEQ, S, D])
    positions = din("positions", [NSEQ, S], I32)
    ffn_norm = din("ffn_norm", [DEPTH, 2, D])
    ffn_w_gate = din("ffn_w_gate", [DEPTH, 2, D, DFF])
    ffn_w_up = din("ffn_w_up", [DEPTH, 2, D, DFF])
    ffn_w_down = din("ffn_w_down", [DEPTH, 2, DFF, D])
    mix_norm = din("mix_norm", [DEPTH, D])
    even_w_in = din("even_w_in", [2, D, 928])
    q_a_norm = din("q_a_norm", [2, 256])
    kv_a_norm = din("kv_a_norm", [2, 128])
    w_uq = din("w_uq", [2, 256, 768])
    w_ukv = din("w_ukv", [2, 128, 1024])
    q_norm = din("q_norm", [2, 96])
    k_norm = din("k_norm", [2, 96])
    pool_w = din("pool_w", [2, 4, 128, 128])
    pool_scale = din("pool_scale", [2, 512])
    even_w_out = din("even_w_out", [2, D, D])
    odd_w_in = din("odd_w_in", [2, D, 2048])
    sg_norm = din("sg_norm", [2, D])
    sg_w = din("sg_w", [2, 4, 128, 128])
    sg_b = din("sg_b", [2, 4, 128])
    odd_w_out = din("odd_w_out", [2, D, D])
    c_ident = din("c_ident", [128, 128])
    c_triu = din("c_triu", [128, 128])
    c_maskneg = din("c_maskneg", [128, 128])
    c_sgn = din("c_sgn", [128, 1])
    c_rotm = din("c_rotm", [128, 128])
    c_invfreq = din("c_invfreq", [128, 1])
    c_invcnt = din("c_invcnt", [128, 64])
    y = nc.dram_tensor("y", [NSEQ, S, D], F32, kind="ExternalOutput").ap()

    need_ffn = sorted({(l, j) for (l, p) in plan for j in ((0,) if p == "ffn1" else (1,) if p == "ffn2" else ())})
    need_even = sorted({l // 2 for (l, p) in plan if p == "mix" and l % 2 == 0})
    need_odd = sorted({l // 2 for (l, p) in plan if p == "mix" and l % 2 == 1})

    s_gu = {lj: dscr("s_gu_%d_%d" % lj, [NFF, 128, 2, 8, 128]) for lj in need_ffn}
    s_d = {lj: dscr("s_d_%d_%d" % lj, [8, 128, NFF, 128]) for lj in need_ffn}
    s_ewin = {i: dscr("s_ewin_%d" % i, [7, 128, 8, 128]) for i in need_even}
    s_ekpe = {i: dscr("s_ekpe_%d" % i, [128, 8, 32]) for i in need_even}
    s_euq = {i: dscr("s_euq_%d" % i, [8, 128, 2, 96]) for i in need_even}
    s_eukv = {i: dscr("s_eukv_%d" % i, [128, 1024]) for i in need_even}
    s_epw = {i: dscr("s_epw_%d" % i, [128, 4, 128]) for i in need_even}
    s_ewoa = {i: dscr("s_ewoa_%d" % i, [512, 1024]) for i in need_even}
    s_ewop = {i: dscr("s_ewop_%d" % i, [8, 128, 4, 128]) for i in need_even}
    s_owu = {i: dscr("s_owu_%d" % i, [8, 128, 8, 128]) for i in need_odd}
    s_owv = {i: dscr("s_owv_%d" % i, [128, 8, 1024]) for i in need_odd}
    s_owo = {i: dscr("s_owo_%d" % i, [8, 128, 8, 128]) for i in need_odd}
    s_cs = dscr("s_cs", [NSEQ, 2, 32, S], F32)

    stack = contextlib.ExitStack()
    with stack:
        K = Sched(nc, stack)
        h_res = nc.alloc_sbuf_tensor_at("h_res", [128, 8, S], F32, offset=SB_BASE)
        ar = Arena(nc, SB_BASE + HRES_BYTES, SB_END)
        ps = [stack.enter_context(nc.psum_tensor("ps%d" % i, [128, 512], F32)) for i in range(8)]

        ones_bf = ar.alloc("ones_bf", [128, 128], BF16)
        ones_f = ar.alloc("ones_f", [128, 128], F32)
        ident = ar.alloc("ident", [128, 128], F32)
        g_ffn = ar.alloc("g_ffn", [128, 8, 8], F32)
        g_mix = ar.alloc("g_mix", [128, 4, 8], F32)
        K.op("dve", lambda e: e.memset(ones_bf[:], 1.0), writes=["ones_bf"])
        K.op("dve", lambda e: e.memset(ones_f[:], 1.0), writes=["ones_f"])
        K.dma("sp", lambda e: e.dma_start(out=ident[:], in_=c_ident[:, :]), writes=["ident"], sem=("c", 0))
        g_raw = ar.alloc("g_raw", [128, 128], F32)
        K.dma("sp", lambda e: e.dma_start(out=g_raw[0:64, :], in_=ffn_norm.rearrange("l j (c p) -> (l j c) p", p=128)),
              writes=["g_raw0"], sem=("c", 1))
        K.dma("sp", lambda e: e.dma_start(out=g_raw[64:96, :], in_=mix_norm.rearrange("l (c p) -> (l c) p", p=128)),
              writes=["g_raw1"], sem=("c", 2))
        K.op("pe", lambda e: e.transpose(out=ps[7][:, 0:96], in_=g_raw[0:96, :], identity=ident[0:96, 0:96]),
             reads=["g_raw0", "g_raw1", "ident"], writes=[("ps", 7)])
        K.op("dve", lambda e: e.tensor_scalar(out=g_ffn[:].rearrange("p a c -> p (a c)"), in0=ps[7][:, 0:64], scalar1=32.0, scalar2=None, op0=ALU.mult),
             reads=[("ps", 7)], writes=["g_ffn"])
        K.op("dve", lambda e: e.tensor_scalar(out=g_mix[:].rearrange("p a c -> p (a c)"), in0=ps[7][:, 64:96], scalar1=32.0, scalar2=None, op0=ALU.mult),
             reads=[("ps", 7)], writes=["g_mix"])
        ar_base = ar.mark()

        def cast_group(grp, items):
            n = len(items)
            for idx, (dst, src) in enumerate(items):
                K.dma("pool", (lambda e, dst=dst, src=src: e.dma_start(out=dst, in_=src)),
                      writes=[("scrp", idx) + tuple(grp)], sem=("cast",) + tuple(grp), token_val=16 * n)
            K.res[("scr",) + tuple(grp)] = [(("dma", "cast") + tuple(grp), 16 * n), {}]

        def cast_ffn(l, j):
            items = []
            for m in range(NFF):
                items.append((s_gu[(l, j)][m, :, 0], ffn_w_gate[l, j, :, m * 128:(m + 1) * 128].rearrange("(k p) n -> p k n", p=128)))
                items.append((s_gu[(l, j)][m, :, 1], ffn_w_up[l, j, :, m * 128:(m + 1) * 128].rearrange("(k p) n -> p k n", p=128)))
            for mo in range(8):
                items.append((s_d[(l, j)][mo], ffn_w_down[l, j, :, mo * 128:(mo + 1) * 128].rearrange("(k p) n -> p k n", p=128)))
            cast_group(("ffn", l, j), items)

        def cast_odd(i):
            items = []
            for fc in range(8):
                items.append((s_owu[i][fc], odd_w_in[i, :, fc * 128:(fc + 1) * 128].rearrange("(k p) n -> p k n", p=128)))
            items.append((s_owv[i][:, :, :], odd_w_in[i, :, 1024:2048].rearrange("(k p) n -> p k n", p=128)))
            for mo in range(8):
                items.append((s_owo[i][mo], odd_w_out[i, :, mo * 128:(mo + 1) * 128].rearrange("(k p) n -> p k n", p=128)))
            cast_group(("odd", i), items)

        def cast_even(i):
            items = []
            cols = [0, 128, 256, 416, 544, 672, 800]
            for t, c0 in enumerate(cols):
                items.append((s_ewin[i][t], even_w_in[i, :, c0:c0 + 128].rearrange("(k p) n -> p k n", p=128)))
            items.append((s_ekpe[i][:, :, :], even_w_in[i, :, 384:416].rearrange("(k p) n -> p k n", p=128)))
            for h in range(8):
                items.append((s_euq[i][h], w_uq[i, :, h * 96:(h + 1) * 96].rearrange("(k p) n -> p k n", p=128)))
            items.append((s_eukv[i][:, :], w_ukv[i, :, :]))
            items.append((s_epw[i][:, :, :], pool_w[i].rearrange("g c d -> c g d")))
            items.append((s_ewoa[i][:, :], even_w_out[i, 0:512, :]))
            for mo in range(8):
                items.append((s_ewop[i][mo], even_w_out[i, 512:1024, mo * 128:(mo + 1) * 128].rearrange("(g p) n -> p g n", p=128)))
            cast_group(("even", i), items)

        done_cast = set()
        for (l, p) in plan:
            if p in ("ffn1", "ffn2"):
                key = ("ffn", l, 0 if p == "ffn1" else 1)
                if key not in done_cast:
                    cast_ffn(l, key[2])
            elif l % 2 == 0:
                key = ("even", l // 2)
                if key not in done_cast:
                    cast_even(l // 2)
            else:
                key = ("odd", l // 2)
                if key not in done_cast:
                    cast_odd(l // 2)
            done_cast.add(key)
        scr_tokens = {k: v for k, v in K.res.items() if k[0] == "scr"}

        def barrier():
            K.barrier_all()
            K.res.update({k: [v[0], {}] for k, v in scr_tokens.items()})
            ar.reset(ar_base)

        def hkey(c, b):
            return ("h", c, b)

        def emit_norm_act(b, sq, nchunk=8, c0=0):
            K.op("act", lambda e: e.activation(out=sq[:, 0:nchunk, :], in_=h_res[:, c0:c0 + nchunk, b * 512:(b + 1) * 512], func=AF.Square),
                 reads=[hkey(c, b) for c in range(c0, c0 + nchunk)], writes=["sq"])

        def emit_norm_rest(b, sq, rstd, hn, hn_key, gain, pstat):
            def mm(e):
                for c in range(8):
                    ins = e.matmul(ps[pstat][:, :], lhsT=ones_bf[:, :], rhs=sq[:, c, :], start=(c == 0), stop=(c == 7))
                return ins
            K.op("pe", mm, reads=["sq", "ones_bf"], writes=[("ps", pstat)])
            K.op("act", lambda e: e.activation(out=rstd[:, :], in_=ps[pstat][:, :], func=AF.Sqrt, bias=float(D * EPS)),
                 reads=[("ps", pstat)], writes=["rstd"])
            K.op("dve", lambda e: e.reciprocal(out=rstd[:, :], in_=rstd[:, :]), reads=["rstd"], writes=["rstd"])
            for c in range(8):
                K.op("dve", lambda e, c=c: e.scalar_tensor_tensor(out=hn[:, c, :], in0=h_res[:, c, b * 512:(b + 1) * 512],
                                                                  scalar=gain[:, c:c + 1], in1=rstd[:, :],
                                                                  op0=ALU.mult, op1=ALU.mult),
                     reads=[hkey(c, b), "rstd", "g_ffn", "g_mix"], writes=[hn_key])

        def load_seq(s):
            barrier()
            xs = [ar.alloc("xs", [128, D], F32) for _ in range(2)]
            for tc in range(NCH):
                sl = tc % 2
                K.dma("sp", lambda e, tc=tc, sl=sl: e.dma_start(out=xs[sl][:], in_=x[s, tc * 128:(tc + 1) * 128, :]),
                      writes=[("xs", sl)], sem=("xs", sl))
                for half in range(2):
                    bank = (tc * 2 + half) % 4

                    def tr(e, tc=tc, sl=sl, half=half, bank=bank):
                        for q in range(4):
                            c = half * 4 + q
                            ins = e.transpose(out=ps[bank][:, q * 128:(q + 1) * 128], in_=xs[sl][:, c * 128:(c + 1) * 128], identity=ident[:, :])
                        return ins
                    K.op("pe", tr, reads=[("xs", sl), "ident"], writes=[("ps", bank)])
                    b = tc // 4
                    eng = "act" if half == 0 else "dve"
                    dst = h_res[:, half * 4:half * 4 + 4, tc * 128:(tc + 1) * 128]
                    src = ps[bank][:, :].rearrange("p (q n) -> p q n", q=4)
                    if eng == "act":
                        K.op("act", lambda e, dst=dst, src=src: e.activation(out=dst, in_=src, func=AF.Copy),
                             reads=[("ps", bank)], writes=[hkey(c, b) for c in range(half * 4, half * 4 + 4)])
                    else:
                        K.op("dve", lambda e, dst=dst, src=src: e.tensor_copy(out=dst, in_=src),
                             reads=[("ps", bank)], writes=[hkey(c, b) for c in range(half * 4, half * 4 + 4)])

        def store_seq(s):
            barrier()
            xs = [ar.alloc("xs", [128, D], F32) for _ in range(2)]
            for tc in range(NCH):
                sl = tc % 2
                b = tc // 4
                for half in range(2):
                    bank = (tc * 2 + half) % 4

                    def tr(e, tc=tc, half=half, bank=bank):
                        for q in range(4):
                            c = half * 4 + q
                            ins = e.transpose(out=ps[bank][:, q * 128:(q + 1) * 128], in_=h_res[:, c, tc * 128:(tc + 1) * 128], identity=ident[:, :])
                        return ins
                    K.op("pe", tr, reads=[hkey(c, b) for c in range(half * 4, half * 4 + 4)] + ["ident"], writes=[("ps", bank)])
                    dst = xs[sl][:, half * 512:(half + 1) * 512]
                    if half == 0:
                        K.op("act", lambda e, dst=dst, bank=bank: e.activation(out=dst, in_=ps[bank][:, :], func=AF.Copy),
                             reads=[("ps", bank)], writes=[("xs", sl, half)])
                    else:
                        K.op("dve", lambda e, dst=dst, bank=bank: e.tensor_copy(out=dst, in_=ps[bank][:, :]),
                             reads=[("ps", bank)], writes=[("xs", sl, half)])
                K.dma("sp", lambda e, tc=tc, sl=sl: e.dma_start(out=y[s, tc * 128:(tc + 1) * 128, :], in_=xs[sl][:]),
                      reads=[("xs", sl, 0), ("xs", sl, 1)], writes=[("yout", tc)], sem=("ys", sl))

        def ffn_phase(s, l, j):
            barrier()
            hn = [ar.alloc("hn", [128, 8, 512], BF16) for _ in range(2)]
            sq = ar.alloc("sq", [128, 8, 512], BF16)
            rstd = ar.alloc("rstd", [128, 512], F32)
            sg = [ar.alloc("sg", [128, 512], F32) for _ in range(2)]
            act = ar.alloc("act", [128, NFF, 512], BF16)
            wgu = [ar.alloc("wgu", [128, 2, 8, 128], BF16) for _ in range(3)]
            wd = [ar.alloc("wd", [128, NFF, 128], BF16) for _ in range(2)]
            gain = g_ffn[:, l * 2 + j, :]
            scr = ("scr", "ffn", l, j)
            gu_units = [(b, m) for b in range(NB) for m in range(NFF)]
            d_units = [(b, mo) for b in range(NB) for mo in range(8)]
            st_gu = Stream(K, "wgu", wgu, gu_units,
                           lambda u, t: (lambda e: e.dma_start(out=t[:], in_=s_gu[(l, j)][u[1]])),
                           lambda u: [scr])
            st_d = Stream(K, "wd", wd, d_units,
                          lambda u, t: (lambda e: e.dma_start(out=t[:], in_=s_d[(l, j)][u[1]])),
                          lambda u: [scr])
            PG, PU, PD, PST = (0, 1), (2, 3), (4, 5), 6

            emit_norm_act(0, sq)
            emit_norm_rest(0, sq, rstd, hn[0], ("hn", 0), gain, PST)
            for b in range(NB):
                hs = b % 2
                for m in range(NFF):
                    wt, wkey = st_gu.get(b * NFF + m)
                    gb, ub = PG[m % 2], PU[m % 2]

                    def mmg(e, wt=wt, gb=gb, hs=hs):
                        for k in range(8):
                            ins = e.matmul(ps[gb][:, :], lhsT=wt[:, 0, k, :], rhs=hn[hs][:, k, :], start=(k == 0), stop=(k == 7))
                        return ins

                    def mmu(e, wt=wt, ub=ub, hs=hs):
                        for k in range(8):
                            ins = e.matmul(ps[ub][:, :], lhsT=wt[:, 1, k, :], rhs=hn[hs][:, k, :], start=(k == 0), stop=(k == 7))
                        return ins
                    K.op("pe", mmg, reads=[wkey, ("hn", hs)], writes=[("ps", gb)])
                    K.op("pe", mmu, reads=[wkey, ("hn", hs)], writes=[("ps", ub)])
                    K.op("act", lambda e, gb=gb, m=m: e.activation(out=sg[m % 2][:, :], in_=ps[gb][:, :], func=AF.Silu),
                         reads=[("ps", gb)], writes=[("sg", m % 2)])
                    K.op("dve", lambda e, ub=ub, m=m: e.tensor_tensor(out=act[:, m, :], in0=ps[ub][:, :], in1=sg[m % 2][:, :], op=ALU.mult),
                         reads=[("ps", ub), ("sg", m % 2)], writes=[("act", m)])
                    if b + 1 < NB and m == 4:
                        emit_norm_act(b + 1, sq)
                    if b + 1 < NB and m == 12:
                        emit_norm_rest(b + 1, sq, rstd, hn[1 - hs], ("hn", 1 - hs), gain, PST)
                for mo in range(8):
                    wt, wkey = st_d.get(b * 8 + mo)
                    db = PD[mo % 2]

                    def mmd(e, wt=wt, db=db):
                        for k in range(NFF):
                            ins = e.matmul(ps[db][:, :], lhsT=wt[:, k, :], rhs=act[:, k, :], start=(k == 0), stop=(k == NFF - 1))
                        return ins
                    K.op("pe", mmd, reads=[wkey] + [("act", k) for k in range(NFF)], writes=[("ps", db)])
                    hv = h_res[:, mo, b * 512:(b + 1) * 512]
                    K.op("dve", lambda e, db=db, hv=hv: e.scalar_tensor_tensor(out=hv, in0=ps[db][:, :], scalar=0.5, in1=hv,
                                                                               op0=ALU.mult, op1=ALU.add),
                         reads=[("ps", db), hkey(mo, b)], writes=[hkey(mo, b)])

        def odd_phase(s, l):
            i = l // 2
            barrier()
            scr = ("scr", "odd", i)
            wv = ar.alloc("wv", [128, 8, 1024], BF16)
            hn = ar.alloc("hn", [128, 8, 512], BF16)
            sqg = ar.alloc("sqg", [128, 8, 512], BF16)
            rstd = ar.alloc("rstd", [128, 512], F32)
            v32 = ar.alloc("v32", [128, 1024], F32)
            ss = ar.alloc("ss", [128, 8], F32)
            vn = ar.alloc("vn", [128, 4, 1024], BF16)
            uf = [ar.alloc("uf", [128, 512], F32) for _ in range(2)]
            wu = [ar.alloc("wu", [128, 8, 128], BF16) for _ in range(3)]
            wo = [ar.alloc("wo", [128, 8, 128], BF16) for _ in range(3)]
            sgwT = ar.alloc("sgwT", [128, 4, 128], BF16)
            sgw_raw = ar.alloc("sgw_raw", [128, 4, 128], F32)
            sgb = ar.alloc("sgb", [128, 512], F32)
            gsg = ar.alloc("gsg", [128, 1024], F32)
            triu = ar.alloc("triu", [128, 128], F32)
            K.dma("sp", lambda e: e.dma_start(out=wv[:], in_=s_owv[i][:, :, :]), reads=[scr], writes=["wv"], sem=("o", 0))
            K.dma("sp", lambda e: e.dma_start(out=sgw_raw[:], in_=sg_w[i].rearrange("g t s -> t g s")), writes=["sgw_raw"], sem=("o", 1))
            K.dma("sp", lambda e: e.dma_start(out=triu[:], in_=c_triu[:, :]), writes=["triu"], sem=("o", 2))
            K.dma("sp", lambda e: e.dma_start(out=sgb[0:1, :], in_=sg_b[i:i + 1].rearrange("o g t -> o (g t)")), writes=["sgb"], sem=("o", 3))
            K.dma("sp", lambda e: e.dma_start(out=gsg[:], in_=sg_norm[i:i + 1, :].broadcast_to([128, 1024])), writes=["gsg"], sem=("o", 4))
            K.op("dve", lambda e: e.tensor_scalar(out=gsg[:], in0=gsg[:], scalar1=32.0, scalar2=None, op0=ALU.mult), reads=["gsg"], writes=["gsg"])
            for g in range(4):
                K.op("pe", lambda e, g=g: e.transpose(out=ps[g % 2][:, 0:128], in_=sgw_raw[:, g, :], identity=ident[:, :]),
                     reads=["sgw_raw", "ident"], writes=[("ps", g % 2)])
                K.op("dve", lambda e, g=g: e.tensor_tensor(out=sgwT[:, g, :], in0=ps[g % 2][:, 0:128], in1=triu[:, :], op=ALU.mult),
                     reads=[("ps", g % 2), "triu"], writes=["sgwT"])
            u_units = [(b, fc) for b in range(NB) for fc in range(8)]
            st_u = Stream(K, "wu", wu, u_units, lambda u, t: (lambda e: e.dma_start(out=t[:], in_=s_owu[i][u[1]])), lambda u: [scr])
            st_o = Stream(K, "wo", wo, u_units, lambda u, t: (lambda e: e.dma_start(out=t[:], in_=s_owo[i][u[1]])), lambda u: [scr])
            gain = g_mix[:, l, :]
            for b in range(NB):
                emit_norm_act(b, sqg)
                emit_norm_rest(b, sqg, rstd, hn, "hn", gain, 6)
                for ch in range(4):
                    for half in range(2):
                        bank = half

                        def mmv(e, ch=ch, half=half, bank=bank):
                            for k in range(8):
                                ins = e.matmul(ps[bank][:, :], lhsT=hn[:, k, ch * 128:(ch + 1) * 128], rhs=wv[:, k, half * 512:(half + 1) * 512],
                                               start=(k == 0), stop=(k == 7))
                            return ins
                        K.op("pe", mmv, reads=["hn", "wv"], writes=[("ps", bank)])
                        K.op("act", lambda e, half=half, bank=bank: e.activation(out=v32[:, half * 512:(half + 1) * 512], in_=ps[bank][:, :], func=AF.Gelu_apprx_tanh),
                             reads=[("ps", bank)], writes=[("v32", half)])
                    K.op("act", lambda e, ch=ch: e.activation(out=vn[:, ch, :], in_=v32[:, :], func=AF.Square, accum_out=ss[:, ch:ch + 1]),
                         reads=[("v32", 0), ("v32", 1)], writes=[("vn", ch), ("ss", ch)])
                    K.op("act", lambda e, ch=ch: e.activation(out=ss[:, ch:ch + 1], in_=ss[:, ch:ch + 1], func=AF.Sqrt, bias=float(D * EPS)),
                         reads=[("ss", ch)], writes=[("ss", ch)])
                    K.op("dve", lambda e, ch=ch: e.reciprocal(out=ss[:, ch:ch + 1], in_=ss[:, ch:ch + 1]), reads=[("ss", ch)], writes=[("ss", ch)])
                    K.op("dve", lambda e, ch=ch: e.scalar_tensor_tensor(out=vn[:, ch, :], in0=v32[:, :], scalar=ss[:, ch:ch + 1], in1=gsg[:, :],
                                                                        op0=ALU.mult, op1=ALU.mult),
                         reads=[("v32", 0), ("v32", 1), ("ss", ch), "gsg"], writes=[("vn", ch)])
                for fc in range(8):
                    g = fc // 2
                    wt, wkey = st_u.get(b * 8 + fc)
                    ub = 2 + fc % 2
                    mb = 4 + fc % 2

                    def mmu(e, wt=wt, ub=ub):
                        for k in range(8):
                            ins = e.matmul(ps[ub][:, :], lhsT=wt[:, k, :], rhs=hn[:, k, :], start=(k == 0), stop=(k == 7))
                        return ins
                    K.op("pe", mmu, reads=[wkey, "hn"], writes=[("ps", ub)])
                    K.op("act", lambda e, ub=ub, fc=fc: e.activation(out=uf[fc % 2][:, :], in_=ps[ub][:, :], func=AF.Gelu_apprx_tanh),
                         reads=[("ps", ub)], writes=[("uf", fc % 2)])

                    def mmx(e, fc=fc, g=g, mb=mb):
                        for ch in range(4):
                            e.matmul(ps[mb][:, ch * 128:(ch + 1) * 128], lhsT=vn[:, ch, fc * 128:(fc + 1) * 128], rhs=sgwT[:, g, :], start=True, stop=False)
                            ins = e.matmul(ps[mb][:, ch * 128:(ch + 1) * 128], lhsT=ones_f[0:1, :], rhs=sgb[0:1, g * 128:(g + 1) * 128], start=False, stop=True)
                        return ins
                    K.op("pe", mmx, reads=[("vn", c) for c in range(4)] + ["sgwT", "sgb", "ones_f"], writes=[("ps", mb)])
                    K.op("dve", lambda e, fc=fc, mb=mb: e.tensor_tensor(out=sqg[:, fc, :], in0=ps[mb][:, :], in1=uf[fc % 2][:, :], op=ALU.mult),
                         reads=[("ps", mb), ("uf", fc % 2)], writes=["sq"])
                for mo in range(8):
                    wt, wkey = st_o.get(b * 8 + mo)
                    db = 6 + mo % 2

                    def mmo(e, wt=wt, db=db):
                        for k in range(8):
                            ins = e.matmul(ps[db][:, :], lhsT=wt[:, k, :], rhs=sqg[:, k, :], start=(k == 0), stop=(k == 7))
                        return ins
                    K.op("pe", mmo, reads=[wkey, "sq"], writes=[("ps", db)])
                    hv = h_res[:, mo, b * 512:(b + 1) * 512]
                    K.op("dve", lambda e, db=db, hv=hv: e.tensor_tensor(out=hv, in0=ps[db][:, :], in1=hv, op=ALU.add),
                         reads=[("ps", db), hkey(mo, b)], writes=[hkey(mo, b)])

        s_kpe = dscr("s_kpe", [32, S], F32)
        s_rl = dscr("s_rl", [2, 512], F32)
        TWO_PI = 2.0 * np.pi
        C1 = 6.28125
        C2 = TWO_PI - C1
        SC = 1.0 - 1e-6

        def tables_phase(s):
            barrier()
            invf = ar.alloc("invf", [128, 1], F32)
            pi_ = ar.alloc("pi_", [128, 512], I32)
            pf = ar.alloc("pf", [128, 512], F32)
            av = ar.alloc("av", [128, 512], F32)
            tf = ar.alloc("tf", [128, 512], F32)
            ki = ar.alloc("ki", [128, 512], I32)
            kf = ar.alloc("kf", [128, 512], F32)
            rr = ar.alloc("rr", [128, 512], F32)
            outt = [ar.alloc("outt", [128, 512], F32) for _ in range(2)]
            R = slice(64, 96)
            K.dma("sp", lambda e: e.dma_start(out=invf[:], in_=c_invfreq[:, :]), writes=["invf"], sem=("t", 0))
            sgn = ar.alloc("sgn", [128, 1], F32)
            hpi = ar.alloc("hpi", [128, 1], F32)
            K.dma("sp", lambda e: e.dma_start(out=sgn[:], in_=c_sgn[:, :]), writes=["sgn"], sem=("t", 4))
            K.op("dve", lambda e: e.memset(hpi[:], float(np.pi / 2 * SC)), writes=["hpi"])
            for b in range(NB):
                bl = slice(b * 512, (b + 1) * 512)
                K.dma("sp", lambda e, bl=bl: e.dma_start(out=pi_[R, :], in_=positions[s:s + 1, bl].broadcast_to([32, 512])), writes=["pi"], sem=("t", 1))
                K.op("dve", lambda e: e.tensor_copy(out=pf[R, :], in_=pi_[R, :]), reads=["pi"], writes=["pf"])
                K.op("dve", lambda e: e.tensor_scalar(out=av[R, :], in0=pf[R, :], scalar1=invf[R, 0:1], scalar2=None, op0=ALU.mult), reads=["pf", "invf"], writes=["av"])
                for which in range(2):
                    off = 0.25 if which == 0 else 0.0
                    K.op("dve", lambda e, off=off: e.tensor_scalar(out=ki[R, :], in0=av[R, :], scalar1=float(1.0 / TWO_PI), scalar2=float(off), op0=ALU.mult, op1=ALU.add),
                         reads=["av"], writes=["ki"])
                    K.op("dve", lambda e: e.tensor_copy(out=kf[R, :], in_=ki[R, :]), reads=["ki"], writes=["kf"])
                    K.op("dve", lambda e: e.scalar_tensor_tensor(out=rr[R, :], in0=kf[R, :], scalar=float(-C1), in1=av[R, :], op0=ALU.mult, op1=ALU.add),
                         reads=["kf", "av"], writes=["rr"])
                    K.op("dve", lambda e: e.scalar_tensor_tensor(out=rr[R, :], in0=kf[R, :], scalar=float(-C2), in1=rr[R, :], op0=ALU.mult, op1=ALU.add),
                         reads=["kf", "rr"], writes=["rr"])
                    if which == 0:
                        K.op("act", lambda e: e.activation(out=outt[0][R, :], in_=rr[R, :], func=AF.Sin, scale=float(SC), bias=hpi[R, 0:1]),
                             reads=["rr", "hpi"], writes=[("outt", 0)])
                    else:
                        K.op("act", lambda e: e.activation(out=outt[1][R, :], in_=rr[R, :], func=AF.Sin, scale=sgn[R, 0:1]),
                             reads=["rr", "sgn"], writes=[("outt", 1)])
                    K.dma("sp", lambda e, which=which, bl=bl: e.dma_start(out=s_cs[s, which, :, bl], in_=outt[which][R, :]),
                          reads=[("outt", which)], writes=[("cs", b, which)], sem=("t", 2 + which))

        def even_phase(s, l):
            i = l // 2
            barrier()
            scr = ("scr", "even", i)
            R = slice(64, 96)
            cqn = ar.alloc("cqn", [128, 2, S], BF16)
            ckvn = ar.alloc("ckvn", [128, S], BF16)
            gq = ar.alloc("gq", [128, 2], F32)
            gkv = ar.alloc("gkv", [128, 1], F32)
            qg = ar.alloc("qg", [128, 1], F32)
            kg = ar.alloc("kg", [128, 1], F32)
            pscale = ar.alloc("pscale", [128, 4], F32)
            K.dma("sp", lambda e: e.dma_start(out=gq[:], in_=q_a_norm[i:i + 1, :].rearrange("o (c p) -> p (o c)", p=128)), writes=["gq"], sem=("e", 0))
            K.dma("sp", lambda e: e.dma_start(out=gkv[:], in_=kv_a_norm[i:i + 1, :].rearrange("o p -> p o")), writes=["gkv"], sem=("e", 1))
            K.dma("sp", lambda e: e.dma_start(out=qg[0:96, :], in_=q_norm[i:i + 1, :].rearrange("o p -> p o")), writes=["qg"], sem=("e", 2))
            K.dma("sp", lambda e: e.dma_start(out=kg[0:96, :], in_=k_norm[i:i + 1, :].rearrange("o p -> p o")), writes=["kg"], sem=("e", 3))
            K.dma("sp", lambda e: e.dma_start(out=pscale[:], in_=pool_scale[i:i + 1, :].rearrange("o (g p) -> p (o g)", p=128)), writes=["pscale"], sem=("e", 4))
            K.op("dve", lambda e: e.tensor_scalar(out=gq[:], in0=gq[:], scalar1=16.0, scalar2=None, op0=ALU.mult), reads=["gq"], writes=["gq"])
            K.op("dve", lambda e: e.tensor_scalar(out=gkv[:], in0=gkv[:], scalar1=float(np.sqrt(128.0)), scalar2=None, op0=ALU.mult), reads=["gkv"], writes=["gkv"])
            K.op("dve", lambda e: e.tensor_scalar(out=kg[0:96, :], in0=kg[0:96, :], scalar1=float(np.sqrt(96.0)), scalar2=None, op0=ALU.mult), reads=["kg"], writes=["kg"])
            mA = ar.mark()
            hn = ar.alloc("hn", [128, 8, 512], BF16)
            sq = ar.alloc("sq", [128, 8, 512], BF16)
            rstd = ar.alloc("rstd", [128, 512], F32)
            rstc = ar.alloc("rstc", [128, 512], F32)
            win = [ar.alloc("win", [128, 8, 128], BF16) for _ in range(3)]
            wkpe = ar.alloc("wkpe", [128, 8, 32], BF16)
            pbuf = ar.alloc("pbuf", [128, 528], F32)
            t1 = ar.alloc("t1", [128, 528], F32)
            t2 = ar.alloc("t2", [128, 528], F32)
            halo = ar.alloc("halo", [128, 4, 16], F32)
            pooled = [ar.alloc("pooled", [128, 512], BF16) for _ in range(2)]
            pfix = ar.alloc("pfix", [128, 16], F32)
            pout = ar.alloc("pout", [128, 4, 512], BF16)
            pw = ar.alloc("pw", [128, 4, 128], BF16)
            invc = ar.alloc("invc", [128, 64], F32)
            wop = [ar.alloc("wop", [128, 4, 128], BF16) for _ in range(3)]
            kst = ar.alloc("kst", [128, 512], F32)
            K.dma("sp", lambda e: e.dma_start(out=wkpe[:], in_=s_ekpe[i][:, :, :]), reads=[scr], writes=["wkpe"], sem=("e", 5))
            K.dma("sp", lambda e: e.dma_start(out=pw[:], in_=s_epw[i][:, :, :]), reads=[scr], writes=["pw"], sem=("e", 6))
            K.dma("sp", lambda e: e.dma_start(out=invc[:], in_=c_invcnt[:, :]), writes=["invc"], sem=("e", 7))
            K.op("dve", lambda e: e.memset(halo[:], 0.0), writes=["halo"])
            in_units = [(b, t) for b in range(NB) for t in range(7)]
            st_in = Stream(K, "win", win, in_units, lambda u, t: (lambda e: e.dma_start(out=t[:], in_=s_ewin[i][u[1]])), lambda u: [scr])
            op_units = [(b, mo) for b in range(NB) for mo in range(8)]
            st_op = Stream(K, "wop", wop, op_units, lambda u, t: (lambda e: e.dma_start(out=t[:], in_=s_ewop[i][u[1]])), lambda u: [scr])
            gain = g_mix[:, l, :]
            WIN = (2, 4, 8, 16)
            for b in range(NB):
                bl = slice(b * 512, (b + 1) * 512)
                emit_norm_act(b, sq)
                emit_norm_rest(b, sq, rstd, hn, "hn", gain, 6)
                banks = [0, 1, 2, 4, 5, 4, 5]
                for t in range(7):
                    wt, wkey = st_in.get(b * 7 + t)
                    bk = banks[t]

                    def mmi(e, wt=wt, bk=bk):
                        for k in range(8):
                            ins = e.matmul(ps[bk][:, :], lhsT=wt[:, k, :], rhs=hn[:, k, :], start=(k == 0), stop=(k == 7))
                        return ins
                    K.op("pe", mmi, reads=[wkey, "hn"], writes=[("ps", bk)])
                    if t == 1:
                        for c in range(2):
                            K.op("act", lambda e, c=c: e.activation(out=sq[:, c, :], in_=ps[c][:, :], func=AF.Square), reads=[("ps", c)], writes=["sq"])

                        def mms(e):
                            e.matmul(ps[6][:, :], lhsT=ones_bf[:, :], rhs=sq[:, 0, :], start=True, stop=False)
                            return e.matmul(ps[6][:, :], lhsT=ones_bf[:, :], rhs=sq[:, 1, :], start=False, stop=True)
                        K.op("pe", mms, reads=["sq", "ones_bf"], writes=[("ps", 6)])
                        K.op("act", lambda e: e.activation(out=rstc[:, :], in_=ps[6][:, :], func=AF.Sqrt, bias=float(256 * EPS)), reads=[("ps", 6)], writes=["rstc"])
                        K.op("dve", lambda e: e.reciprocal(out=rstc[:, :], in_=rstc[:, :]), reads=["rstc"], writes=["rstc"])
                        for c in range(2):
                            K.op("dve", lambda e, c=c, bl=bl: e.scalar_tensor_tensor(out=cqn[:, c, bl], in0=ps[c][:, :], scalar=gq[:, c:c + 1], in1=rstc[:, :],
                                                                                     op0=ALU.mult, op1=ALU.mult),
                                 reads=[("ps", c), "gq", "rstc"], writes=[("cqn", b)])
                    if t == 2:
                        K.op("act", lambda e: e.activation(out=sq[:, 2, :], in_=ps[2][:, :], func=AF.Square), reads=[("ps", 2)], writes=["sq"])
                        K.op("pe", lambda e: e.matmul(ps[6][:, :], lhsT=ones_bf[:, :], rhs=sq[:, 2, :], start=True, stop=True), reads=["sq", "ones_bf"], writes=[("ps", 6)])
                        K.op("act", lambda e: e.activation(out=rstc[:, :], in_=ps[6][:, :], func=AF.Sqrt, bias=float(128 * EPS)), reads=[("ps", 6)], writes=["rstc"])
                        K.op("dve", lambda e: e.reciprocal(out=rstc[:, :], in_=rstc[:, :]), reads=["rstc"], writes=["rstc"])
                        K.op("dve", lambda e, bl=bl: e.scalar_tensor_tensor(out=ckvn[:, bl], in0=ps[2][:, :], scalar=gkv[:, 0:1], in1=rstc[:, :], op0=ALU.mult, op1=ALU.mult),
                             reads=[("ps", 2), "gkv", "rstc"], writes=[("ckvn", b)])

                        def mmk(e):
                            for k in range(8):
                                ins = e.matmul(ps[3][R, :], lhsT=wkpe[:, k, :], rhs=hn[:, k, :], start=(k == 0), stop=(k == 7))
                            return ins
                        K.op("pe", mmk, reads=["wkpe", "hn"], writes=[("ps", 3)])
                        K.op("act", lambda e: e.activation(out=kst[R, :], in_=ps[3][R, :], func=AF.Copy), reads=[("ps", 3)], writes=["kst"])
                        K.dma("sp", lambda e, bl=bl: e.dma_start(out=s_kpe[:, bl], in_=kst[R, :]), reads=["kst"], writes=[("kpe", b)], sem=("e", 8))
                    if t >= 3:
                        g = t - 3
                        w = WIN[g]
                        K.op("act", lambda e, bk=bk: e.activation(out=pbuf[:, 16:528], in_=ps[bk][:, :], func=AF.Copy), reads=[("ps", bk)], writes=["pbuf"])
                        K.op("dve", lambda e, g=g: e.tensor_copy(out=pbuf[:, 0:16], in_=halo[:, g, :]), reads=["halo"], writes=["pbuf"])
                        K.op("dve", lambda e: e.tensor_tensor(out=t1[:, 1:528], in0=pbuf[:, 1:528], in1=pbuf[:, 0:527], op=ALU.add), reads=["pbuf"], writes=["t1"])
                        fin = t1
                        if g >= 1:
                            K.op("dve", lambda e: e.tensor_tensor(out=t2[:, 3:528], in0=t1[:, 3:528], in1=t1[:, 1:526], op=ALU.add), reads=["t1"], writes=["t2"])
                            fin = t2
                        if g >= 2:
                            K.op("dve", lambda e: e.tensor_tensor(out=t1[:, 7:528], in0=t2[:, 7:528], in1=t2[:, 3:524], op=ALU.add), reads=["t2"], writes=["t1"])
                            fin = t1
                        if g >= 3:
                            K.op("dve", lambda e: e.tensor_tensor(out=t2[:, 15:528], in0=t1[:, 15:528], in1=t1[:, 7:520], op=ALU.add), reads=["t1"], writes=["t2"])
                            fin = t2
                        fkey = "t1" if fin is t1 else "t2"
                        pl = pooled[g % 2]
                        K.op("dve", lambda e, fin=fin, pl=pl, w=w: e.scalar_tensor_tensor(out=pl[:, :], in0=fin[:, 16:528], scalar=float(1.0 / w), in1=pbuf[:, 16:528],
                                                                                           op0=ALU.mult, op1=ALU.subtract),
                             reads=[fkey, "pbuf"], writes=[("pooled", g % 2)])
                        if b == 0:
                            K.op("dve", lambda e, fin=fin, g=g: e.tensor_tensor(out=pfix[:, :], in0=fin[:, 16:32], in1=invc[:, g * 16:(g + 1) * 16], op=ALU.mult),
                                 reads=[fkey, "invc"], writes=["pfix"])
                            K.op("dve", lambda e, pl=pl: e.tensor_tensor(out=pl[:, 0:16], in0=pfix[:, :], in1=pbuf[:, 16:32], op=ALU.subtract),
                                 reads=["pfix", "pbuf", ("pooled", g % 2)], writes=[("pooled", g % 2)])
                        K.op("dve", lambda e, g=g: e.tensor_copy(out=halo[:, g, :], in_=pbuf[:, 512:528]), reads=["pbuf"], writes=["halo"])
                        K.op("pe", lambda e, g=g, pl=pl: e.matmul(ps[7][:, :], lhsT=pw[:, g, :], rhs=pl[:, :], start=True, stop=True),
                             reads=["pw", ("pooled", g % 2)], writes=[("ps", 7)])
                        K.op("act", lambda e, g=g: e.activation(out=pout[:, g, :], in_=ps[7][:, :], func=AF.Copy, scale=pscale[:, g:g + 1]),
                             reads=[("ps", 7), "pscale"], writes=[("pout", g)])
                for mo in range(8):
                    wt, wkey = st_op.get(b * 8 + mo)
                    db = mo % 2

                    def mmo(e, wt=wt, db=db):
                        for g in range(4):
                            ins = e.matmul(ps[db][:, :], lhsT=wt[:, g, :], rhs=pout[:, g, :], start=(g == 0), stop=(g == 3))
                        return ins
                    K.op("pe", mmo, reads=[wkey] + [("pout", g) for g in range(4)], writes=[("ps", db)])
                    hv = h_res[:, mo, bl]
                    K.op("dve", lambda e, db=db, hv=hv: e.tensor_tensor(out=hv, in0=ps[db][:, :], in1=hv, op=ALU.add),
                         reads=[("ps", db), hkey(mo, b)], writes=[hkey(mo, b)])

            K.barrier_all()
            K.res.update({k: [v[0], {}] for k, v in scr_tokens.items()})
            ar.reset(mA)
            kh = [ar.alloc("kh", [128, S], BF16) for _ in range(2)]
            vx = ar.alloc("vx", [128, NCH, 65], BF16)
            wuq = [ar.alloc("wuq", [128, 2, 96], BF16) for _ in range(2)]
            wukv = ar.alloc("wukv", [128, 1024], BF16)
            woa = [ar.alloc("woa", [128, 1024], BF16) for _ in range(2)]
            cosT = ar.alloc("cosT", [128, 512], F32)
            sinT = ar.alloc("sinT", [128, 512], F32)
            xk = ar.alloc("xk", [128, 512], F32)
            sqk = ar.alloc("sqk", [128, 512], BF16)
            rstb = ar.alloc("rstb", [128, 512], F32)
            xr = ar.alloc("xr", [128, 512], F32)
            tmp = ar.alloc("tmp", [128, 512], F32)
            qh = [ar.alloc("qh", [128, 512], BF16) for _ in range(2)]
            pT = [ar.alloc("pT", [128, 512], BF16) for _ in range(3)]
            bc = ar.alloc("bc", [128, 512], F32)
            rl = bc
            ao = ar.alloc("ao", [128, 512], BF16)
            rotm = ar.alloc("rotm", [128, 128], F32)
            trb = ar.alloc("trb", [128, 128], BF16)
            idb = ar.alloc("idb", [128, 128], BF16)
            K.dma("sp", lambda e: e.dma_start(out=wukv[:], in_=s_eukv[i][:, :]), reads=[scr], writes=["wukv"], sem=("e", 9))
            K.dma("sp", lambda e: e.dma_start(out=rotm[:], in_=c_rotm[:, :]), writes=["rotm"], sem=("e", 10))
            K.dma("pool", lambda e: e.dma_start(out=trb[:], in_=c_maskneg[:, :]), writes=["trb"], sem=("e", 11))
            K.dma("pool", lambda e: e.dma_start(out=idb[:], in_=c_ident[:, :]), writes=["idb"], sem=("e", 15))
            K.op("dve", lambda e: e.memset(vx[:, :, 64:65], 1.0), writes=["vx1"])
            eps96 = ar.alloc("eps96", [128, 1], F32)
            K.op("dve", lambda e: e.memset(eps96[:], float(96 * EPS)), writes=["eps96"])
            PQ, PST, PDR, PSC, PO = 0, 1, (2, 3), (4, 5), (6, 7)

            def g_norm_rope(src_nope, src_rope, rope_key, gcol, dst, dst_key, j):
                bl = slice(j * 512, (j + 1) * 512)
                K.dma("sp", lambda e, bl=bl: e.dma_start(out=cosT[R, :], in_=s_cs[s, 0, :, bl]), reads=[("cs", j, 0)], writes=["cosT"], sem=("e", 12))
                K.dma("sp", lambda e, bl=bl: e.dma_start(out=sinT[R, :], in_=s_cs[s, 1, :, bl]), reads=[("cs", j, 1)], writes=["sinT"], sem=("e", 13))
                yield
                if src_rope is None:
                    K.op("act", lambda e: e.activation(out=sqk[0:96, :], in_=src_nope[0:96, :], func=AF.Square), reads=[("ps", PQ)], writes=["sqk"])
                    rsrc = src_nope
                else:
                    K.op("act", lambda e: e.activation(out=sqk[0:64, :], in_=src_nope[0:64, :], func=AF.Square), reads=[("ps", PQ)], writes=["sqk"])
                    K.op("act", lambda e: e.activation(out=sqk[R, :], in_=src_rope[R, :], func=AF.Square), reads=[rope_key, "sqk"], writes=["sqk"])
                    rsrc = src_rope
                yield
                K.op("pe", lambda e: e.matmul(ps[PST][0:96, :], lhsT=ones_bf[0:96, 0:96], rhs=sqk[0:96, :], start=True, stop=True), reads=["sqk", "ones_bf"], writes=[("ps", PST)])
                yield
                K.op("act", lambda e: e.activation(out=rstb[0:96, :], in_=ps[PST][0:96, :], func=AF.Ln, bias=eps96[0:96, 0:1]), reads=[("ps", PST), "eps96"], writes=["rstb"])
                yield
                K.op("act", lambda e: e.activation(out=rstb[0:96, :], in_=rstb[0:96, :], func=AF.Exp, scale=-0.5), reads=["rstb"], writes=["rstb"])
                yield
                K.op("dve", lambda e: e.scalar_tensor_tensor(out=dst[0:64, :], in0=src_nope[0:64, :], scalar=gcol[0:64, 0:1], in1=rstb[0:64, :], op0=ALU.mult, op1=ALU.mult),
                     reads=[("ps", PQ), "rstb", "qg", "kg"], writes=[dst_key])
                K.op("dve", lambda e: e.scalar_tensor_tensor(out=xr[R, :], in0=rsrc[R, :], scalar=gcol[R, 0:1], in1=rstb[R, :], op0=ALU.mult, op1=ALU.mult),
                     reads=[("ps", PQ), rope_key, "rstb", "qg", "kg"], writes=["xr"])
                yield
                K.op("pe", lambda e: e.matmul(ps[PST][R, :], lhsT=rotm[R, 0:32], rhs=xr[R, :], start=True, stop=True), reads=["xr", "rotm"], writes=[("ps", PST)])
                yield
                K.op("dve", lambda e: e.tensor_tensor(out=tmp[R, :], in0=ps[PST][R, :], in1=sinT[R, :], op=ALU.mult), reads=[("ps", PST), "sinT"], writes=["tmp"])
                K.op("dve", lambda e: e.tensor_tensor(out=xr[R, :], in0=xr[R, :], in1=cosT[R, :], op=ALU.mult), reads=["xr", "cosT", ("ps", PST)], writes=["xr"])
                yield
                K.op("dve", lambda e: e.tensor_tensor(out=dst[R, :], in0=xr[R, :], in1=tmp[R, :], op=ALU.add), reads=["xr", "tmp"], writes=[dst_key])
                yield

            def g_kblock(h, kb, j):
                bl = slice(j * 512, (j + 1) * 512)
                K.dma("sp", lambda e, bl=bl: e.dma_start(out=xk[R, :], in_=s_kpe[:, bl]), reads=[("kpe", j)], writes=["xk"], sem=("e", 14))
                K.op("pe", lambda e, h=h, bl=bl: e.matmul(ps[PQ][0:64, :], lhsT=wukv[:, h * 128:h * 128 + 64], rhs=ckvn[:, bl], start=True, stop=True),
                     reads=["wukv", ("ckvn", j)], writes=[("ps", PQ)])
                yield
                yield from g_norm_rope(ps[PQ], xk, "xk", kg, kh[kb][:, bl], ("kh", kb, j), j)

            def g_qbuild(hs, j):
                bl = slice(j * 512, (j + 1) * 512)

                def mmq(e, hs=hs, bl=bl):
                    e.matmul(ps[PQ][0:96, :], lhsT=wuq[hs][:, 0, :], rhs=cqn[:, 0, bl], start=True, stop=False)
                    return e.matmul(ps[PQ][0:96, :], lhsT=wuq[hs][:, 1, :], rhs=cqn[:, 1, bl], start=False, stop=True)
                K.op("pe", mmq, reads=[("wuq", hs), ("cqn", j)], writes=[("ps", PQ)])
                yield
                yield from g_norm_rope(ps[PQ], None, ("ps", PQ), qg, qh[j % 2], ("qh", j % 2), j)

            def g_tail(hs, j):
                ob = PO[j % 2]
                bl = slice(j * 512, (j + 1) * 512)
                K.op("act", lambda e: e.activation(out=rl[64:65, :], in_=ps[ob][64:65, :], func=AF.Copy), reads=[("ps", ob)], writes=["rl"])
                yield
                K.op("pe", lambda e: e.matmul(ps[PDR[0]][0:64, :], lhsT=ones_f[64:65, 0:64], rhs=rl[64:65, :], start=True, stop=True), reads=["rl", "ones_f"], writes=[("ps", PDR[0])])
                yield
                K.op("act", lambda e: e.activation(out=bc[0:64, :], in_=ps[PDR[0]][0:64, :], func=AF.Ln), reads=[("ps", PDR[0])], writes=["bc"])
                yield
                K.op("act", lambda e: e.activation(out=bc[0:64, :], in_=bc[0:64, :], func=AF.Exp, scale=-1.0), reads=["bc"], writes=["bc"])
                yield
                K.op("dve", lambda e: e.tensor_tensor(out=ao[0:64, :], in0=ps[ob][0:64, :], in1=bc[0:64, :], op=ALU.mult), reads=[("ps", ob), "bc"], writes=["ao"])
                yield
                for mo in range(8):
                    db = PDR[mo % 2]
                    K.op("pe", lambda e, mo=mo, db=db: e.matmul(ps[db][:, :], lhsT=woa[hs][0:64, mo * 128:(mo + 1) * 128], rhs=ao[0:64, :], start=True, stop=True),
                         reads=[("woa", hs), "ao"], writes=[("ps", db)])
                    hv = h_res[:, mo, bl]
                    K.op("dve", lambda e, db=db, hv=hv: e.tensor_tensor(out=hv, in0=ps[db][:, :], in1=hv, op=ALU.add),
                         reads=[("ps", db), hkey(mo, j)], writes=[hkey(mo, j)])
                    yield

            class Lanes:
                def __init__(self):
                    self.tail = None
                    self.cur = None
                    self.qpend = None
                    self.kpend = []

                def _step(self, g):
                    try:
                        next(g)
                        return True
                    except StopIteration:
                        return False

                def step_b(self):
                    while True:
                        if self.cur is None:
                            if self.qpend is not None:
                                self.cur, self.qpend = self.qpend, None
                            elif self.kpend:
                                self.cur = self.kpend.pop(0)
                            else:
                                return False
                        if self._step(self.cur):
                            return True
                        self.cur = None

                def step_a(self):
                    if self.tail is not None:
                        if self._step(self.tail):
                            return True
                        self.tail = None
                    return False

                def pump(self, n):
                    for t in range(n):
                        if t % 2 == 0:
                            if not self.step_a():
                                self.step_b()
                        else:
                            if not self.step_b():
                                self.step_a()

                def finish_tail(self):
                    while self.step_a():
                        pass

                def finish_q(self):
                    while self.cur is not None or self.qpend is not None:
                        if self.cur is None:
                            self.cur, self.qpend = self.qpend, None
                        if not self._step(self.cur):
                            self.cur = None

                def finish_all(self):
                    self.finish_tail()
                    while self.step_b():
                        pass

            def drain(g):
                if g is not None:
                    for _ in g:
                        pass

            def load_head_w(h):
                hs = h % 2
                K.dma("sp", lambda e: e.dma_start(out=wuq[hs][:], in_=s_euq[i][h]), reads=[scr], writes=[("wuq", hs)], sem=("wuq", hs))
                K.dma("sp", lambda e: e.dma_start(out=woa[hs][0:64, :], in_=s_ewoa[i][h * 64:(h + 1) * 64, :]), reads=[scr], writes=[("woa", hs)], sem=("woa", hs))

            sc_i = 0
            load_head_w(0)
            for j in range(NB):
                drain(g_kblock(0, 0, j))
            L = Lanes()
            for h in range(8):
                hs = h % 2
                kb = h % 2
                if h + 1 < 8:
                    load_head_w(h + 1)
                for c0 in range(0, NCH, 8):
                    nq = min(8, NCH - c0)

                    def mmv(e, h=h, c0=c0, nq=nq):
                        for q in range(nq):
                            ins = e.matmul(ps[PDR[1]][:, q * 64:(q + 1) * 64], lhsT=ckvn[:, (c0 + q) * 128:(c0 + q + 1) * 128], rhs=wukv[:, h * 128 + 64:(h + 1) * 128],
                                           start=True, stop=True)
                        return ins
                    K.op("pe", mmv, reads=["wukv"] + [("ckvn", (c0 + q) // 4) for q in range(nq)], writes=[("ps", PDR[1])])
                    K.op("act", lambda e, c0=c0, nq=nq: e.activation(out=vx[:, c0:c0 + nq, 0:64], in_=ps[PDR[1]][:, 0:nq * 64].rearrange("p (q n) -> p q n", q=nq), func=AF.Copy),
                         reads=[("ps", PDR[1])], writes=[("vx", c0 // 8)])
                if h + 1 < 8:
                    L.kpend = [g_kblock(h + 1, 1 - kb, jj) for jj in range(NB)]
                L.qpend = g_qbuild(hs, 0)
                L.finish_q()
                for j in range(NB):
                    if j + 1 < NB:
                        L.qpend = g_qbuild(hs, j + 1)
                    qt = qh[j % 2]
                    ob = PO[j % 2]
                    nkt = 4 * (j + 1)
                    pend_pv = None
                    for kt in range(nkt):
                        d = kt - 4 * j
                        lo = max(0, d) * 128
                        sb = PSC[sc_i % 2]
                        pt = pT[sc_i % 3]
                        pkey = ("pT", sc_i % 3)
                        sc_i += 1

                        def mms(e, kt=kt, lo=lo, sb=sb, qt=qt, d=d, kb=kb):
                            ins = e.matmul(ps[sb][:, lo:512], lhsT=kh[kb][0:96, kt * 128:(kt + 1) * 128], rhs=qt[0:96, lo:512], start=True, stop=(d < 0))
                            if d >= 0:
                                ins = e.matmul(ps[sb][:, lo:lo + 128], lhsT=idb[:, :], rhs=trb[:, :], start=False, stop=True)
                            return ins
                        K.op("pe", mms, reads=[("kh", kb, kt // 4), ("qh", j % 2), "trb", "idb"], writes=[("ps", sb)])
                        K.op("act", lambda e, lo=lo, sb=sb, pt=pt: e.activation(out=pt[:, lo:512], in_=ps[sb][:, lo:512], func=AF.Exp), reads=[("ps", sb)], writes=[pkey])
                        if pend_pv is not None:
                            pend_pv()
                        pend_pv = (lambda kt=kt, lo=lo, pt=pt, nkt=nkt, ob=ob, pkey=pkey: K.op(
                            "pe", lambda e: e.matmul(ps[ob][0:65, lo:512], lhsT=vx[:, kt, 0:65], rhs=pt[:, lo:512], start=(kt == 0), stop=(kt == nkt - 1)),
                            reads=[pkey, ("vx", kt // 8), "vx1"], writes=[("ps", ob)]))
                        L.pump(2)
                    pend_pv()
                    pend_pv = None
                    L.finish_tail()
                    L.finish_q()
                    L.tail = g_tail(hs, j)
                L.finish_all()

        PHASES = {"ffn": ffn_phase, "odd": odd_phase, "even": even_phase}

        for s in range(NSEQ):
            load_seq(s)
            if need_even:
                tables_phase(s)
            for (l, p) in plan:
                if p == "ffn1":
                    ffn_phase(s, l, 0)
                elif p == "ffn2":
                    ffn_phase(s, l, 1)
                elif l % 2 == 1:
                    PHASES["odd"](s, l)
                else:
                    PHASES["even"](s, l)
            store_seq(s)
        K.barrier_all()
        with nc.allow_non_contiguous_dma(reason="small constant loads"):
            K.emit()
    return nc


FULL_PLAN = [(l, p) for l in range(DEPTH) for p in ("ffn1", "mix", "ffn2")]


def make_consts():
    ident = np.eye(128, dtype=np.float32)
    triu = np.triu(np.ones((128, 128), np.float32))
    rotm = np.zeros((128, 128), np.float32)
    for i in range(16):
        rotm[64 + i + 16, i] = 1.0
        rotm[64 + i, i + 16] = 1.0
    invf = np.zeros((128, 1), np.float32)
    f = (10000.0 ** (-np.arange(0, 32, 2, dtype=np.float32) / 32)).astype(np.float32)
    invf[64:80, 0] = f
    invf[80:96, 0] = f
    invcnt = np.zeros((128, 64), np.float32)
    for g, w in enumerate((2, 4, 8, 16)):
        for t in range(16):
            invcnt[:, g * 16 + t] = 1.0 / min(t + 1, w)
    maskneg = ((1.0 - triu) * -30000.0).astype(np.float32)
    sgn = np.full((128, 1), 1.0 - 1e-6, np.float32)
    sgn[64:80] = -(1.0 - 1e-6)
    return {"c_ident": ident, "c_triu": triu, "c_maskneg": maskneg, "c_sgn": sgn, "c_rotm": rotm, "c_invfreq": invf, "c_invcnt": invcnt}


_CACHE = {}


def kernel(**inputs):
    n = 8
    S = inputs["x"].shape[1]
    B = inputs["x"].shape[0]
    nseq = B // n
    key = (S, nseq)
    if key not in _CACHE:
        _CACHE[key] = build(S, nseq, FULL_PLAN)
    nc = _CACHE[key]
    consts = make_consts()
    in_maps = []
    for c in range(n):
        m = {k: np.ascontiguousarray(v) for k, v in inputs.items() if k not in ("x", "positions")}
        m["x"] = np.ascontiguousarray(inputs["x"][c * nseq:(c + 1) * nseq])
        m["positions"] = np.ascontiguousarray(inputs["positions"][c * nseq:(c + 1) * nseq]).astype(np.int32)
        m.update(consts)
        in_maps.append(m)
    res = run_bass_kernel_spmd(nc, in_maps, core_ids=list(range(n)))
    return np.concatenate([np.asarray(r["y"]) for r in res.results], axis=0).astype(np.float32)
```
